# Optimizing a Trainium2 kernel written in Bass

```python
import jax, jax.numpy as jnp
from jax import lax
import numpy as np

D_MODEL = 2048
BATCH = 2
SEQ = 8192
DEPTH = 4

CHUNK = 64
MIX_A = D_MODEL
MIX_B = D_MODEL
MIX = MIX_A + MIX_B
H_A = 8
DV_A = MIX_A // H_A
DK_A = DV_A // 2
QK_A = H_A * DK_A
H_B = 8
DB = MIX_B // H_B
CONV_W = 4
LRU_C = 8.0
EPS = 1e-6
SEG_SIZES = [2 * QK_A, MIX_A, MIX_A, MIX_A, H_A, H_A, MIX_B, MIX_B]
SPLIT_AT = [int(s) for s in np.cumsum(SEG_SIZES)[:-1]]
N_IN = int(sum(SEG_SIZES))

kernel_name = "hymba_style_mlstm_rglru_trunk"


def rms_norm(x, g):
    x32 = x.astype(jnp.float32)
    y = x32 * lax.rsqrt(jnp.mean(x32 * x32, axis=-1, keepdims=True) + EPS)
    return (y * g.astype(jnp.float32)).astype(x.dtype)


def causal_dwconv(x, w):
    c = x.shape[-1]
    return lax.conv_general_dilated(
        x, w[:, None, :].astype(x.dtype), window_strides=(1,), padding=[(CONV_W - 1, 0)],
        dimension_numbers=("NWC", "WIO", "NWC"), feature_group_count=c)


def mlstm_chunkwise(q, k, v, i_pre, log_f):
    b_, s_, h_, dk = q.shape
    dv = v.shape[-1]
    nc = s_ // CHUNK

    def chunks(t):
        return t.reshape(b_, nc, CHUNK, h_, t.shape[-1]).transpose(1, 0, 3, 2, 4)

    def gchunks(t):
        return t.reshape(b_, nc, CHUNK, h_).transpose(1, 0, 3, 2)

    causal = jnp.tril(jnp.ones((CHUNK, CHUNK), dtype=bool))

    def step(carry, xs):
        c_st, n_st, m_st = carry
        qc, kc, vc, ic, fc = xs
        bcum = jnp.cumsum(fc, axis=-1)
        dmat = jnp.where(causal, bcum[..., :, None] - bcum[..., None, :] + ic[..., None, :], -jnp.inf)
        inter = bcum + m_st[..., None]
        m_t = jnp.maximum(inter, jnp.max(dmat, axis=-1))
        w_inter = jnp.exp(inter - m_t)
        sc = jnp.einsum("bhtk,bhsk->bhts", qc, kc) * jnp.exp(dmat - m_t[..., None])
        num = jnp.einsum("bhts,bhsv->bhtv", sc, vc) + w_inter[..., None] * jnp.einsum("bhtk,bhkv->bhtv", qc, c_st)
        den = jnp.sum(sc, axis=-1) + w_inter * jnp.einsum("bhtk,bhk->bht", qc, n_st)
        hc = num / jnp.maximum(jnp.abs(den), jnp.exp(-m_t))[..., None]
        b_last = bcum[..., -1]
        g = b_last[..., None] - bcum + ic
        m_new = jnp.maximum(b_last + m_st, jnp.max(g, axis=-1))
        decay = jnp.exp(b_last + m_st - m_new)
        wk = jnp.exp(g - m_new[..., None])
        c_new = decay[..., None, None] * c_st + jnp.einsum("bhs,bhsk,bhsv->bhkv", wk, kc, vc)
        n_new = decay[..., None] * n_st + jnp.einsum("bhs,bhsk->bhk", wk, kc)
        return (c_new, n_new, m_new), hc

    init = (jnp.zeros((b_, h_, dk, dv), jnp.float32),
            jnp.zeros((b_, h_, dk), jnp.float32),
            jnp.zeros((b_, h_), jnp.float32))
    _, hs = lax.scan(step, init, (chunks(q), chunks(k), chunks(v), gchunks(i_pre), gchunks(log_f)))
    return hs.transpose(1, 0, 3, 2, 4).reshape(b_, s_, h_, dv)


def rg_lru(x, w_a, b_a, w_x, b_x, lam):
    b_, s_, _ = x.shape
    xh = x.reshape(b_, s_, H_B, DB)
    r = jax.nn.sigmoid(jnp.einsum("bshi,hij->bshj", xh, w_a.astype(jnp.float32)).reshape(b_, s_, MIX_B)
                       + b_a.astype(jnp.float32))
    i = jax.nn.sigmoid(jnp.einsum("bshi,hij->bshj", xh, w_x.astype(jnp.float32)).reshape(b_, s_, MIX_B)
                       + b_x.astype(jnp.float32))
    log_a = -LRU_C * r * jax.nn.softplus(-lam.astype(jnp.float32))
    a = jnp.exp(log_a)
    u = x * i * jnp.sqrt(-jnp.expm1(2.0 * log_a))

    def combine(left, right):
        a1, h1 = left
        a2, h2 = right
        return a1 * a2, a2 * h1 + h2

    _, h = lax.associative_scan(combine, (a, u), axis=1)
    return h


def hybrid_layer(x, norm_g, w_in, i_bias, f_bias, qk_conv, head_norm_g,
                 lru_conv_w, lru_conv_b, w_a, b_a, w_x, b_x, lam, w_out):
    b_, s_, _ = x.shape
    f32 = jnp.float32
    u = rms_norm(x, norm_g)
    p = u @ w_in
    qk, v, o, z_a, ig, fg, xb, z_b = jnp.split(p, SPLIT_AT, axis=-1)

    qk = jax.nn.silu(causal_dwconv(qk, qk_conv))
    q, k = jnp.split(qk, 2, axis=-1)
    q = q.reshape(b_, s_, H_A, DK_A).astype(f32)
    k = k.reshape(b_, s_, H_A, DK_A).astype(f32) * (DK_A ** -0.5)
    v = v.reshape(b_, s_, H_A, DV_A).astype(f32)
    i_pre = ig.astype(f32) + i_bias.astype(f32)
    log_f = jax.nn.log_sigmoid(fg.astype(f32) + f_bias.astype(f32))
    h = mlstm_chunkwise(q, k, v, i_pre, log_f)
    h = h * jax.nn.sigmoid(o.astype(f32)).reshape(b_, s_, H_A, DV_A)
    mu = jnp.mean(h, axis=-1, keepdims=True)
    var = jnp.mean(jnp.square(h - mu), axis=-1, keepdims=True)
    h = ((h - mu) * lax.rsqrt(var + EPS)).reshape(b_, s_, MIX_A) * head_norm_g.astype(f32)
    y_a = h.astype(x.dtype) * jax.nn.silu(z_a)

    xc = causal_dwconv(xb, lru_conv_w) + lru_conv_b
    y_b = rg_lru(xc.astype(f32), w_a, b_a, w_x, b_x, lam).astype(x.dtype) * jax.nn.silu(z_b)

    return x + jnp.concatenate([y_a, y_b], axis=-1) @ w_out


def setup_inputs(seed: int = 0) -> dict:
    key = jax.random.key(seed)
    ks = jax.random.split(key, 20)
    f32 = jnp.float32
    n = lambda k, shp: jax.random.normal(k, shp, f32)
    u_lam = jax.random.uniform(ks[13], (DEPTH, MIX_B), f32, 0.9, 0.999)
    s_lam = u_lam ** (1.0 / LRU_C)
    return {
        "x": n(ks[0], (BATCH, SEQ, D_MODEL)),
        "norm_g": 1.0 + 0.02 * n(ks[1], (DEPTH, D_MODEL)),
        "w_in": n(ks[2], (DEPTH, D_MODEL, N_IN)) * (D_MODEL ** -0.5),
        "i_bias": 0.1 * n(ks[3], (DEPTH, H_A)),
        "f_bias": jnp.linspace(3.0, 6.0, H_A, dtype=f32)[None, :] + 0.1 * n(ks[4], (DEPTH, H_A)),
        "qk_conv": n(ks[5], (DEPTH, CONV_W, 2 * QK_A)) * (CONV_W ** -0.5),
        "head_norm_g": 1.0 + 0.02 * n(ks[6], (DEPTH, MIX_A)),
        "lru_conv_w": n(ks[7], (DEPTH, CONV_W, MIX_B)) * (CONV_W ** -0.5),
        "lru_conv_b": 0.02 * n(ks[8], (DEPTH, MIX_B)),
        "w_a": n(ks[9], (DEPTH, H_B, DB, DB)) * (DB ** -0.5),
        "b_a": 0.02 * n(ks[10], (DEPTH, MIX_B)),
        "w_x": n(ks[11], (DEPTH, H_B, DB, DB)) * (DB ** -0.5),
        "b_x": 0.02 * n(ks[12], (DEPTH, MIX_B)),
        "lam": jnp.log(s_lam) - jnp.log1p(-s_lam),
        "w_out": n(ks[14], (DEPTH, MIX, D_MODEL)) * (MIX ** -0.5),
        "final_g": 1.0 + 0.02 * n(ks[15], (D_MODEL,)),
    }


def reference(x, norm_g, w_in, i_bias, f_bias, qk_conv, head_norm_g,
              lru_conv_w, lru_conv_b, w_a, b_a, w_x, b_x, lam, w_out, final_g):
    for l in range(DEPTH):
        x = hybrid_layer(x, norm_g[l], w_in[l], i_bias[l], f_bias[l], qk_conv[l], head_norm_g[l],
                         lru_conv_w[l], lru_conv_b[l], w_a[l], b_a[l], w_x[l], b_x[l], lam[l], w_out[l])
    return rms_norm(x, final_g)
```

```python
import contextlib
import numpy as np
import concourse.bass as bass
import concourse.mybir as mybir
from concourse.bass_utils import run_bass_kernel_spmd

F32, BF16 = mybir.dt.float32, mybir.dt.bfloat16
ALU = mybir.AluOpType
AF = mybir.ActivationFunctionType
AX = mybir.AxisListType
EPS = 1e-6
LRU_C = 8.0


class Cfg:
    def __init__(self, D=2048, T=8192, L=4, NCORES=8, BATCH=2, TM=2048):
        self.D, self.T, self.L, self.NCORES, self.BATCH = D, T, L, NCORES, BATCH
        self.NSEG = 1
        self.TM = min(TM, T)
        self.NMT = T // self.TM
        self.NTM = self.TM // 128
        H = self.H = D // 256
        self.NQK = H * 128
        self.NIN = 2 * self.NQK + 3 * D + 2 * H + 2 * D
        self.KC = D // 128
        self.NT = T // 128
        self.SW = min(512, self.TM)
        self.NST = T // self.SW
        self.NSTM = self.TM // self.SW
        self.TPS = self.SW // 128
        self.NB = D // 128
        self.c_q = 0
        self.c_k = self.NQK
        self.c_v = 2 * self.NQK
        self.c_o = self.c_v + D
        self.c_za = self.c_o + D
        self.c_i = self.c_za + D
        self.c_xb = self.c_i + 2 * H
        self.c_zb = self.c_xb + D
        self.HW = 3 * D // 128
        self.oC = 0
        self.oD = H * 257
        self.oHE = self.oD + H
        self.oA = self.oHE + self.NB
        self.SUMF = self.oA + self.NB
        self.nqkb = 2 * self.NQK // 128
        self.p_qkc = 0
        self.p_lcw = self.p_qkc + self.nqkb * 4
        self.p_lcb = self.p_lcw + self.NB * 4
        self.p_ba = self.p_lcb + self.NB
        self.p_bx = self.p_ba + self.NB
        self.p_lam = self.p_bx + self.NB
        self.CW = self.p_lam + self.NB
        self.r_ng = 0
        self.r_hg = D
        self.r_ib = 2 * D
        self.RW = 2 * D + 2 * H
        self.KO = 2 * D // 128
        self.OCB = min(512, D)


C_ID, C_L, C_B, C_E0, C_E1 = 0, 128, 256, 384, 512
C_HM0, C_HM1, C_EPS, C_ONE, C_LSK, C_EPS4 = 640, 641, 642, 643, 644, 645
CCW = 648


def make_consts():
    c = np.zeros((128, CCW), np.float32)
    s = np.arange(128)[:, None]
    t = np.arange(128)[None, :]
    c[:, C_ID:C_ID + 128] = (s == t)
    c[:, C_L:C_L + 128] = (s <= t) & (s // 64 == t // 64)
    c[:, C_B:C_B + 128] = (s // 64 == t // 64)
    c[:, C_E0:C_E0 + 128] = (s < 64)
    c[:, C_E1:C_E1 + 128] = (s >= 64)
    c[:, C_HM0] = (np.arange(128) < 64)
    c[:, C_HM1] = (np.arange(128) >= 64)
    c[:, C_EPS] = EPS
    c[:, C_ONE] = 1.0
    c[:, C_LSK] = 0.5 * np.log(128.0)
    c[:, C_EPS4] = EPS
    return c


class Op:
    __slots__ = ("eng", "fn", "deps", "kind", "sig", "slot", "cnt")

    def __init__(self, eng, fn, deps, kind):
        self.eng, self.fn, self.deps, self.kind = eng, fn, deps, kind
        self.sig = False
        self.slot = None
        self.cnt = 0


class _Rec:
    def __init__(self):
        self.call = None

    def __getattr__(self, name):
        def f(*a, **k):
            self.call = (name, a, k)
            return self
        return f

    def play(self, eng):
        name, a, k = self.call
        return getattr(eng, name)(*a, **k)


class Sched:
    NDS = 6
    RESET = False

    def __init__(self, nc, es):
        self.nc = nc
        self.eng = {"sp": nc.sync, "act": nc.scalar, "pool": nc.gpsimd, "dve": nc.vector, "pe": nc.tensor}
        self.esem = {e: es.enter_context(nc.semaphore("e_" + e)) for e in self.eng}
        self.ecnt = {e: 0 for e in self.eng}
        self.dsem = {q: [es.enter_context(nc.semaphore("d_%s%d" % (q, i))) for i in range(self.NDS)]
                     for q in ("sp", "act", "pool")}
        self.dcnt = {q: [0] * self.NDS for q in self.dsem}
        self.dnext = {q: 0 for q in self.dsem}
        self.csem = es.enter_context(nc.semaphore("cc"))
        self.ccnt = 0
        self.seen = {e: {} for e in self.eng}
        self.reset()

    def reset(self):
        self.ops = []
        self.last_w = {}
        self.readers = {}

    def add(self, eng, fn, r=(), w=(), kind="eng"):
        idx = len(self.ops)
        deps = set()
        for k in r:
            if k in self.last_w:
                deps.add(self.last_w[k])
        for k in w:
            if k in self.last_w:
                deps.add(self.last_w[k])
            deps.update(self.readers.get(k, ()))
        for k in r:
            self.readers.setdefault(k, []).append(idx)
        for k in w:
            self.last_w[k] = idx
            self.readers[k] = []
        deps.discard(idx)
        latest = {}
        pruned = set()
        for d in deps:
            o = self.ops[d]
            if o.kind == "eng":
                if o.eng not in latest or latest[o.eng] < d:
                    latest[o.eng] = d
            else:
                pruned.add(d)
        pruned.update(latest.values())
        deps = pruned
        rec = _Rec()
        fn(rec)
        self.ops.append(Op(eng, rec, deps, kind))
        return idx

    def dma(self, q, out, in_, r=(), w=()):
        return self.add(q, lambda e: e.dma_start(out=out, in_=in_), r, w, kind="dma")

    def _wait(self, eng, sem, val):
        key = id(sem)
        if self.seen[eng].get(key, 0) >= val:
            return
        self.seen[eng][key] = val
        self.eng[eng].wait_ge(sem, val)

    def flush(self, barrier=True):
        ops = self.ops
        for op in ops:
            for d in op.deps:
                dep = ops[d]
                if dep.kind == "eng" and dep.eng == "pe" and op.eng == "pe" and op.kind == "eng":
                    continue
                dep.sig = True
        for op in ops:
            eng = op.eng
            for d in sorted(op.deps):
                dep = ops[d]
                if not dep.sig:
                    continue
                if dep.kind == "eng":
                    self._wait(eng, self.esem[dep.eng], dep.cnt)
                elif dep.kind == "dma":
                    self._wait(eng, self.dsem[dep.eng][dep.slot], dep.cnt)
                else:
                    self._wait(eng, self.csem, dep.cnt)
            if op.kind == "dma":
                s = self.dnext[eng]
                self.dnext[eng] = (s + 1) % self.NDS
                self._wait(eng, self.dsem[eng][s], self.dcnt[eng][s])
                ins = op.fn.play(self.eng[eng])
                self.dcnt[eng][s] += 16
                op.slot, op.cnt = s, self.dcnt[eng][s]
                ins.then_inc(self.dsem[eng][s], 16)
            elif op.kind == "cc":
                self._wait(eng, self.csem, self.ccnt)
                ins = op.fn.play(self.eng[eng])
                self.ccnt += 1
                op.cnt = self.ccnt
                ins.then_inc(self.csem, 1)
            else:
                ins = op.fn.play(self.eng[eng])
                if op.sig:
                    self.ecnt[eng] += 1
                    op.cnt = self.ecnt[eng]
                    ins.then_inc(self.esem[eng], 1)
        self.reset()
        if barrier:
            self.drain()

    def drain(self):
        for q in self.dsem:
            for s in range(self.NDS):
                self._wait(q, self.dsem[q][s], self.dcnt[q][s])
        self.nc.all_engine_barrier()
        if self.RESET:
            for e in self.esem:
                self.nc.gpsimd.sem_clear(self.esem[e])
                self.ecnt[e] = 0
            for q in self.dsem:
                for i in range(self.NDS):
                    self.nc.gpsimd.sem_clear(self.dsem[q][i])
                    self.dcnt[q][i] = 0
            self.seen = {e: {} for e in self.eng}
            self.nc.all_engine_barrier()


def apx(ap, extra):
    return bass.AP(ap.tensor, ap.offset, [list(a) for a in ap.ap] + [list(e) for e in extra])


def bc_mid(ap2, n):
    a = [list(x) for x in ap2.ap]
    return bass.AP(ap2.tensor, ap2.offset, [a[0], [0, n]] + a[1:])


def bc_last(ap, n):
    return apx(ap, [[0, n]])


def build(cfg, dbg=(), stop=None):
    D, T, L, H, KC, NT, SW, NST, TPS, NB = cfg.D, cfg.T, cfg.L, cfg.H, cfg.KC, cfg.NT, cfg.SW, cfg.NST, cfg.TPS, cfg.NB
    TM, NMT, NTM, NSTM = cfg.TM, cfg.NMT, cfg.NTM, cfg.NSTM
    NQK, NIN, KO, OCB = cfg.NQK, cfg.NIN, cfg.KO, cfg.OCB
    WB = min(512, D)
    NSB = WB // 128
    nc = bass.Bass("TRN2", target_bir_lowering=False)

    def din(name, shape, dt=F32):
        return nc.dram_tensor(name, list(shape), dt, kind="ExternalInput").ap()

    def dscr(name, shape, dt=F32):
        kind = "ExternalOutput" if name in dbg else "Internal"
        return nc.dram_tensor(name, list(shape), dt, kind=kind).ap()

    x_in = din("x", [T, D])
    w_in = din("w_in", [L, D, NIN])
    w_out = din("w_out", [L, 2 * D, D])
    w_a = din("w_a", [L, H, 256, 256])
    w_x = din("w_x", [L, H, 256, 256])
    chan = din("chan", [L, 128, cfg.CW])
    rowv = din("rowv", [L, cfg.RW])
    fin_g = din("final_g", [1, D])
    cst_d = din("cst", [128, CCW])
    out_d = nc.dram_tensor("out", [T, D], F32, kind="ExternalOutput").ap()

    xres = dscr("xres", [T, D])
    qk_fm = dscr("qk_fm", [2 * NQK, T], BF16)
    v_tok = dscr("v_tok", [T, D], BF16)
    so_tok = dscr("so_tok", [T, D], BF16)
    sza_tok = dscr("sza_tok", [T, D], BF16)
    if_d = dscr("if_d", [128, NT * 2 * H])
    xc_fm = dscr("xc_fm", [D, T], BF16)
    szb_fm = dscr("szb_fm", [D, T], BF16)
    y_fm = dscr("y_fm", [2 * D, T], BF16)
    NWB = (NIN - 2 * H) // WB
    wbd = dscr("wbd", [NWB, 128, KC * WB], BF16)
    wod = dscr("wod", [D // OCB, 128, KO * OCB], BF16)

    with contextlib.ExitStack() as es0:
        S = Sched(nc, es0)
        uid = [0]

        def sb(es, name, shape, dt=F32):
            uid[0] += 1
            return es.enter_context(nc.sbuf_tensor("s%d_%s" % (uid[0], name), list(shape), dt))

        def ps(es, name, shape, dt=F32):
            uid[0] += 1
            return es.enter_context(nc.psum_tensor("p%d_%s" % (uid[0], name), list(shape), dt))

        cst = sb(es0, "cst", [128, CCW])
        idb = sb(es0, "idb", [128, 128], BF16)
        mLb = sb(es0, "mLb", [128, 128], F32)
        chs = sb(es0, "chs", [128, cfg.CW])
        S.dma("sp", cst[:], cst_d, w=["cst"])
        S.add("dve", lambda e: e.tensor_copy(out=idb[:], in_=cst[:, C_ID:C_ID + 128]), r=["cst"], w=["idb"])
        S.add("dve", lambda e: e.tensor_copy(out=mLb[:], in_=cst[:, C_L:C_L + 128]), r=["cst"], w=["mLb"])
        S.flush()

        def cc(col):
            return cst[:, col:col + 1]

        for l in range(L):
            xsrc = x_in if l == 0 else xres
            S.dma("sp", chs[:], chan[l], w=["chs"])
            S.flush()

            blocks = []
            for c0 in range(cfg.c_q, cfg.c_v, WB):
                blocks.append(("qk", c0))
            for c0 in range(cfg.c_za, cfg.c_i, WB):
                blocks.append(("za", c0))
            for c0 in range(cfg.c_zb, NIN, WB):
                blocks.append(("zb", c0))
            for c0 in range(cfg.c_o, cfg.c_za, WB):
                blocks.append(("o", c0))
            for c0 in range(cfg.c_v, cfg.c_o, WB):
                blocks.append(("v", c0))
            for c0 in range(cfg.c_xb, cfg.c_zb, WB):
                blocks.append(("xb", c0))
            blocks.append(("if", cfg.c_i))
            NBLK = len(blocks)
            assert NBLK - 1 == NWB

            with contextlib.ExitStack() as es:
                KW = max(KC, KO)
                fst = [sb(es, "fst%d" % i, [128, KW // 2, WB]) for i in range(2)]
                bst = [sb(es, "bst%d" % i, [128, KW // 2, WB], BF16) for i in range(2)]
                jobs = []
                for bi in range(NWB):
                    c0 = blocks[bi][1]
                    for hf in range(2):
                        k0 = hf * (KC // 2)
                        jobs.append((w_in[l][k0 * 128:(k0 + KC // 2) * 128, c0:c0 + WB], KC // 2,
                                     wbd[bi][:, k0 * WB:(k0 + KC // 2) * WB]))
                for cb in range(D // OCB):
                    for hf in range(KO // (KW // 2)):
                        k0 = hf * (KW // 2)
                        jobs.append((w_out[l][k0 * 128:(k0 + KW // 2) * 128, cb * OCB:(cb + 1) * OCB], KW // 2,
                                     wod[cb][:, k0 * OCB:(k0 + KW // 2) * OCB]))
                for ji, (src, nk, dst) in enumerate(jobs):
                    fb, bb = fst[ji % 2], bst[ji % 2]
                    kf, kb_ = "fst%d" % (ji % 2), "bst%d" % (ji % 2)
                    q2 = max(1, nk // 2)
                    for k0 in range(0, nk, q2):
                        S.dma("act" if ji % 2 else "sp", fb[:, k0:k0 + q2, :], src[k0 * 128:(k0 + q2) * 128, :].rearrange("(kc p) c -> p kc c", p=128),
                              w=[(kf, k0)])
                    fkeys = [(kf, k0) for k0 in range(0, nk, q2)]
                    if ji % 2 == 0:
                        S.add("dve", lambda e: e.tensor_copy(out=bb[:, 0:nk, :], in_=fb[:, 0:nk, :]), r=fkeys, w=[kb_])
                    else:
                        S.add("act", lambda e: e.copy(out=bb[:, 0:nk, :], in_=fb[:, 0:nk, :]), r=fkeys, w=[kb_])
                    seg = min(nk, max(1, 2048 // WB))
                    for k0 in range(0, nk, seg):
                        S.dma("sp" if ji % 2 else "act", dst[:, k0 * WB:(k0 + seg) * WB], bb[:, k0:k0 + seg, :].rearrange("p k c -> p (k c)"),
                              r=[kb_], w=["wbd"])
                S.flush()

            with contextlib.ExitStack() as es:
                uT = sb(es, "uT", [128, KC, TM], BF16)
                grep = sb(es, "grep", [128, D])
                xt = [sb(es, "xt%d" % i, [128, D]) for i in range(2)]
                ub = [sb(es, "ub%d" % i, [128, D], BF16) for i in range(2)]
                junk = sb(es, "junk", [128, D], BF16)
                sst = sb(es, "sst", [128, 8])
                wbf = [sb(es, "wbf%d" % i, [128, KC, WB], BF16) for i in range(2)]
                wif = sb(es, "wif", [128, KC, 2 * H], BF16)
                pre = [sb(es, "pre%d" % i, [128, 4 + TM]) for i in range(2)]
                cacc = [sb(es, "cacc%d" % i, [128, TM]) for i in range(1)]
                stg = [sb(es, "stg%d" % i, [128, TM], BF16) for i in range(2)]
                tstg = [sb(es, "tstg%d" % i, [128, NTM, WB], BF16) for i in range(1)]
                ifs = sb(es, "ifs", [128, NTM, 2 * H])
                hsv = sb(es, "hsv", [128, cfg.nqkb + NB, 4])
                psA = [ps(es, "psA%d" % i, [128, 512]) for i in range(5)]
                pst = [ps(es, "pst%d" % i, [128, 4, 128], BF16) for i in range(2)]

                S.dma("act", grep[:], bass.AP(rowv.tensor, l * cfg.RW + cfg.r_ng, [[0, 128], [1, D]]), w=["grep"])
                S.add("pool", lambda e: e.memset(hsv[:], 0.0), w=["hsv"])

                def norm_tile(gn, n):
                    bi = gn % 2
                    xb_, ub_ = xt[bi], ub[bi]
                    kx, ku = "xt%d" % bi, "ub%d" % bi
                    S.dma("sp", xb_[:], xsrc[gn * 128:(gn + 1) * 128, :], r=["xres"], w=[kx])
                    c0 = (bi * 4)
                    S.add("act", lambda e: e.activation(out=junk[:], in_=xb_[:], func=AF.Square,
                                                        accum_out=sst[:, c0:c0 + 1]), r=[kx], w=["junk", ("sst", c0)])
                    S.add("act", lambda e: e.activation(out=sst[:, c0 + 1:c0 + 2], in_=sst[:, c0:c0 + 1], func=AF.Ln,
                                                        scale=1.0 / D, bias=cc(C_EPS)), r=[("sst", c0), "cst"], w=[("sst", c0 + 1)])
                    S.add("act", lambda e: e.activation(out=sst[:, c0 + 2:c0 + 3], in_=sst[:, c0 + 1:c0 + 2], func=AF.Exp,
                                                        scale=-0.5), r=[("sst", c0 + 1)], w=[("sst", c0 + 2)])
                    S.add("dve", lambda e: e.scalar_tensor_tensor(out=ub_[:], in0=xb_[:], scalar=sst[:, c0 + 2:c0 + 3],
                                                                  in1=grep[:], op0=ALU.mult, op1=ALU.mult),
                          r=[kx, ("sst", c0 + 2), "grep"], w=[ku])
                    for g in range(KC // 4):
                        pt = pst[g % 2]
                        for i in range(4):
                            kc = g * 4 + i
                            S.add("pe", lambda e: e.transpose(pt[:, i, :], ub_[:, kc * 128:(kc + 1) * 128], idb[:]),
                                  r=[ku, "idb"], w=["pst%d" % (g % 2)])
                        dst, kd = uT[:, g * 4:(g + 1) * 4, n * 128:(n + 1) * 128], ("uT", n)
                        if g % 2 == 0:
                            S.add("act", lambda e: e.copy(out=dst, in_=pt[:]), r=["pst%d" % (g % 2)], w=[kd])
                        else:
                            S.add("dve", lambda e: e.tensor_copy(out=dst, in_=pt[:]), r=["pst%d" % (g % 2)], w=[kd])

                wstep = max(1, KC // 4)
                wseq = [0]

                def load_w(bi):
                    kind, c0 = blocks[bi]
                    if kind == "if":
                        for k0 in range(0, KC, wstep):
                            S.dma("pool", wif[:, k0:k0 + wstep, :], w_in[l][k0 * 128:(k0 + wstep) * 128, c0:c0 + 2 * H].rearrange("(kc p) c -> p kc c", p=128), w=["wif"])
                        return None
                    slot = wseq[0] % 2
                    wseq[0] += 1
                    wb = wbf[slot]
                    for k0 in range(0, KC, wstep):
                        S.dma("pool", wb[:, k0:k0 + wstep, :].rearrange("p k c -> p (k c)"),
                              wbd[bi][:, k0 * WB:(k0 + wstep) * WB], r=["wbd"], w=[("wbf", slot, k0)])
                    return slot

                psi = [0]
                fmi = [0]

                def next_ps():
                    i = psi[0] % 5
                    psi[0] += 1
                    return psA[i], "psA%d" % i

                def fm_block(bi, slot, mt):
                    kind, c0 = blocks[bi]
                    wb = wbf[slot]
                    wkeys = [("wbf", slot, k0) for k0 in range(0, KC, wstep)]
                    t0 = mt * TM
                    for sbk in range(NSB):
                        fi = fmi[0] % 2
                        fmi[0] += 1
                        pr, ca, sg = pre[fi], cacc[0], stg[fi]
                        kpr, kca, ksg = "pre%d" % fi, "cacc0", "stg%d" % fi
                        conv = kind in ("qk", "xb")
                        if conv:
                            if kind == "qk":
                                blk = (c0 - cfg.c_q) // 128 + sbk
                                hb_ = blk
                                wcol = cfg.p_qkc + blk * 4
                                dst_rows = qk_fm[blk * 128:(blk + 1) * 128, t0:t0 + TM]
                            else:
                                blk = (c0 - cfg.c_xb) // 128 + sbk
                                hb_ = cfg.nqkb + blk
                                wcol = cfg.p_lcw + blk * 4
                                dst_rows = xc_fm[blk * 128:(blk + 1) * 128, t0:t0 + TM]
                            S.add("dve", lambda e: e.tensor_copy(out=pr[:, 0:3], in_=hsv[:, hb_, 0:3]), r=["hsv", ("hsv", hb_)], w=[(kpr, "h")])
                        else:
                            blk = (c0 - cfg.c_zb) // 128 + sbk
                            dst_rows = szb_fm[blk * 128:(blk + 1) * 128, t0:t0 + TM]
                        for st in range(NSTM):
                            pA, kA = next_ps()
                            for kc in range(KC):
                                S.add("pe", lambda e: e.matmul(
                                    pA[:, 0:SW], lhsT=wb[:, kc, sbk * 128:(sbk + 1) * 128], rhs=uT[:, kc, st * SW:(st + 1) * SW],
                                    start=(kc == 0), stop=(kc == KC - 1)),
                                    r=wkeys + [("uT", st * TPS + i) for i in range(TPS)], w=[kA])
                            if conv:
                                S.add("act", lambda e: e.copy(out=pr[:, 3 + st * SW:3 + (st + 1) * SW], in_=pA[:, 0:SW]),
                                      r=[kA], w=[(kpr, st)])
                            else:
                                S.add("act", lambda e: e.activation(out=sg[:, st * SW:(st + 1) * SW], in_=pA[:, 0:SW], func=AF.Silu),
                                      r=[kA], w=[ksg])
                        if conv:
                            prk = [(kpr, "h")] + [(kpr, st) for st in range(NSTM)]
                            S.add("act", lambda e: e.copy(out=hsv[:, hb_, 0:3], in_=pr[:, TM:TM + 3]), r=prk, w=[("hsv", hb_)])
                            if kind == "xb":
                                bcol = cfg.p_lcb + blk
                                S.add("dve", lambda e: e.tensor_scalar(out=ca[:], in0=pr[:, 0:TM], scalar1=chs[:, wcol:wcol + 1],
                                                                       scalar2=chs[:, bcol:bcol + 1], op0=ALU.mult, op1=ALU.add),
                                      r=prk + ["chs"], w=[kca])
                            else:
                                S.add("dve", lambda e: e.tensor_scalar(out=ca[:], in0=pr[:, 0:TM], scalar1=chs[:, wcol:wcol + 1],
                                                                       scalar2=0.0, op0=ALU.mult, op1=ALU.add),
                                      r=prk + ["chs"], w=[kca])
                            for j in (1, 2):
                                S.add("dve", lambda e: e.scalar_tensor_tensor(out=ca[:], in0=pr[:, j:j + TM], scalar=chs[:, wcol + j:wcol + j + 1],
                                                                              in1=ca[:], op0=ALU.mult, op1=ALU.add),
                                      r=prk + ["chs", kca], w=[kca])
                            if kind == "xb":
                                S.add("dve", lambda e: e.scalar_tensor_tensor(out=sg[:], in0=pr[:, 3:3 + TM], scalar=chs[:, wcol + 3:wcol + 4],
                                                                              in1=ca[:], op0=ALU.mult, op1=ALU.add),
                                      r=prk + ["chs", kca], w=[ksg])
                            else:
                                S.add("dve", lambda e: e.scalar_tensor_tensor(out=ca[:], in0=pr[:, 3:3 + TM], scalar=chs[:, wcol + 3:wcol + 4],
                                                                              in1=ca[:], op0=ALU.mult, op1=ALU.add),
                                      r=prk + ["chs", kca], w=[kca])
                                S.add("act", lambda e: e.activation(out=sg[:], in_=ca[:], func=AF.Silu), r=[kca], w=[ksg])
                        S.dma("sp", dst_rows, sg[:], r=[ksg], w=["fm_out"])

                def tm_block(bi, slot, mt):
                    kind, c0 = blocks[bi]
                    wb = wbf[slot]
                    wkeys = [("wbf", slot, k0) for k0 in range(0, KC, wstep)]
                    tg, ktg = tstg[0], "tstg0"
                    for n in range(NTM):
                        pA, kA = next_ps()
                        for kc in range(KC):
                            S.add("pe", lambda e: e.matmul(
                                pA[:, 0:WB], lhsT=uT[:, kc, n * 128:(n + 1) * 128], rhs=wb[:, kc, :],
                                start=(kc == 0), stop=(kc == KC - 1)), r=wkeys + [("uT", n)], w=[kA])
                        if kind == "v":
                            if n % 2 == 0:
                                S.add("dve", lambda e: e.tensor_copy(out=tg[:, n, :], in_=pA[:, 0:WB]), r=[kA], w=[(ktg, n)])
                            else:
                                S.add("act", lambda e: e.copy(out=tg[:, n, :], in_=pA[:, 0:WB]), r=[kA], w=[(ktg, n)])
                        else:
                            fn = AF.Sigmoid if kind == "o" else AF.Silu
                            S.add("act", lambda e: e.activation(out=tg[:, n, :], in_=pA[:, 0:WB], func=fn), r=[kA], w=[(ktg, n)])
                    base = {"v": (v_tok, cfg.c_v), "o": (so_tok, cfg.c_o), "za": (sza_tok, cfg.c_za)}[kind]
                    cc0 = c0 - base[1]
                    q4 = max(1, NTM // 4)
                    for n0 in range(0, NTM, q4):
                        r0 = mt * TM + n0 * 128
                        S.dma("sp", base[0][r0:r0 + q4 * 128, cc0:cc0 + WB].rearrange("(n p) c -> p n c", p=128),
                              tg[:, n0:n0 + q4, :], r=[(ktg, n) for n in range(n0, n0 + q4)], w=["tm_out"])

                def if_block(mt):
                    for n in range(NTM):
                        pA, kA = next_ps()
                        for kc in range(KC):
                            S.add("pe", lambda e: e.matmul(
                                pA[:, 0:2 * H], lhsT=uT[:, kc, n * 128:(n + 1) * 128], rhs=wif[:, kc, :],
                                start=(kc == 0), stop=(kc == KC - 1)), r=["wif", ("uT", n)], w=[kA])
                        S.add("dve", lambda e: e.tensor_copy(out=ifs[:, n, :], in_=pA[:, 0:2 * H]), r=[kA], w=["ifs"])
                    S.dma("sp", if_d[:, mt * NTM * 2 * H:(mt + 1) * NTM * 2 * H], ifs[:].rearrange("p n c -> p (n c)"), r=["ifs"], w=["if_d"])

                seq = [(mt, bi) for mt in range(NMT) for bi in range(NBLK)]
                slots = {}
                slots[0] = load_w(seq[0][1])
                for si, (mt, bi) in enumerate(seq):
                    if bi == 0:
                        for n in range(NTM):
                            norm_tile(mt * NTM + n, n)
                    if si + 1 < len(seq):
                        slots[si + 1] = load_w(seq[si + 1][1])
                    kind = blocks[bi][0]
                    if kind in ("qk", "xb", "zb"):
                        fm_block(bi, slots[si], mt)
                    elif kind == "if":
                        if_block(mt)
                    else:
                        tm_block(bi, slots[si], mt)
                S.flush()
            if stop == "P1":
                return nc

            with contextlib.ExitStack() as es:
                ifs = sb(es, "ifs2", [128, NT, 2 * H])
                ibf = sb(es, "ibf", [128, 2 * H])
                zf = sb(es, "zf", [128, NT, H])
                lp = sb(es, "lp", [128, NT, H])
                t1 = sb(es, "t1", [128, NT, H])
                t2 = sb(es, "t2", [128, NT, H])
                ga = sb(es, "ga", [128, NT, H])
                gwk = sb(es, "gwk", [128, NT, H])
                gwk0 = sb(es, "gwk0", [128, NT, H])
                gwk1 = sb(es, "gwk1", [128, NT, H])
                ge = sb(es, "ge", [128, NT, H])
                gd0 = sb(es, "gd0", [128, NT, H])
                gd1 = sb(es, "gd1", [128, NT, H])
                Cf = sb(es, "Cf", [128, H, 257])
                Cb = sb(es, "Cb", [128, H, 257], BF16)
                hgrep = sb(es, "hgrep", [128, D])
                qs = [sb(es, "qs%d" % i, [128, H, SW], BF16) for i in range(2)]
                ks = [sb(es, "ks%d" % i, [128, H, SW], BF16) for i in range(2)]
                va = [sb(es, "va%d" % i, [128, H, 257], BF16) for i in range(2)]
                sos = [sb(es, "sos%d" % i, [128, D], BF16) for i in range(2)]
                szs = [sb(es, "szs%d" % i, [128, D], BF16) for i in range(2)]
                qz0 = [sb(es, "qz0_%d" % i, [128, H, 128], BF16) for i in range(2)]
                qz1 = [sb(es, "qz1_%d" % i, [128, H, 128], BF16) for i in range(2)]
                kw0 = [sb(es, "kw0_%d" % i, [128, 128], BF16) for i in range(4)]
                kw1 = [sb(es, "kw1_%d" % i, [128, 128], BF16) for i in range(4)]
                scT = [sb(es, "scT%d" % i, [128, 128], BF16) for i in range(4)]
                dsm = sb(es, "dsm", [128, 16])
                hall = [sb(es, "hall%d" % i, [128, H, 256]) for i in range(1)]
                sqb = sb(es, "sqb", [128, H, 256])
                st8 = sb(es, "st8", [128, 8, H])
                ybf = sb(es, "ybf", [128, D], BF16)
                yts = [sb(es, "yts%d" % i, [128, KC, 128], BF16) for i in range(2)]
                wab = sb(es, "wab", [128, H, 2, 256], BF16)
                wxb = sb(es, "wxb", [128, H, 2, 256], BF16)
                ccoef = sb(es, "ccoef", [128, 4, NB])
                hcar = sb(es, "hcar", [128, NB])
                xcs = [sb(es, "xcs%d" % i, [128, 2, SW], BF16) for i in range(2)]
                zbs = [sb(es, "zbs%d" % i, [128, 2, SW], BF16) for i in range(2)]
                lt = {nm: [sb(es, "%s%d" % (nm, i), [128, SW]) for i in range(2)]
                      for nm in ("er", "ei", "la", "a2", "hh")}
                yb = [sb(es, "yb%d" % i, [128, SW], BF16) for i in range(2)]

                p_acc = [ps(es, "pacc%d" % i, [128, 512]) for i in range(4)]
                p_dc = [ps(es, "pdc%d" % i, [128, 512]) for i in range(2)]
                p_ss = ps(es, "pss", [128, 4, 128])
                p_kt = ps(es, "pkt", [128, 4, 128], BF16)

                S.dma("sp", ifs[:].rearrange("p n c -> p (n c)"), if_d, r=["if_d"], w=["ifs"])
                S.dma("act", ibf[:], bass.AP(rowv.tensor, l * cfg.RW + cfg.r_ib, [[0, 128], [1, 2 * H]]), w=["ibf"])
                S.dma("act", hgrep[:], bass.AP(rowv.tensor, l * cfg.RW + cfg.r_hg, [[0, 128], [1, D]]), w=["hgrep"])
                for h0 in range(0, H, 2):
                    S.dma("pool", wab[:, h0:h0 + 2], w_a[l][h0:h0 + 2].rearrange("h (ib p) j -> p h ib j", p=128), w=["wab"])
                    S.dma("pool", wxb[:, h0:h0 + 2], w_x[l][h0:h0 + 2].rearrange("h (ib p) j -> p h ib j", p=128), w=["wxb"])
                S.add("dve", lambda e: e.tensor_tensor(out=ifs[:, :, 0:H], in0=ifs[:, :, 0:H], in1=bc_mid(ibf[:, 0:H], NT), op=ALU.add),
                      r=["ifs", "ibf"], w=["ifs"])
                S.add("dve", lambda e: e.tensor_tensor(out=zf[:], in0=ifs[:, :, H:2 * H], in1=bc_mid(ibf[:, H:2 * H], NT), op=ALU.add),
                      r=["ifs", "ibf"], w=["zf"])
                S.add("act", lambda e: e.activation(out=zf[:], in_=zf[:], func=AF.Exp, scale=-1.0), r=["zf"], w=["zf"])
                S.add("act", lambda e: e.activation(out=lp[:], in_=zf[:], func=AF.Ln, scale=1.0, bias=cc(C_ONE)), r=["zf", "cst"], w=["lp"])
                GN = max(1, 512 // H)
                for g0 in range(0, NT, GN):
                    g1 = min(NT, g0 + GN)
                    NH = (g1 - g0) * H
                    lp2 = lp[:, g0:g1, :].rearrange("p n h -> p (n h)")
                    pg, pb_, pd0, pd1 = p_acc[0], p_acc[1], p_acc[2], p_acc[3]
                    S.add("pe", lambda e: e.matmul(pg[:, 0:NH], lhsT=cst[:, C_L:C_L + 128], rhs=lp2, start=True, stop=True), r=["lp", "cst"], w=["pacc0"])
                    S.add("pe", lambda e: e.matmul(pb_[:, 0:NH], lhsT=cst[:, C_B:C_B + 128], rhs=lp2, start=True, stop=True), r=["lp", "cst"], w=["pacc1"])
                    S.add("pe", lambda e: e.matmul(pd0[:, 0:NH], lhsT=cst[:, C_E0:C_E0 + 128], rhs=lp2, start=True, stop=True), r=["lp", "cst"], w=["pacc2"])
                    S.add("pe", lambda e: e.matmul(pd1[:, 0:NH], lhsT=cst[:, C_E1:C_E1 + 128], rhs=lp2, start=True, stop=True), r=["lp", "cst"], w=["pacc3"])

                    def v3(p):
                        return p[:, 0:NH].rearrange("p (n h) -> p n h", h=H)
                    gs = slice(g0, g1)
                    S.add("dve", lambda e: e.tensor_tensor(out=t1[:, gs, :], in0=ifs[:, gs, 0:H], in1=v3(pg), op=ALU.add), r=["ifs", "pacc0"], w=["t1"])
                    S.add("act", lambda e: e.activation(out=ga[:, gs, :], in_=t1[:, gs, :], func=AF.Exp), r=["t1"], w=["ga"])
                    S.add("dve", lambda e: e.tensor_tensor(out=t2[:, gs, :], in0=t1[:, gs, :], in1=v3(pb_), op=ALU.subtract), r=["t1", "pacc1"], w=["t2"])
                    S.add("act", lambda e: e.activation(out=gwk[:, gs, :], in_=t2[:, gs, :], func=AF.Exp), r=["t2"], w=["gwk"])
                    S.add("act", lambda e: e.activation(out=ge[:, gs, :], in_=v3(pg), func=AF.Exp, scale=1.0, bias=cc(C_LSK)), r=["pacc0", "cst"], w=["ge"])
                    S.add("act", lambda e: e.activation(out=gd0[:, gs, :], in_=v3(pd0), func=AF.Exp, scale=-1.0), r=["pacc2"], w=["gd0"])
                    S.add("act", lambda e: e.activation(out=gd1[:, gs, :], in_=v3(pd1), func=AF.Exp, scale=-1.0), r=["pacc3"], w=["gd1"])
                S.add("dve", lambda e: e.tensor_scalar(out=gwk0[:], in0=gwk[:], scalar1=cc(C_HM0), scalar2=0.0, op0=ALU.mult, op1=ALU.add), r=["gwk", "cst"], w=["gwk0"])
                S.add("dve", lambda e: e.tensor_scalar(out=gwk1[:], in0=gwk[:], scalar1=cc(C_HM1), scalar2=0.0, op0=ALU.mult, op1=ALU.add), r=["gwk", "cst"], w=["gwk1"])
                lam = chs[:, cfg.p_lam:cfg.p_lam + NB]
                S.add("act", lambda e: e.activation(out=ccoef[:, 0, :], in_=lam, func=AF.Exp, scale=-1.0), r=["chs"], w=["ccoef"])
                S.add("act", lambda e: e.activation(out=ccoef[:, 0, :], in_=ccoef[:, 0, :], func=AF.Ln, scale=1.0, bias=cc(C_ONE)), r=["ccoef", "cst"], w=["ccoef"])
                S.add("dve", lambda e: e.tensor_scalar(out=ccoef[:, 1, :], in0=ccoef[:, 0, :], scalar1=-2.0 * LRU_C, scalar2=0.0, op0=ALU.mult, op1=ALU.add), r=["ccoef"], w=["ccoef"])
                S.add("dve", lambda e: e.tensor_scalar(out=ccoef[:, 0, :], in0=ccoef[:, 0, :], scalar1=-LRU_C, scalar2=0.0, op0=ALU.mult, op1=ALU.add), r=["ccoef"], w=["ccoef"])
                S.add("dve", lambda e: e.tensor_scalar(out=ccoef[:, 2, :], in0=chs[:, cfg.p_ba:cfg.p_ba + NB], scalar1=-1.0, scalar2=0.0, op0=ALU.mult, op1=ALU.add),
                      r=["chs"], w=["ccoef2"])
                S.add("dve", lambda e: e.tensor_scalar(out=ccoef[:, 3, :], in0=chs[:, cfg.p_bx:cfg.p_bx + NB], scalar1=-1.0, scalar2=0.0, op0=ALU.mult, op1=ALU.add),
                      r=["chs"], w=["ccoef2"])
                S.add("pool", lambda e: e.memset(Cf[:], 0.0), w=["Cf"] + [("Cf", h) for h in range(H)])
                S.add("pool", lambda e: e.memset(Cb[:], 0.0), w=[("Cb", h) for h in range(H)])
                S.add("pool", lambda e: e.memset(hcar[:], 0.0), w=["hcar"] + [("hcar", b) for b in range(NB)])
                for i in range(2):
                    S.add("pool", lambda e: e.memset(qz0[i][:], 0.0), w=["qz0_%d" % i])
                    S.add("pool", lambda e: e.memset(qz1[i][:], 0.0), w=["qz1_%d" % i])
                for i in range(2):
                    S.add("pool", lambda e: e.memset(va[i][:, :, 256:257], 1.0), w=["va%d" % i])

                cnt = {"kw": 0, "sc": 0, "acc": 0, "dc": 0}

                def mlstm_tile(n):
                    st, ti = n // TPS, n % TPS
                    qb, kb = qs[st % 2], ks[st % 2]
                    kq, kk = "qs%d" % (st % 2), "ks%d" % (st % 2)
                    if ti == 0:
                        S.dma("sp", kb[:], qk_fm[NQK:2 * NQK, st * SW:(st + 1) * SW].rearrange("(h p) t -> p h t", p=128), r=["fm_out"], w=[kk])
                        S.dma("sp", qb[:], qk_fm[0:NQK, st * SW:(st + 1) * SW].rearrange("(h p) t -> p h t", p=128), r=["fm_out"], w=[kq])
                    vb, kv = va[n % 2], "va%d" % (n % 2)
                    S.dma("act", vb[:, :, 0:256], v_tok[n * 128:(n + 1) * 128, :].rearrange("p (h v) -> p h v", v=256), r=["tm_out"], w=[kv])
                    tc0, tc1 = ti * 128, (ti + 1) * 128
                    sob, szb_ = sos[n % 2], szs[n % 2]
                    kso, ksz = "sos%d" % (n % 2), "szs%d" % (n % 2)
                    S.dma("act", sob[:], so_tok[n * 128:(n + 1) * 128, :], r=["tm_out"], w=[kso])
                    S.dma("act", szb_[:], sza_tok[n * 128:(n + 1) * 128, :], r=["tm_out"], w=[ksz])
                    z0, z1 = qz0[n % 2], qz1[n % 2]
                    kz0, kz1 = "qz0_%d" % (n % 2), "qz1_%d" % (n % 2)
                    S.add("pool", lambda e: e.tensor_copy(out=z0[:, :, 0:64], in_=qb[:, :, tc0:tc0 + 64]), r=[kq], w=[kz0])
                    S.add("pool", lambda e: e.tensor_copy(out=z1[:, :, 64:128], in_=qb[:, :, tc0 + 64:tc0 + 128]), r=[kq], w=[kz1])
                    hl, khl = hall[0], "hall0"
                    for h0 in range(0, H, 2):
                        grp = list(range(h0, min(H, h0 + 2)))
                        info = {}
                        for h in grp:
                            ki = cnt["kw"] % 4
                            cnt["kw"] += 1
                            k0t, k1t = kw0[ki], kw1[ki]
                            kk0, kk1 = "kw0_%d" % ki, "kw1_%d" % ki
                            pk = ("pkt", ki)
                            S.add("pe", lambda e: e.transpose(p_kt[:, ki, :], kb[:, h, tc0:tc1], idb[:]), r=[kk, "idb"], w=[pk])
                            S.add("act", lambda e: e.activation(out=k0t[:], in_=p_kt[:, ki, :], func=AF.Copy, scale=gwk0[:, n, h:h + 1]),
                                  r=[pk, "gwk0"], w=[kk0])
                            S.add("act", lambda e: e.activation(out=k1t[:], in_=p_kt[:, ki, :], func=AF.Copy, scale=gwk1[:, n, h:h + 1]),
                                  r=[pk, "gwk1"], w=[kk1])
                            si = cnt["sc"] % 4
                            cnt["sc"] += 1
                            sct, ksc, pss_k = scT[si], "scT%d" % si, ("pss", si)
                            S.add("pe", lambda e: e.matmul(p_ss[:, si, :], lhsT=kb[:, h, tc0:tc1], rhs=qb[:, h, tc0:tc1], start=True, stop=True),
                                  r=[kk, kq], w=[pss_k])
                            S.add("dve", lambda e: e.scalar_tensor_tensor(out=sct[:], in0=p_ss[:, si, :], scalar=ga[:, n, h:h + 1],
                                                                          in1=mLb[:], op0=ALU.mult, op1=ALU.mult),
                                  r=[pss_k, "ga", "mLb"], w=[ksc])
                            ai = cnt["acc"] % 2
                            cnt["acc"] += 1
                            pa, kpa = p_acc[ai], "pacc%d" % ai
                            S.add("pe", lambda e: e.matmul(pa[:, 0:257], lhsT=sct[:], rhs=vb[:, h, :], start=True, stop=False),
                                  r=[ksc, kv], w=[kpa])
                            S.add("pe", lambda e: e.matmul(pa[:, 0:257], lhsT=z0[:, h, :], rhs=Cb[:, h, :], start=False, stop=False),
                                  r=[kz0, ("Cb", h)], w=[kpa])
                            info[h] = (k0t, k1t, kk0, kk1, pa, kpa)
                            di = cnt["dc"] % 2
                            cnt["dc"] += 1
                            pd, kpd = p_dc[di], "pdc%d" % di
                            S.add("pe", lambda e: e.matmul(pd[:, 0:257], lhsT=k0t[:], rhs=vb[:, h, :], start=True, stop=True),
                                  r=[kk0, kv], w=[kpd])
                            S.add("dve", lambda e: e.scalar_tensor_tensor(out=Cf[:, h, :], in0=Cf[:, h, :], scalar=gd0[:, n, h:h + 1],
                                                                          in1=pd[:, 0:257], op0=ALU.mult, op1=ALU.add),
                                  r=[("Cf", h), "gd0", kpd], w=[("Cf", h)])
                            S.add("act", lambda e: e.copy(out=Cb[:, h, :], in_=Cf[:, h, :]), r=[("Cf", h)], w=[("Cb", h)])
                        for h in grp:
                            k0t, k1t, kk0, kk1, pa, kpa = info[h]
                            S.add("pe", lambda e: e.matmul(pa[:, 0:257], lhsT=z1[:, h, :], rhs=Cb[:, h, :], start=False, stop=True),
                                  r=[kz1, ("Cb", h)], w=[kpa])
                            di = cnt["dc"] % 2
                            cnt["dc"] += 1
                            pd, kpd = p_dc[di], "pdc%d" % di
                            S.add("pe", lambda e: e.matmul(pd[:, 0:257], lhsT=k1t[:], rhs=vb[:, h, :], start=True, stop=True),
                                  r=[kk1, kv], w=[kpd])
                            S.add("dve", lambda e: e.scalar_tensor_tensor(out=Cf[:, h, :], in0=Cf[:, h, :], scalar=gd1[:, n, h:h + 1],
                                                                          in1=pd[:, 0:257], op0=ALU.mult, op1=ALU.add),
                                  r=[("Cf", h), "gd1", kpd], w=[("Cf", h)])
                            S.add("act", lambda e: e.copy(out=Cb[:, h, :], in_=Cf[:, h, :]), r=[("Cf", h)], w=[("Cb", h)])
                            dcol = (h % 8) * 2
                            S.add("act", lambda e: e.activation(out=dsm[:, dcol:dcol + 1], in_=pa[:, 256:257], func=AF.Abs),
                                  r=[kpa], w=[("dsm", dcol)])
                            S.add("dve", lambda e: e.tensor_tensor(out=dsm[:, dcol:dcol + 1], in0=dsm[:, dcol:dcol + 1], in1=ge[:, n, h:h + 1], op=ALU.max),
                                  r=[("dsm", dcol), "ge"], w=[("dsm", dcol)])
                            S.add("dve", lambda e: e.reciprocal(out=dsm[:, dcol + 1:dcol + 2], in_=dsm[:, dcol:dcol + 1]),
                                  r=[("dsm", dcol)], w=[("dsm", dcol + 1)])
                            S.add("act", lambda e: e.activation(out=hl[:, h, :], in_=pa[:, 0:256], func=AF.Copy, scale=dsm[:, dcol + 1:dcol + 2]),
                                  r=[kpa, ("dsm", dcol + 1)], w=[(khl, h)])
                    hk = [(khl, h) for h in range(H)]
                    hl2 = hl[:].rearrange("p h v -> p (h v)")
                    S.add("dve", lambda e: e.tensor_tensor(out=hl2, in0=hl2, in1=sob[:], op=ALU.mult), r=hk + [kso], w=[khl])
                    S.add("dve", lambda e: e.tensor_reduce(out=st8[:, 0, :], in_=hl[:], axis=AX.X, op=ALU.add), r=[khl], w=[("st8", 0)])
                    S.add("act", lambda e: e.activation(out=sqb[:], in_=hl[:], func=AF.Square), r=[khl], w=["sqb"])
                    S.add("dve", lambda e: e.tensor_reduce(out=st8[:, 1, :], in_=sqb[:], axis=AX.X, op=ALU.add), r=["sqb"], w=[("st8", 1)])
                    S.add("dve", lambda e: e.tensor_scalar(out=st8[:, 2, :], in0=st8[:, 0, :], scalar1=1.0 / 256, scalar2=0.0, op0=ALU.mult, op1=ALU.add),
                          r=[("st8", 0)], w=[("st8", 2)])
                    S.add("dve", lambda e: e.tensor_tensor(out=st8[:, 3, :], in0=st8[:, 2, :], in1=st8[:, 2, :], op=ALU.mult), r=[("st8", 2)], w=[("st8", 3)])
                    S.add("dve", lambda e: e.scalar_tensor_tensor(out=st8[:, 4, :], in0=st8[:, 1, :], scalar=1.0 / 256, in1=st8[:, 3, :], op0=ALU.mult, op1=ALU.subtract),
                          r=[("st8", 1), ("st8", 3)], w=[("st8", 4)])
                    S.add("act", lambda e: e.activation(out=st8[:, 5, :], in_=st8[:, 4, :], func=AF.Ln, scale=1.0, bias=cc(C_EPS)), r=[("st8", 4), "cst"], w=[("st8", 5)])
                    S.add("act", lambda e: e.activation(out=st8[:, 6, :], in_=st8[:, 5, :], func=AF.Exp, scale=-0.5), r=[("st8", 5)], w=[("st8", 6)])
                    S.add("dve", lambda e: e.tensor_tensor(out=hl[:], in0=hl[:], in1=bc_last(st8[:, 2, :], 256), op=ALU.subtract), r=[khl, ("st8", 2)], w=[khl])
                    S.add("dve", lambda e: e.tensor_tensor(out=hl[:], in0=hl[:], in1=bc_last(st8[:, 6, :], 256), op=ALU.mult), r=[khl, ("st8", 6)], w=[khl])
                    S.add("dve", lambda e: e.tensor_tensor(out=hl2, in0=hl2, in1=hgrep[:], op=ALU.mult), r=[khl, "hgrep"], w=[khl])
                    S.add("dve", lambda e: e.tensor_tensor(out=ybf[:], in0=hl2, in1=szb_[:], op=ALU.mult), r=[khl, ksz], w=["ybf"])
                    yt, kyt = yts[n % 2], "yts%d" % (n % 2)
                    for g in range(KC // 4):
                        for i in range(4):
                            kc = g * 4 + i
                            S.add("pe", lambda e: e.transpose(p_kt[:, i, :], ybf[:, kc * 128:(kc + 1) * 128], idb[:]),
                                  r=["ybf", "idb"], w=[("pkt", i)])
                        S.add("act", lambda e: e.copy(out=yt[:, g * 4:(g + 1) * 4, :], in_=p_kt[:]),
                              r=[("pkt", i) for i in range(4)], w=[kyt])
                    for k0 in range(0, KC, 4):
                        S.dma("sp", y_fm[k0 * 128:(k0 + 4) * 128, n * 128:(n + 1) * 128].rearrange("(kc p) t -> p kc t", p=128), yt[:, k0:k0 + 4, :], r=[kyt], w=["y_fm"])

                lc = [0]

                def lru_step(h, st):
                    bi = lc[0] % 2
                    lc[0] += 1
                    xcb, kxc = xcs[bi], "xcs%d" % bi
                    zbb, kzb = zbs[bi], "zbs%d" % bi
                    S.dma("sp", xcb[:], xc_fm[h * 256:(h + 1) * 256, st * SW:(st + 1) * SW].rearrange("(b p) t -> p b t", p=128), r=["fm_out"], w=[kxc])
                    S.dma("sp", zbb[:], szb_fm[h * 256:(h + 1) * 256, st * SW:(st + 1) * SW].rearrange("(b p) t -> p b t", p=128), r=["fm_out"], w=[kzb])
                    for jb in range(2):
                        blk = h * 2 + jb
                        pr_, kpr_ = p_acc[2], "pacc2"
                        pi_, kpi_ = p_acc[3], "pacc3"
                        for ib in range(2):
                            S.add("pe", lambda e: e.matmul(pr_[:, 0:SW], lhsT=wab[:, h, ib, jb * 128:(jb + 1) * 128], rhs=xcb[:, ib, :], start=(ib == 0), stop=(ib == 1)),
                                  r=["wab", kxc], w=[kpr_])
                        for ib in range(2):
                            S.add("pe", lambda e: e.matmul(pi_[:, 0:SW], lhsT=wxb[:, h, ib, jb * 128:(jb + 1) * 128], rhs=xcb[:, ib, :], start=(ib == 0), stop=(ib == 1)),
                                  r=["wxb", kxc], w=[kpi_])
                        (er, ker), (ei, kei), (la, kla), (a2, ka2), (hh, khh) = [(lt[nm][jb], "%s%d" % (nm, jb)) for nm in ("er", "ei", "la", "a2", "hh")]
                        S.add("act", lambda e: e.activation(out=er[:], in_=pr_[:, 0:SW], func=AF.Exp, scale=-1.0, bias=ccoef[:, 2, blk:blk + 1]),
                              r=[kpr_, "ccoef2"], w=[ker])
                        S.add("act", lambda e: e.activation(out=ei[:], in_=pi_[:, 0:SW], func=AF.Exp, scale=-1.0, bias=ccoef[:, 3, blk:blk + 1]),
                              r=[kpi_, "ccoef2"], w=[kei])
                        S.add("dve", lambda e: e.tensor_scalar(out=er[:], in0=er[:], scalar1=1.0, scalar2=0.0, op0=ALU.add, op1=ALU.add), r=[ker], w=[ker])
                        S.add("dve", lambda e: e.reciprocal(out=er[:], in_=er[:]), r=[ker], w=[ker])
                        S.add("dve", lambda e: e.tensor_scalar(out=ei[:], in0=ei[:], scalar1=1.0, scalar2=0.0, op0=ALU.add, op1=ALU.add), r=[kei], w=[kei])
                        S.add("dve", lambda e: e.reciprocal(out=ei[:], in_=ei[:]), r=[kei], w=[kei])
                        S.add("act", lambda e: e.activation(out=la[:], in_=er[:], func=AF.Exp, scale=ccoef[:, 0, blk:blk + 1]), r=[ker, "ccoef"], w=[kla])
                        S.add("act", lambda e: e.activation(out=a2[:], in_=er[:], func=AF.Exp, scale=ccoef[:, 1, blk:blk + 1]), r=[ker, "ccoef"], w=[ka2])
                        S.add("act", lambda e: e.activation(out=a2[:], in_=a2[:], func=AF.Ln, scale=-1.0, bias=cc(C_ONE)), r=[ka2, "cst"], w=[ka2])
                        S.add("act", lambda e: e.activation(out=a2[:], in_=a2[:], func=AF.Exp, scale=0.5), r=[ka2], w=[ka2])
                        S.add("dve", lambda e: e.tensor_tensor(out=ei[:], in0=ei[:], in1=xcb[:, jb, :], op=ALU.mult), r=[kei, kxc], w=[kei])
                        S.add("dve", lambda e: e.tensor_tensor(out=ei[:], in0=ei[:], in1=a2[:], op=ALU.mult), r=[kei, ka2], w=[kei])
                        S.add("dve", lambda e: e.tensor_tensor_scan(out=hh[:], data0=la[:], data1=ei[:], initial=hcar[:, blk:blk + 1],
                                                                    op0=ALU.mult, op1=ALU.add),
                              r=[kla, kei, ("hcar", blk)], w=[khh])
                        S.add("act", lambda e: e.copy(out=hcar[:, blk:blk + 1], in_=hh[:, SW - 1:SW]), r=[khh], w=[("hcar", blk)])
                        ybt, kyb = yb[jb], "yb%d" % jb
                        S.add("dve", lambda e: e.tensor_tensor(out=ybt[:], in0=hh[:], in1=zbb[:, jb, :], op=ALU.mult), r=[khh, kzb], w=[kyb])
                        S.dma("act", y_fm[D + blk * 128:D + (blk + 1) * 128, st * SW:(st + 1) * SW], ybt[:], r=[kyb], w=["y_fm"])

                steps = [(h, st) for h in range(H) for st in range(NST)]
                per = (len(steps) + NT - 1) // NT
                si_ = 0
                for n in range(NT):
                    mlstm_tile(n)
                    for _ in range(per):
                        if si_ < len(steps):
                            lru_step(*steps[si_])
                            si_ += 1
                while si_ < len(steps):
                    lru_step(*steps[si_])
                    si_ += 1
                S.flush()
            if stop == "MR":
                return nc

            with contextlib.ExitStack() as es:
                wob = [sb(es, "wob%d" % i, [128, KO, OCB], BF16) for i in range(2)]
                yT = [sb(es, "yT%d" % i, [128, KO, SW], BF16) for i in range(2)]
                xr = [sb(es, "xr%d" % i, [128, OCB]) for i in range(3)]
                pso = [ps(es, "pso%d" % i, [128, 512]) for i in range(4)]
                NCB = D // OCB
                ostep = max(1, KO // 8)

                def load_wo(cb):
                    wb = wob[cb % 2]
                    for k0 in range(0, KO, ostep):
                        S.dma("pool", wb[:, k0:k0 + ostep, :].rearrange("p k c -> p (k c)"),
                              wod[cb][:, k0 * OCB:(k0 + ostep) * OCB], r=["wbd"], w=[("wob", cb % 2, k0)])
                load_wo(0)
                yi = 0
                xi = 0
                for cb in range(NCB):
                    if cb + 1 < NCB:
                        load_wo(cb + 1)
                    wb = wob[cb % 2]
                    wkeys = [("wob", cb % 2, k0) for k0 in range(0, KO, ostep)]
                    for st in range(NST):
                        yb_, kyb_ = yT[yi % 2], "yT%d" % (yi % 2)
                        yi += 1
                        hk = max(1, KO // 4)
                        for k0 in range(0, KO, hk):
                            S.dma("sp", yb_[:, k0:k0 + hk, :], y_fm[k0 * 128:(k0 + hk) * 128, st * SW:(st + 1) * SW].rearrange("(kc p) t -> p kc t", p=128),
                                  r=["y_fm"], w=[(kyb_, k0)])
                        for ti in range(TPS):
                            n = st * TPS + ti
                            xb_, kxb_ = xr[xi % 3], "xr%d" % (xi % 3)
                            po, kpo = pso[xi % 4], "pso%d" % (xi % 4)
                            xi += 1
                            S.dma("act", xb_[:], xsrc[n * 128:(n + 1) * 128, cb * OCB:(cb + 1) * OCB], r=[("xres", n, cb)], w=[kxb_])
                            for kc in range(KO):
                                S.add("pe", lambda e: e.matmul(po[:, 0:OCB], lhsT=yb_[:, kc, ti * 128:(ti + 1) * 128], rhs=wb[:, kc, :],
                                                               start=(kc == 0), stop=(kc == KO - 1)),
                                      r=wkeys + [(kyb_, k0) for k0 in range(0, KO, hk)], w=[kpo])
                            S.add("dve", lambda e: e.tensor_tensor(out=xb_[:], in0=po[:, 0:OCB], in1=xb_[:], op=ALU.add), r=[kpo, kxb_], w=[kxb_])
                            S.dma("act", xres[n * 128:(n + 1) * 128, cb * OCB:(cb + 1) * OCB], xb_[:], r=[kxb_], w=[("xres", n, cb)])
                S.flush()

        if stop == "P4":
            return nc
        with contextlib.ExitStack() as es:
            grep = sb(es, "fgrep", [128, D])
            xt = [sb(es, "fxt%d" % i, [128, D]) for i in range(2)]
            ot = [sb(es, "fot%d" % i, [128, D]) for i in range(2)]
            junk = sb(es, "fjunk", [128, D], BF16)
            sst = sb(es, "fsst", [128, 8])
            S.dma("act", grep[:], bass.AP(fin_g.tensor, 0, [[0, 128], [1, D]]), w=["grep"])
            for n in range(NT):
                bi = n % 2
                xb_, ob_ = xt[bi], ot[bi]
                kx, ko = "xt%d" % bi, "ot%d" % bi
                c0 = bi * 4
                S.dma("sp", xb_[:], xres[n * 128:(n + 1) * 128, :], w=[kx])
                S.add("act", lambda e: e.activation(out=junk[:], in_=xb_[:], func=AF.Square, accum_out=sst[:, c0:c0 + 1]), r=[kx], w=["junk", ("sst", c0)])
                S.add("act", lambda e: e.activation(out=sst[:, c0 + 1:c0 + 2], in_=sst[:, c0:c0 + 1], func=AF.Ln, scale=1.0 / D, bias=cc(C_EPS)),
                      r=[("sst", c0), "cst"], w=[("sst", c0 + 1)])
                S.add("act", lambda e: e.activation(out=sst[:, c0 + 2:c0 + 3], in_=sst[:, c0 + 1:c0 + 2], func=AF.Exp, scale=-0.5), r=[("sst", c0 + 1)], w=[("sst", c0 + 2)])
                S.add("dve", lambda e: e.scalar_tensor_tensor(out=ob_[:], in0=xb_[:], scalar=sst[:, c0 + 2:c0 + 3], in1=grep[:], op0=ALU.mult, op1=ALU.mult),
                      r=[kx, ("sst", c0 + 2), "grep"], w=[ko])
                hw_ = min(D, 1024)
                for c1 in range(0, D, hw_):
                    S.dma("sp", out_d[n * 128:(n + 1) * 128, c1:c1 + hw_], ob_[:, c1:c1 + hw_], r=[ko], w=["out"])
            S.flush()
    return nc


def host_inputs(cfg, x, norm_g, w_in, i_bias, f_bias, qk_conv, head_norm_g, lru_conv_w, lru_conv_b,
                w_a, b_a, w_x, b_x, lam, w_out, final_g):
    f = lambda a: np.ascontiguousarray(np.asarray(a, dtype=np.float32))
    L, D, T, NB = cfg.L, cfg.D, cfg.T, cfg.NB
    x = f(x)

    def chanpack(v):
        v = f(v)
        return v.reshape(L, -1, 128).transpose(0, 2, 1)

    def convpack(wc):
        wc = f(wc)
        C = wc.shape[2]
        return wc.reshape(L, 4, C // 128, 128).transpose(0, 3, 2, 1).reshape(L, 128, (C // 128) * 4)

    chan = np.concatenate([convpack(qk_conv), convpack(lru_conv_w), chanpack(lru_conv_b), chanpack(b_a),
                           chanpack(b_x), chanpack(lam)], axis=2)
    assert chan.shape == (L, 128, cfg.CW), chan.shape
    rowv = np.concatenate([f(norm_g), f(head_norm_g), f(i_bias), f(f_bias)], axis=1)
    assert rowv.shape == (L, cfg.RW)
    cstv = make_consts()
    shared = {
        "w_in": f(w_in)[:L], "w_out": f(w_out)[:L], "w_a": f(w_a)[:L], "w_x": f(w_x)[:L],
        "chan": np.ascontiguousarray(chan), "rowv": np.ascontiguousarray(rowv),
        "final_g": f(final_g).reshape(1, D), "cst": cstv,
    }
    maps = []
    zx = np.zeros((T, D), np.float32)
    idle = dict(shared)
    for k in ("w_in", "w_out", "w_a", "w_x"):
        idle[k] = np.zeros_like(shared[k])
    for c in range(cfg.NCORES):
        m = dict(shared) if c < cfg.BATCH else dict(idle)
        m["x"] = np.ascontiguousarray(x[c]) if c < cfg.BATCH else zx
        maps.append(m)
    return maps


_NC_CACHE = {}


def run(cfg, inputs, dbg=()):
    key = (cfg.D, cfg.T, cfg.L, tuple(dbg))
    if key not in _NC_CACHE:
        _NC_CACHE[key] = build(cfg, dbg)
    nc = _NC_CACHE[key]
    maps = host_inputs(cfg, **inputs)
    res = run_bass_kernel_spmd(nc, maps, core_ids=list(range(cfg.NCORES)))
    return res


def kernel(**inputs):
    cfg = Cfg()
    res = run(cfg, inputs)
    return np.stack([np.asarray(res.results[b]["out"], dtype=np.float32) for b in range(cfg.BATCH)], axis=0)
```

```python
import contextlib
import numpy as np
import concourse.bass as bass
import concourse.mybir as mybir
from concourse.bass_utils import run_bass_kernel_spmd

F32, BF16 = mybir.dt.float32, mybir.dt.bfloat16
ALU = mybir.AluOpType
AF = mybir.ActivationFunctionType
AX = mybir.AxisListType
EPS = 1e-6
LRU_C = 8.0


class Cfg:
    def __init__(self, D=2048, T=8192, L=4, NCORES=8, BATCH=2, TM=2048):
        self.D, self.T, self.L, self.NCORES, self.BATCH = D, T, L, NCORES, BATCH
        self.NSEG = 1
        self.TM = min(TM, T)
        self.NMT = T // self.TM
        self.NTM = self.TM // 128
        H = self.H = D // 256
        self.NQK = H * 128
        self.NIN = 2 * self.NQK + 3 * D + 2 * H + 2 * D
        self.KC = D // 128
        self.NT = T // 128
        self.SW = min(512, self.TM)
        self.NST = T // self.SW
        self.NSTM = self.TM // self.SW
        self.TPS = self.SW // 128
        self.NB = D // 128
        self.c_q = 0
        self.c_k = self.NQK
        self.c_v = 2 * self.NQK
        self.c_o = self.c_v + D
        self.c_za = self.c_o + D
        self.c_i = self.c_za + D
        self.c_xb = self.c_i + 2 * H
        self.c_zb = self.c_xb + D
        self.HW = 3 * D // 128
        self.oC = 0
        self.oD = H * 257
        self.oHE = self.oD + H
        self.oA = self.oHE + self.NB
        self.SUMF = self.oA + self.NB
        self.nqkb = 2 * self.NQK // 128
        self.p_qkc = 0
        self.p_lcw = self.p_qkc + self.nqkb * 4
        self.p_lcb = self.p_lcw + self.NB * 4
        self.p_ba = self.p_lcb + self.NB
        self.p_bx = self.p_ba + self.NB
        self.p_lam = self.p_bx + self.NB
        self.CW = self.p_lam + self.NB
        self.r_ng = 0
        self.r_hg = D
        self.r_ib = 2 * D
        self.RW = 2 * D + 2 * H
        self.KO = 2 * D // 128
        self.OCB = min(512, D)


C_ID, C_L, C_B, C_E0, C_E1 = 0, 128, 256, 384, 512
C_HM0, C_HM1, C_EPS, C_ONE, C_LSK, C_EPS4 = 640, 641, 642, 643, 644, 645
CCW = 648


def make_consts():
    c = np.zeros((128, CCW), np.float32)
    s = np.arange(128)[:, None]
    t = np.arange(128)[None, :]
    c[:, C_ID:C_ID + 128] = (s == t)
    c[:, C_L:C_L + 128] = (s <= t) & (s // 64 == t // 64)
    c[:, C_B:C_B + 128] = (s // 64 == t // 64)
    c[:, C_E0:C_E0 + 128] = (s < 64)
    c[:, C_E1:C_E1 + 128] = (s >= 64)
    c[:, C_HM0] = (np.arange(128) < 64)
    c[:, C_HM1] = (np.arange(128) >= 64)
    c[:, C_EPS] = EPS
    c[:, C_ONE] = 1.0
    c[:, C_LSK] = 0.5 * np.log(128.0)
    c[:, C_EPS4] = EPS
    return c


class Op:
    __slots__ = ("eng", "fn", "deps", "kind", "sig", "slot", "cnt")

    def __init__(self, eng, fn, deps, kind):
        self.eng, self.fn, self.deps, self.kind = eng, fn, deps, kind
        self.sig = False
        self.slot = None
        self.cnt = 0


class _Rec:
    def __init__(self):
        self.call = None

    def __getattr__(self, name):
        def f(*a, **k):
            self.call = (name, a, k)
            return self
        return f

    def play(self, eng):
        name, a, k = self.call
        return getattr(eng, name)(*a, **k)


class Sched:
    NDS = 6
    RESET = False

    def __init__(self, nc, es):
        self.nc = nc
        self.eng = {"sp": nc.sync, "act": nc.scalar, "pool": nc.gpsimd, "dve": nc.vector, "pe": nc.tensor}
        self.esem = {e: es.enter_context(nc.semaphore("e_" + e)) for e in self.eng}
        self.ecnt = {e: 0 for e in self.eng}
        self.dsem = {q: [es.enter_context(nc.semaphore("d_%s%d" % (q, i))) for i in range(self.NDS)]
                     for q in ("sp", "act", "pool")}
        self.dcnt = {q: [0] * self.NDS for q in self.dsem}
        self.dnext = {q: 0 for q in self.dsem}
        self.csem = es.enter_context(nc.semaphore("cc"))
        self.ccnt = 0
        self.seen = {e: {} for e in self.eng}
        self.reset()

    def reset(self):
        self.ops = []
        self.last_w = {}
        self.readers = {}

    def add(self, eng, fn, r=(), w=(), kind="eng"):
        idx = len(self.ops)
        deps = set()
        for k in r:
            if k in self.last_w:
                deps.add(self.last_w[k])
        for k in w:
            if k in self.last_w:
                deps.add(self.last_w[k])
            deps.update(self.readers.get(k, ()))
        for k in r:
            self.readers.setdefault(k, []).append(idx)
        for k in w:
            self.last_w[k] = idx
            self.readers[k] = []
        deps.discard(idx)
        latest = {}
        pruned = set()
        for d in deps:
            o = self.ops[d]
            if o.kind == "eng":
                if o.eng not in latest or latest[o.eng] < d:
                    latest[o.eng] = d
            else:
                pruned.add(d)
        pruned.update(latest.values())
        deps = pruned
        rec = _Rec()
        fn(rec)
        self.ops.append(Op(eng, rec, deps, kind))
        return idx

    def dma(self, q, out, in_, r=(), w=()):
        return self.add(q, lambda e: e.dma_start(out=out, in_=in_), r, w, kind="dma")

    def _wait(self, eng, sem, val):
        key = id(sem)
        if self.seen[eng].get(key, 0) >= val:
            return
        self.seen[eng][key] = val
        self.eng[eng].wait_ge(sem, val)

    def flush(self, barrier=True):
        ops = self.ops
        for op in ops:
            for d in op.deps:
                dep = ops[d]
                if dep.kind == "eng" and dep.eng == "pe" and op.eng == "pe" and op.kind == "eng":
                    continue
                dep.sig = True
        for op in ops:
            eng = op.eng
            for d in sorted(op.deps):
                dep = ops[d]
                if not dep.sig:
                    continue
                if dep.kind == "eng":
                    self._wait(eng, self.esem[dep.eng], dep.cnt)
                elif dep.kind == "dma":
                    self._wait(eng, self.dsem[dep.eng][dep.slot], dep.cnt)
                else:
                    self._wait(eng, self.csem, dep.cnt)
            if op.kind == "dma":
                s = self.dnext[eng]
                self.dnext[eng] = (s + 1) % self.NDS
                self._wait(eng, self.dsem[eng][s], self.dcnt[eng][s])
                ins = op.fn.play(self.eng[eng])
                self.dcnt[eng][s] += 16
                op.slot, op.cnt = s, self.dcnt[eng][s]
                ins.then_inc(self.dsem[eng][s], 16)
            elif op.kind == "cc":
                self._wait(eng, self.csem, self.ccnt)
                ins = op.fn.play(self.eng[eng])
                self.ccnt += 1
                op.cnt = self.ccnt
                ins.then_inc(self.csem, 1)
            else:
                ins = op.fn.play(self.eng[eng])
                if op.sig:
                    self.ecnt[eng] += 1
                    op.cnt = self.ecnt[eng]
                    ins.then_inc(self.esem[eng], 1)
        self.reset()
        if barrier:
            self.drain()

    def drain(self):
        for q in self.dsem:
            for s in range(self.NDS):
                self._wait(q, self.dsem[q][s], self.dcnt[q][s])
        self.nc.all_engine_barrier()
        if self.RESET:
            for e in self.esem:
                self.nc.gpsimd.sem_clear(self.esem[e])
                self.ecnt[e] = 0
            for q in self.dsem:
                for i in range(self.NDS):
                    self.nc.gpsimd.sem_clear(self.dsem[q][i])
                    self.dcnt[q][i] = 0
            self.seen = {e: {} for e in self.eng}
            self.nc.all_engine_barrier()


def apx(ap, extra):
    return bass.AP(ap.tensor, ap.offset, [list(a) for a in ap.ap] + [list(e) for e in extra])


def bc_mid(ap2, n):
    a = [list(x) for x in ap2.ap]
    return bass.AP(ap2.tensor, ap2.offset, [a[0], [0, n]] + a[1:])


def bc_last(ap, n):
    return apx(ap, [[0, n]])


def build(cfg, dbg=(), stop=None):
    D, T, L, H, KC, NT, SW, NST, TPS, NB = cfg.D, cfg.T, cfg.L, cfg.H, cfg.KC, cfg.NT, cfg.SW, cfg.NST, cfg.TPS, cfg.NB
    TM, NMT, NTM, NSTM = cfg.TM, cfg.NMT, cfg.NTM, cfg.NSTM
    NQK, NIN, KO, OCB = cfg.NQK, cfg.NIN, cfg.KO, cfg.OCB
    WB = min(512, D)
    NSB = WB // 128
    nc = bass.Bass("TRN2", target_bir_lowering=False)

    def din(name, shape, dt=F32):
        return nc.dram_tensor(name, list(shape), dt, kind="ExternalInput").ap()

    def dscr(name, shape, dt=F32):
        kind = "ExternalOutput" if name in dbg else "Internal"
        return nc.dram_tensor(name, list(shape), dt, kind=kind).ap()

    x_in = din("x", [T, D])
    w_in = din("w_in", [L, D, NIN])
    w_out = din("w_out", [L, 2 * D, D])
    w_a = din("w_a", [L, H, 256, 256])
    w_x = din("w_x", [L, H, 256, 256])
    chan = din("chan", [L, 128, cfg.CW])
    rowv = din("rowv", [L, cfg.RW])
    fin_g = din("final_g", [1, D])
    cst_d = din("cst", [128, CCW])
    out_d = nc.dram_tensor("out", [T, D], F32, kind="ExternalOutput").ap()

    xres = dscr("xres", [T, D])
    qk_fm = dscr("qk_fm", [2 * NQK, T], BF16)
    v_tok = dscr("v_tok", [T, D], BF16)
    so_tok = dscr("so_tok", [T, D], BF16)
    sza_tok = dscr("sza_tok", [T, D], BF16)
    if_d = dscr("if_d", [128, NT * 2 * H])
    xc_fm = dscr("xc_fm", [D, T], BF16)
    szb_fm = dscr("szb_fm", [D, T], BF16)
    y_fm = dscr("y_fm", [2 * D, T], BF16)
    NWB = (NIN - 2 * H) // WB
    wbd = dscr("wbd", [NWB, 128, KC * WB], BF16)
    wod = dscr("wod", [D // OCB, 128, KO * OCB], BF16)

    with contextlib.ExitStack() as es0:
        S = Sched(nc, es0)
        uid = [0]

        def sb(es, name, shape, dt=F32):
            uid[0] += 1
            return es.enter_context(nc.sbuf_tensor("s%d_%s" % (uid[0], name), list(shape), dt))

        def ps(es, name, shape, dt=F32):
            uid[0] += 1
            return es.enter_context(nc.psum_tensor("p%d_%s" % (uid[0], name), list(shape), dt))

        cst = sb(es0, "cst", [128, CCW])
        idb = sb(es0, "idb", [128, 128], BF16)
        mLb = sb(es0, "mLb", [128, 128], F32)
        chs = sb(es0, "chs", [128, cfg.CW])
        S.dma("sp", cst[:], cst_d, w=["cst"])
        S.add("dve", lambda e: e.tensor_copy(out=idb[:], in_=cst[:, C_ID:C_ID + 128]), r=["cst"], w=["idb"])
        S.add("dve", lambda e: e.tensor_copy(out=mLb[:], in_=cst[:, C_L:C_L + 128]), r=["cst"], w=["mLb"])
        S.flush()

        def cc(col):
            return cst[:, col:col + 1]

        for l in range(L):
            xsrc = x_in if l == 0 else xres
            S.dma("sp", chs[:], chan[l], w=["chs"])
            S.flush()

            blocks = []
            for c0 in range(cfg.c_q, cfg.c_v, WB):
                blocks.append(("qk", c0))
            for c0 in range(cfg.c_za, cfg.c_i, WB):
                blocks.append(("za", c0))
            for c0 in range(cfg.c_zb, NIN, WB):
                blocks.append(("zb", c0))
            for c0 in range(cfg.c_o, cfg.c_za, WB):
                blocks.append(("o", c0))
            for c0 in range(cfg.c_v, cfg.c_o, WB):
                blocks.append(("v", c0))
            for c0 in range(cfg.c_xb, cfg.c_zb, WB):
                blocks.append(("xb", c0))
            blocks.append(("if", cfg.c_i))
            NBLK = len(blocks)
            assert NBLK - 1 == NWB

            with contextlib.ExitStack() as es:
                KW = max(KC, KO)
                fst = [sb(es, "fst%d" % i, [128, KW // 2, WB]) for i in range(2)]
                bst = [sb(es, "bst%d" % i, [128, KW // 2, WB], BF16) for i in range(2)]
                jobs = []
                for bi in range(NWB):
                    c0 = blocks[bi][1]
                    for hf in range(2):
                        k0 = hf * (KC // 2)
                        jobs.append((w_in[l][k0 * 128:(k0 + KC // 2) * 128, c0:c0 + WB], KC // 2,
                                     wbd[bi][:, k0 * WB:(k0 + KC // 2) * WB]))
                for cb in range(D // OCB):
                    for hf in range(KO // (KW // 2)):
                        k0 = hf * (KW // 2)
                        jobs.append((w_out[l][k0 * 128:(k0 + KW // 2) * 128, cb * OCB:(cb + 1) * OCB], KW // 2,
                                     wod[cb][:, k0 * OCB:(k0 + KW // 2) * OCB]))
                for ji, (src, nk, dst) in enumerate(jobs):
                    fb, bb = fst[ji % 2], bst[ji % 2]
                    kf, kb_ = "fst%d" % (ji % 2), "bst%d" % (ji % 2)
                    q2 = max(1, nk // 2)
                    for k0 in range(0, nk, q2):
                        S.dma("act" if ji % 2 else "sp", fb[:, k0:k0 + q2, :], src[k0 * 128:(k0 + q2) * 128, :].rearrange("(kc p) c -> p kc c", p=128),
                              w=[(kf, k0)])
                    fkeys = [(kf, k0) for k0 in range(0, nk, q2)]
                    if ji % 2 == 0:
                        S.add("dve", lambda e: e.tensor_copy(out=bb[:, 0:nk, :], in_=fb[:, 0:nk, :]), r=fkeys, w=[kb_])
                    else:
                        S.add("act", lambda e: e.copy(out=bb[:, 0:nk, :], in_=fb[:, 0:nk, :]), r=fkeys, w=[kb_])
                    seg = min(nk, max(1, 2048 // WB))
                    for k0 in range(0, nk, seg):
                        S.dma("sp" if ji % 2 else "act", dst[:, k0 * WB:(k0 + seg) * WB], bb[:, k0:k0 + seg, :].rearrange("p k c -> p (k c)"),
                              r=[kb_], w=["wbd"])
                S.flush()

            with contextlib.ExitStack() as es:
                uT = sb(es, "uT", [128, KC, TM], BF16)
                grep = sb(es, "grep", [128, D])
                xt = [sb(es, "xt%d" % i, [128, D]) for i in range(2)]
                ub = [sb(es, "ub%d" % i, [128, D], BF16) for i in range(2)]
                junk = sb(es, "junk", [128, D], BF16)
                sst = sb(es, "sst", [128, 8])
                wbf = [sb(es, "wbf%d" % i, [128, KC, WB], BF16) for i in range(2)]
                wif = sb(es, "wif", [128, KC, 2 * H], BF16)
                pre = [sb(es, "pre%d" % i, [128, 4 + TM]) for i in range(2)]
                cacc = [sb(es, "cacc%d" % i, [128, TM]) for i in range(1)]
                stg = [sb(es, "stg%d" % i, [128, TM], BF16) for i in range(2)]
                tstg = [sb(es, "tstg%d" % i, [128, NTM, WB], BF16) for i in range(1)]
                ifs = sb(es, "ifs", [128, NTM, 2 * H])
                hsv = sb(es, "hsv", [128, cfg.nqkb + NB, 4])
                psA = [ps(es, "psA%d" % i, [128, 512]) for i in range(5)]
                pst = [ps(es, "pst%d" % i, [128, 4, 128], BF16) for i in range(2)]

                S.dma("act", grep[:], bass.AP(rowv.tensor, l * cfg.RW + cfg.r_ng, [[0, 128], [1, D]]), w=["grep"])
                S.add("pool", lambda e: e.memset(hsv[:], 0.0), w=["hsv"])

                def norm_tile(gn, n):
                    bi = gn % 2
                    xb_, ub_ = xt[bi], ub[bi]
                    kx, ku = "xt%d" % bi, "ub%d" % bi
                    S.dma("sp", xb_[:], xsrc[gn * 128:(gn + 1) * 128, :], r=["xres"], w=[kx])
                    c0 = (bi * 4)
                    S.add("act", lambda e: e.activation(out=junk[:], in_=xb_[:], func=AF.Square,
                                                        accum_out=sst[:, c0:c0 + 1]), r=[kx], w=["junk", ("sst", c0)])
                    S.add("act", lambda e: e.activation(out=sst[:, c0 + 1:c0 + 2], in_=sst[:, c0:c0 + 1], func=AF.Ln,
                                                        scale=1.0 / D, bias=cc(C_EPS)), r=[("sst", c0), "cst"], w=[("sst", c0 + 1)])
                    S.add("act", lambda e: e.activation(out=sst[:, c0 + 2:c0 + 3], in_=sst[:, c0 + 1:c0 + 2], func=AF.Exp,
                                                        scale=-0.5), r=[("sst", c0 + 1)], w=[("sst", c0 + 2)])
                    S.add("dve", lambda e: e.scalar_tensor_tensor(out=ub_[:], in0=xb_[:], scalar=sst[:, c0 + 2:c0 + 3],
                                                                  in1=grep[:], op0=ALU.mult, op1=ALU.mult),
                          r=[kx, ("sst", c0 + 2), "grep"], w=[ku])
                    for g in range(KC // 4):
                        pt = pst[g % 2]
                        for i in range(4):
                            kc = g * 4 + i
                            S.add("pe", lambda e: e.transpose(pt[:, i, :], ub_[:, kc * 128:(kc + 1) * 128], idb[:]),
                                  r=[ku, "idb"], w=["pst%d" % (g % 2)])
                        dst, kd = uT[:, g * 4:(g + 1) * 4, n * 128:(n + 1) * 128], ("uT", n)
                        if g % 2 == 0:
                            S.add("act", lambda e: e.copy(out=dst, in_=pt[:]), r=["pst%d" % (g % 2)], w=[kd])
                        else:
                            S.add("dve", lambda e: e.tensor_copy(out=dst, in_=pt[:]), r=["pst%d" % (g % 2)], w=[kd])

                wstep = max(1, KC // 4)
                wseq = [0]

                def load_w(bi):
                    kind, c0 = blocks[bi]
                    if kind == "if":
                        for k0 in range(0, KC, wstep):
                            S.dma("pool", wif[:, k0:k0 + wstep, :], w_in[l][k0 * 128:(k0 + wstep) * 128, c0:c0 + 2 * H].rearrange("(kc p) c -> p kc c", p=128), w=["wif"])
                        return None
                    slot = wseq[0] % 2
                    wseq[0] += 1
                    wb = wbf[slot]
                    for k0 in range(0, KC, wstep):
                        S.dma("pool", wb[:, k0:k0 + wstep, :].rearrange("p k c -> p (k c)"),
                              wbd[bi][:, k0 * WB:(k0 + wstep) * WB], r=["wbd"], w=[("wbf", slot, k0)])
                    return slot

                psi = [0]
                fmi = [0]

                def next_ps():
                    i = psi[0] % 5
                    psi[0] += 1
                    return psA[i], "psA%d" % i

                def fm_block(bi, slot, mt):
                    kind, c0 = blocks[bi]
                    wb = wbf[slot]
                    wkeys = [("wbf", slot, k0) for k0 in range(0, KC, wstep)]
                    t0 = mt * TM
                    for sbk in range(NSB):
                        fi = fmi[0] % 2
                        fmi[0] += 1
                        pr, ca, sg = pre[fi], cacc[0], stg[fi]
                        kpr, kca, ksg = "pre%d" % fi, "cacc0", "stg%d" % fi
                        conv = kind in ("qk", "xb")
                        if conv:
                            if kind == "qk":
                                blk = (c0 - cfg.c_q) // 128 + sbk
                                hb_ = blk
                                wcol = cfg.p_qkc + blk * 4
                                dst_rows = qk_fm[blk * 128:(blk + 1) * 128, t0:t0 + TM]
                            else:
                                blk = (c0 - cfg.c_xb) // 128 + sbk
                                hb_ = cfg.nqkb + blk
                                wcol = cfg.p_lcw + blk * 4
                                dst_rows = xc_fm[blk * 128:(blk + 1) * 128, t0:t0 + TM]
                            S.add("dve", lambda e: e.tensor_copy(out=pr[:, 0:3], in_=hsv[:, hb_, 0:3]), r=["hsv", ("hsv", hb_)], w=[(kpr, "h")])
                        else:
                            blk = (c0 - cfg.c_zb) // 128 + sbk
                            dst_rows = szb_fm[blk * 128:(blk + 1) * 128, t0:t0 + TM]
                        for st in range(NSTM):
                            pA, kA = next_ps()
                            for kc in range(KC):
                                S.add("pe", lambda e: e.matmul(
                                    pA[:, 0:SW], lhsT=wb[:, kc, sbk * 128:(sbk + 1) * 128], rhs=uT[:, kc, st * SW:(st + 1) * SW],
                                    start=(kc == 0), stop=(kc == KC - 1)),
                                    r=wkeys + [("uT", st * TPS + i) for i in range(TPS)], w=[kA])
                            if conv:
                                S.add("act", lambda e: e.copy(out=pr[:, 3 + st * SW:3 + (st + 1) * SW], in_=pA[:, 0:SW]),
                                      r=[kA], w=[(kpr, st)])
                            else:
                                S.add("act", lambda e: e.activation(out=sg[:, st * SW:(st + 1) * SW], in_=pA[:, 0:SW], func=AF.Silu),
                                      r=[kA], w=[ksg])
                        if conv:
                            prk = [(kpr, "h")] + [(kpr, st) for st in range(NSTM)]
                            S.add("act", lambda e: e.copy(out=hsv[:, hb_, 0:3], in_=pr[:, TM:TM + 3]), r=prk, w=[("hsv", hb_)])
                            if kind == "xb":
                                bcol = cfg.p_lcb + blk
                                S.add("dve", lambda e: e.tensor_scalar(out=ca[:], in0=pr[:, 0:TM], scalar1=chs[:, wcol:wcol + 1],
                                                                       scalar2=chs[:, bcol:bcol + 1], op0=ALU.mult, op1=ALU.add),
                                      r=prk + ["chs"], w=[kca])
                            else:
                                S.add("dve", lambda e: e.tensor_scalar(out=ca[:], in0=pr[:, 0:TM], scalar1=chs[:, wcol:wcol + 1],
                                                                       scalar2=0.0, op0=ALU.mult, op1=ALU.add),
                                      r=prk + ["chs"], w=[kca])
                            for j in (1, 2):
                                S.add("dve", lambda e: e.scalar_tensor_tensor(out=ca[:], in0=pr[:, j:j + TM], scalar=chs[:, wcol + j:wcol + j + 1],
                                                                              in1=ca[:], op0=ALU.mult, op1=ALU.add),
                                      r=prk + ["chs", kca], w=[kca])
                            if kind == "xb":
                                S.add("dve", lambda e: e.scalar_tensor_tensor(out=sg[:], in0=pr[:, 3:3 + TM], scalar=chs[:, wcol + 3:wcol + 4],
                                                                              in1=ca[:], op0=ALU.mult, op1=ALU.add),
                                      r=prk + ["chs", kca], w=[ksg])
                            else:
                                S.add("dve", lambda e: e.scalar_tensor_tensor(out=ca[:], in0=pr[:, 3:3 + TM], scalar=chs[:, wcol + 3:wcol + 4],
                                                                              in1=ca[:], op0=ALU.mult, op1=ALU.add),
                                      r=prk + ["chs", kca], w=[kca])
                                S.add("act", lambda e: e.activation(out=sg[:], in_=ca[:], func=AF.Silu), r=[kca], w=[ksg])
                        S.dma("sp", dst_rows, sg[:], r=[ksg], w=["fm_out"])

                def tm_block(bi, slot, mt):
                    kind, c0 = blocks[bi]
                    wb = wbf[slot]
                    wkeys = [("wbf", slot, k0) for k0 in range(0, KC, wstep)]
                    tg, ktg = tstg[0], "tstg0"
                    for n in range(NTM):
                        pA, kA = next_ps()
                        for kc in range(KC):
                            S.add("pe", lambda e: e.matmul(
                                pA[:, 0:WB], lhsT=uT[:, kc, n * 128:(n + 1) * 128], rhs=wb[:, kc, :],
                                start=(kc == 0), stop=(kc == KC - 1)), r=wkeys + [("uT", n)], w=[kA])
                        if kind == "v":
                            if n % 2 == 0:
                                S.add("dve", lambda e: e.tensor_copy(out=tg[:, n, :], in_=pA[:, 0:WB]), r=[kA], w=[(ktg, n)])
                            else:
                                S.add("act", lambda e: e.copy(out=tg[:, n, :], in_=pA[:, 0:WB]), r=[kA], w=[(ktg, n)])
                        else:
                            fn = AF.Sigmoid if kind == "o" else AF.Silu
                            S.add("act", lambda e: e.activation(out=tg[:, n, :], in_=pA[:, 0:WB], func=fn), r=[kA], w=[(ktg, n)])
                    base = {"v": (v_tok, cfg.c_v), "o": (so_tok, cfg.c_o), "za": (sza_tok, cfg.c_za)}[kind]
                    cc0 = c0 - base[1]
                    q4 = max(1, NTM // 4)
                    for n0 in range(0, NTM, q4):
                        r0 = mt * TM + n0 * 128
                        S.dma("sp", base[0][r0:r0 + q4 * 128, cc0:cc0 + WB].rearrange("(n p) c -> p n c", p=128),
                              tg[:, n0:n0 + q4, :], r=[(ktg, n) for n in range(n0, n0 + q4)], w=["tm_out"])

                def if_block(mt):
                    for n in range(NTM):
                        pA, kA = next_ps()
                        for kc in range(KC):
                            S.add("pe", lambda e: e.matmul(
                                pA[:, 0:2 * H], lhsT=uT[:, kc, n * 128:(n + 1) * 128], rhs=wif[:, kc, :],
                                start=(kc == 0), stop=(kc == KC - 1)), r=["wif", ("uT", n)], w=[kA])
                        S.add("dve", lambda e: e.tensor_copy(out=ifs[:, n, :], in_=pA[:, 0:2 * H]), r=[kA], w=["ifs"])
                    S.dma("sp", if_d[:, mt * NTM * 2 * H:(mt + 1) * NTM * 2 * H], ifs[:].rearrange("p n c -> p (n c)"), r=["ifs"], w=["if_d"])

                seq = [(mt, bi) for mt in range(NMT) for bi in range(NBLK)]
                slots = {}
                slots[0] = load_w(seq[0][1])
                for si, (mt, bi) in enumerate(seq):
                    if bi == 0:
                        for n in range(NTM):
                            norm_tile(mt * NTM + n, n)
                    if si + 1 < len(seq):
                        slots[si + 1] = load_w(seq[si + 1][1])
                    kind = blocks[bi][0]
                    if kind in ("qk", "xb", "zb"):
                        fm_block(bi, slots[si], mt)
                    elif kind == "if":
                        if_block(mt)
                    else:
                        tm_block(bi, slots[si], mt)
                S.flush()
            if stop == "P1":
                return nc

            with contextlib.ExitStack() as es:
                ifs = sb(es, "ifs2", [128, NT, 2 * H])
                ibf = sb(es, "ibf", [128, 2 * H])
                zf = sb(es, "zf", [128, NT, H])
                lp = sb(es, "lp", [128, NT, H])
                t1 = sb(es, "t1", [128, NT, H])
                t2 = sb(es, "t2", [128, NT, H])
                ga = sb(es, "ga", [128, NT, H])
                gwk = sb(es, "gwk", [128, NT, H])
                gwk0 = sb(es, "gwk0", [128, NT, H])
                gwk1 = sb(es, "gwk1", [128, NT, H])
                ge = sb(es, "ge", [128, NT, H])
                gd0 = sb(es, "gd0", [128, NT, H])
                gd1 = sb(es, "gd1", [128, NT, H])
                Cf = sb(es, "Cf", [128, H, 257])
                Cb = sb(es, "Cb", [128, H, 257], BF16)
                hgrep = sb(es, "hgrep", [128, D])
                qs = [sb(es, "qs%d" % i, [128, H, SW], BF16) for i in range(2)]
                ks = [sb(es, "ks%d" % i, [128, H, SW], BF16) for i in range(2)]
                va = [sb(es, "va%d" % i, [128, H, 257], BF16) for i in range(2)]
                sos = [sb(es, "sos%d" % i, [128, D], BF16) for i in range(2)]
                szs = [sb(es, "szs%d" % i, [128, D], BF16) for i in range(2)]
                qz0 = [sb(es, "qz0_%d" % i, [128, H, 128], BF16) for i in range(2)]
                qz1 = [sb(es, "qz1_%d" % i, [128, H, 128], BF16) for i in range(2)]
                kw0 = [sb(es, "kw0_%d" % i, [128, 128], BF16) for i in range(4)]
                kw1 = [sb(es, "kw1_%d" % i, [128, 128], BF16) for i in range(4)]
                scT = [sb(es, "scT%d" % i, [128, 128], BF16) for i in range(4)]
                dsm = sb(es, "dsm", [128, 16])
                hall = [sb(es, "hall%d" % i, [128, H, 256]) for i in range(1)]
                sqb = sb(es, "sqb", [128, H, 256])
                st8 = sb(es, "st8", [128, 8, H])
                ybf = sb(es, "ybf", [128, D], BF16)
                yts = [sb(es, "yts%d" % i, [128, KC, 128], BF16) for i in range(2)]
                wab = sb(es, "wab", [128, H, 2, 256], BF16)
                wxb = sb(es, "wxb", [128, H, 2, 256], BF16)
                ccoef = sb(es, "ccoef", [128, 4, NB])
                hcar = sb(es, "hcar", [128, NB])
                xcs = [sb(es, "xcs%d" % i, [128, 2, SW], BF16) for i in range(2)]
                zbs = [sb(es, "zbs%d" % i, [128, 2, SW], BF16) for i in range(2)]
                lt = {nm: [sb(es, "%s%d" % (nm, i), [128, SW]) for i in range(2)]
                      for nm in ("er", "ei", "la", "a2", "hh")}
                yb = [sb(es, "yb%d" % i, [128, SW], BF16) for i in range(2)]

                p_acc = [ps(es, "pacc%d" % i, [128, 512]) for i in range(4)]
                p_dc = [ps(es, "pdc%d" % i, [128, 512]) for i in range(2)]
                p_ss = ps(es, "pss", [128, 4, 128])
                p_kt = ps(es, "pkt", [128, 4, 128], BF16)

                S.dma("sp", ifs[:].rearrange("p n c -> p (n c)"), if_d, r=["if_d"], w=["ifs"])
                S.dma("act", ibf[:], bass.AP(rowv.tensor, l * cfg.RW + cfg.r_ib, [[0, 128], [1, 2 * H]]), w=["ibf"])
                S.dma("act", hgrep[:], bass.AP(rowv.tensor, l * cfg.RW + cfg.r_hg, [[0, 128], [1, D]]), w=["hgrep"])
                for h0 in range(0, H, 2):
                    S.dma("pool", wab[:, h0:h0 + 2], w_a[l][h0:h0 + 2].rearrange("h (ib p) j -> p h ib j", p=128), w=["wab"])
                    S.dma("pool", wxb[:, h0:h0 + 2], w_x[l][h0:h0 + 2].rearrange("h (ib p) j -> p h ib j", p=128), w=["wxb"])
                S.add("dve", lambda e: e.tensor_tensor(out=ifs[:, :, 0:H], in0=ifs[:, :, 0:H], in1=bc_mid(ibf[:, 0:H], NT), op=ALU.add),
                      r=["ifs", "ibf"], w=["ifs"])
                S.add("dve", lambda e: e.tensor_tensor(out=zf[:], in0=ifs[:, :, H:2 * H], in1=bc_mid(ibf[:, H:2 * H], NT), op=ALU.add),
                      r=["ifs", "ibf"], w=["zf"])
                S.add("act", lambda e: e.activation(out=zf[:], in_=zf[:], func=AF.Exp, scale=-1.0), r=["zf"], w=["zf"])
                S.add("act", lambda e: e.activation(out=lp[:], in_=zf[:], func=AF.Ln, scale=1.0, bias=cc(C_ONE)), r=["zf", "cst"], w=["lp"])
                GN = max(1, 512 // H)
                for g0 in range(0, NT, GN):
                    g1 = min(NT, g0 + GN)
                    NH = (g1 - g0) * H
                    lp2 = lp[:, g0:g1, :].rearrange("p n h -> p (n h)")
                    pg, pb_, pd0, pd1 = p_acc[0], p_acc[1], p_acc[2], p_acc[3]
                    S.add("pe", lambda e: e.matmul(pg[:, 0:NH], lhsT=cst[:, C_L:C_L + 128], rhs=lp2, start=True, stop=True), r=["lp", "cst"], w=["pacc0"])
                    S.add("pe", lambda e: e.matmul(pb_[:, 0:NH], lhsT=cst[:, C_B:C_B + 128], rhs=lp2, start=True, stop=True), r=["lp", "cst"], w=["pacc1"])
                    S.add("pe", lambda e: e.matmul(pd0[:, 0:NH], lhsT=cst[:, C_E0:C_E0 + 128], rhs=lp2, start=True, stop=True), r=["lp", "cst"], w=["pacc2"])
                    S.add("pe", lambda e: e.matmul(pd1[:, 0:NH], lhsT=cst[:, C_E1:C_E1 + 128], rhs=lp2, start=True, stop=True), r=["lp", "cst"], w=["pacc3"])

                    def v3(p):
                        return p[:, 0:NH].rearrange("p (n h) -> p n h", h=H)
                    gs = slice(g0, g1)
                    S.add("dve", lambda e: e.tensor_tensor(out=t1[:, gs, :], in0=ifs[:, gs, 0:H], in1=v3(pg), op=ALU.add), r=["ifs", "pacc0"], w=["t1"])
                    S.add("act", lambda e: e.activation(out=ga[:, gs, :], in_=t1[:, gs, :], func=AF.Exp), r=["t1"], w=["ga"])
                    S.add("dve", lambda e: e.tensor_tensor(out=t2[:, gs, :], in0=t1[:, gs, :], in1=v3(pb_), op=ALU.subtract), r=["t1", "pacc1"], w=["t2"])
                    S.add("act", lambda e: e.activation(out=gwk[:, gs, :], in_=t2[:, gs, :], func=AF.Exp), r=["t2"], w=["gwk"])
                    S.add("act", lambda e: e.activation(out=ge[:, gs, :], in_=v3(pg), func=AF.Exp, scale=1.0, bias=cc(C_LSK)), r=["pacc0", "cst"], w=["ge"])
                    S.add("act", lambda e: e.activation(out=gd0[:, gs, :], in_=v3(pd0), func=AF.Exp, scale=-1.0), r=["pacc2"], w=["gd0"])
                    S.add("act", lambda e: e.activation(out=gd1[:, gs, :], in_=v3(pd1), func=AF.Exp, scale=-1.0), r=["pacc3"], w=["gd1"])
                S.add("dve", lambda e: e.tensor_scalar(out=gwk0[:], in0=gwk[:], scalar1=cc(C_HM0), scalar2=0.0, op0=ALU.mult, op1=ALU.add), r=["gwk", "cst"], w=["gwk0"])
                S.add("dve", lambda e: e.tensor_scalar(out=gwk1[:], in0=gwk[:], scalar1=cc(C_HM1), scalar2=0.0, op0=ALU.mult, op1=ALU.add), r=["gwk", "cst"], w=["gwk1"])
                lam = chs[:, cfg.p_lam:cfg.p_lam + NB]
                S.add("act", lambda e: e.activation(out=ccoef[:, 0, :], in_=lam, func=AF.Exp, scale=-1.0), r=["chs"], w=["ccoef"])
                S.add("act", lambda e: e.activation(out=ccoef[:, 0, :], in_=ccoef[:, 0, :], func=AF.Ln, scale=1.0, bias=cc(C_ONE)), r=["ccoef", "cst"], w=["ccoef"])
                S.add("dve", lambda e: e.tensor_scalar(out=ccoef[:, 1, :], in0=ccoef[:, 0, :], scalar1=-2.0 * LRU_C, scalar2=0.0, op0=ALU.mult, op1=ALU.add), r=["ccoef"], w=["ccoef"])
                S.add("dve", lambda e: e.tensor_scalar(out=ccoef[:, 0, :], in0=ccoef[:, 0, :], scalar1=-LRU_C, scalar2=0.0, op0=ALU.mult, op1=ALU.add), r=["ccoef"], w=["ccoef"])
                S.add("dve", lambda e: e.tensor_scalar(out=ccoef[:, 2, :], in0=chs[:, cfg.p_ba:cfg.p_ba + NB], scalar1=-1.0, scalar2=0.0, op0=ALU.mult, op1=ALU.add),
                      r=["chs"], w=["ccoef2"])
                S.add("dve", lambda e: e.tensor_scalar(out=ccoef[:, 3, :], in0=chs[:, cfg.p_bx:cfg.p_bx + NB], scalar1=-1.0, scalar2=0.0, op0=ALU.mult, op1=ALU.add),
                      r=["chs"], w=["ccoef2"])
                S.add("pool", lambda e: e.memset(Cf[:], 0.0), w=["Cf"] + [("Cf", h) for h in range(H)])
                S.add("pool", lambda e: e.memset(Cb[:], 0.0), w=[("Cb", h) for h in range(H)])
                S.add("pool", lambda e: e.memset(hcar[:], 0.0), w=["hcar"] + [("hcar", b) for b in range(NB)])
                for i in range(2):
                    S.add("pool", lambda e: e.memset(qz0[i][:], 0.0), w=["qz0_%d" % i])
                    S.add("pool", lambda e: e.memset(qz1[i][:], 0.0), w=["qz1_%d" % i])
                for i in range(2):
                    S.add("pool", lambda e: e.memset(va[i][:, :, 256:257], 1.0), w=["va%d" % i])

                cnt = {"kw": 0, "sc": 0, "acc": 0, "dc": 0}

                def mlstm_tile(n):
                    st, ti = n // TPS, n % TPS
                    qb, kb = qs[st % 2], ks[st % 2]
                    kq, kk = "qs%d" % (st % 2), "ks%d" % (st % 2)
                    if ti == 0:
                        S.dma("sp", kb[:], qk_fm[NQK:2 * NQK, st * SW:(st + 1) * SW].rearrange("(h p) t -> p h t", p=128), r=["fm_out"], w=[kk])
                        S.dma("sp", qb[:], qk_fm[0:NQK, st * SW:(st + 1) * SW].rearrange("(h p) t -> p h t", p=128), r=["fm_out"], w=[kq])
                    vb, kv = va[n % 2], "va%d" % (n % 2)
                    S.dma("act", vb[:, :, 0:256], v_tok[n * 128:(n + 1) * 128, :].rearrange("p (h v) -> p h v", v=256), r=["tm_out"], w=[kv])
                    tc0, tc1 = ti * 128, (ti + 1) * 128
                    sob, szb_ = sos[n % 2], szs[n % 2]
                    kso, ksz = "sos%d" % (n % 2), "szs%d" % (n % 2)
                    S.dma("act", sob[:], so_tok[n * 128:(n + 1) * 128, :], r=["tm_out"], w=[kso])
                    S.dma("act", szb_[:], sza_tok[n * 128:(n + 1) * 128, :], r=["tm_out"], w=[ksz])
                    z0, z1 = qz0[n % 2], qz1[n % 2]
                    kz0, kz1 = "qz0_%d" % (n % 2), "qz1_%d" % (n % 2)
                    S.add("pool", lambda e: e.tensor_copy(out=z0[:, :, 0:64], in_=qb[:, :, tc0:tc0 + 64]), r=[kq], w=[kz0])
                    S.add("pool", lambda e: e.tensor_copy(out=z1[:, :, 64:128], in_=qb[:, :, tc0 + 64:tc0 + 128]), r=[kq], w=[kz1])
                    hl, khl = hall[0], "hall0"
                    for h0 in range(0, H, 2):
                        grp = list(range(h0, min(H, h0 + 2)))
                        info = {}
                        for h in grp:
                            ki = cnt["kw"] % 4
                            cnt["kw"] += 1
                            k0t, k1t = kw0[ki], kw1[ki]
                            kk0, kk1 = "kw0_%d" % ki, "kw1_%d" % ki
                            pk = ("pkt", ki)
                            S.add("pe", lambda e: e.transpose(p_kt[:, ki, :], kb[:, h, tc0:tc1], idb[:]), r=[kk, "idb"], w=[pk])
                            S.add("act", lambda e: e.activation(out=k0t[:], in_=p_kt[:, ki, :], func=AF.Copy, scale=gwk0[:, n, h:h + 1]),
                                  r=[pk, "gwk0"], w=[kk0])
                            S.add("act", lambda e: e.activation(out=k1t[:], in_=p_kt[:, ki, :], func=AF.Copy, scale=gwk1[:, n, h:h + 1]),
                                  r=[pk, "gwk1"], w=[kk1])
                            si = cnt["sc"] % 4
                            cnt["sc"] += 1
                            sct, ksc, pss_k = scT[si], "scT%d" % si, ("pss", si)
                            S.add("pe", lambda e: e.matmul(p_ss[:, si, :], lhsT=kb[:, h, tc0:tc1], rhs=qb[:, h, tc0:tc1], start=True, stop=True),
                                  r=[kk, kq], w=[pss_k])
                            S.add("dve", lambda e: e.scalar_tensor_tensor(out=sct[:], in0=p_ss[:, si, :], scalar=ga[:, n, h:h + 1],
                                                                          in1=mLb[:], op0=ALU.mult, op1=ALU.mult),
                                  r=[pss_k, "ga", "mLb"], w=[ksc])
                            ai = cnt["acc"] % 4
                            cnt["acc"] += 1
                            pa, kpa = p_acc[ai], "pacc%d" % ai
                            S.add("pe", lambda e: e.matmul(pa[:, 0:257], lhsT=sct[:], rhs=vb[:, h, :], start=True, stop=False),
                                  r=[ksc, kv], w=[kpa])
                            S.add("pe", lambda e: e.matmul(pa[:, 0:257], lhsT=z0[:, h, :], rhs=Cb[:, h, :], start=False, stop=False),
                                  r=[kz0, ("Cb", h)], w=[kpa])
                            info[h] = (k0t, k1t, kk0, kk1, pa, kpa)
                            di = cnt["dc"] % 2
                            cnt["dc"] += 1
                            pd, kpd = p_dc[di], "pdc%d" % di
                            S.add("pe", lambda e: e.matmul(pd[:, 0:257], lhsT=k0t[:], rhs=vb[:, h, :], start=True, stop=True),
                                  r=[kk0, kv], w=[kpd])
                            S.add("dve", lambda e: e.scalar_tensor_tensor(out=Cf[:, h, :], in0=Cf[:, h, :], scalar=gd0[:, n, h:h + 1],
                                                                          in1=pd[:, 0:257], op0=ALU.mult, op1=ALU.add),
                                  r=[("Cf", h), "gd0", kpd], w=[("Cf", h)])
                            S.add("act", lambda e: e.copy(out=Cb[:, h, :], in_=Cf[:, h, :]), r=[("Cf", h)], w=[("Cb", h)])
                        for h in grp:
                            k0t, k1t, kk0, kk1, pa, kpa = info[h]
                            S.add("pe", lambda e: e.matmul(pa[:, 0:257], lhsT=z1[:, h, :], rhs=Cb[:, h, :], start=False, stop=True),
                                  r=[kz1, ("Cb", h)], w=[kpa])
                            di = cnt["dc"] % 2
                            cnt["dc"] += 1
                            pd, kpd = p_dc[di], "pdc%d" % di
                            S.add("pe", lambda e: e.matmul(pd[:, 0:257], lhsT=k1t[:], rhs=vb[:, h, :], start=True, stop=True),
                                  r=[kk1, kv], w=[kpd])
                            S.add("dve", lambda e: e.scalar_tensor_tensor(out=Cf[:, h, :], in0=Cf[:, h, :], scalar=gd1[:, n, h:h + 1],
                                                                          in1=pd[:, 0:257], op0=ALU.mult, op1=ALU.add),
                                  r=[("Cf", h), "gd1", kpd], w=[("Cf", h)])
                            S.add("act", lambda e: e.copy(out=Cb[:, h, :], in_=Cf[:, h, :]), r=[("Cf", h)], w=[("Cb", h)])
                            dcol = (h % 8) * 2
                            S.add("act", lambda e: e.activation(out=dsm[:, dcol:dcol + 1], in_=pa[:, 256:257], func=AF.Abs),
                                  r=[kpa], w=[("dsm", dcol)])
                            S.add("dve", lambda e: e.tensor_tensor(out=dsm[:, dcol:dcol + 1], in0=dsm[:, dcol:dcol + 1], in1=ge[:, n, h:h + 1], op=ALU.max),
                                  r=[("dsm", dcol), "ge"], w=[("dsm", dcol)])
                            S.add("dve", lambda e: e.reciprocal(out=dsm[:, dcol + 1:dcol + 2], in_=dsm[:, dcol:dcol + 1]),
                                  r=[("dsm", dcol)], w=[("dsm", dcol + 1)])
                            S.add("act", lambda e: e.activation(out=hl[:, h, :], in_=pa[:, 0:256], func=AF.Copy, scale=dsm[:, dcol + 1:dcol + 2]),
                                  r=[kpa, ("dsm", dcol + 1)], w=[(khl, h)])
                    hk = [(khl, h) for h in range(H)]
                    hl2 = hl[:].rearrange("p h v -> p (h v)")
                    S.add("dve", lambda e: e.tensor_tensor(out=hl2, in0=hl2, in1=sob[:], op=ALU.mult), r=hk + [kso], w=[khl])
                    for h in range(H):
                        S.add("act", lambda e: e.activation(out=sqb[:, h, :], in_=hl[:, h, :], func=AF.Identity, accum_out=st8[:, 0, h:h + 1]),
                              r=[khl], w=["sqb", ("st8", 0)])
                        S.add("act", lambda e: e.activation(out=sqb[:, h, :], in_=hl[:, h, :], func=AF.Square, accum_out=st8[:, 1, h:h + 1]),
                              r=[khl], w=["sqb", ("st8", 1)])
                    S.add("dve", lambda e: e.tensor_scalar(out=st8[:, 2, :], in0=st8[:, 0, :], scalar1=1.0 / 256, scalar2=0.0, op0=ALU.mult, op1=ALU.add),
                          r=[("st8", 0)], w=[("st8", 2)])
                    S.add("dve", lambda e: e.tensor_tensor(out=st8[:, 3, :], in0=st8[:, 2, :], in1=st8[:, 2, :], op=ALU.mult), r=[("st8", 2)], w=[("st8", 3)])
                    S.add("dve", lambda e: e.scalar_tensor_tensor(out=st8[:, 4, :], in0=st8[:, 1, :], scalar=1.0 / 256, in1=st8[:, 3, :], op0=ALU.mult, op1=ALU.subtract),
                          r=[("st8", 1), ("st8", 3)], w=[("st8", 4)])
                    S.add("act", lambda e: e.activation(out=st8[:, 5, :], in_=st8[:, 4, :], func=AF.Ln, scale=1.0, bias=cc(C_EPS)), r=[("st8", 4), "cst"], w=[("st8", 5)])
                    S.add("act", lambda e: e.activation(out=st8[:, 6, :], in_=st8[:, 5, :], func=AF.Exp, scale=-0.5), r=[("st8", 5)], w=[("st8", 6)])
                    S.add("dve", lambda e: e.scalar_tensor_tensor(out=st8[:, 7, :], in0=st8[:, 2, :], scalar=-1.0, in1=st8[:, 6, :], op0=ALU.mult, op1=ALU.mult),
                          r=[("st8", 2), ("st8", 6)], w=[("st8", 7)])
                    for h in range(H):
                        S.add("act", lambda e: e.activation(out=hl[:, h, :], in_=hl[:, h, :], func=AF.Identity, scale=st8[:, 6, h:h + 1], bias=st8[:, 7, h:h + 1]),
                              r=[khl, ("st8", 6), ("st8", 7)], w=[khl])
                    S.add("dve", lambda e: e.tensor_tensor(out=hl2, in0=hl2, in1=hgrep[:], op=ALU.mult), r=[khl, "hgrep"], w=[khl])
                    S.add("dve", lambda e: e.tensor_tensor(out=ybf[:], in0=hl2, in1=szb_[:], op=ALU.mult), r=[khl, ksz], w=["ybf"])
                    yt, kyt = yts[n % 2], "yts%d" % (n % 2)
                    for g in range(KC // 4):
                        for i in range(4):
                            kc = g * 4 + i
                            S.add("pe", lambda e: e.transpose(p_kt[:, i, :], ybf[:, kc * 128:(kc + 1) * 128], idb[:]),
                                  r=["ybf", "idb"], w=[("pkt", i)])
                        S.add("act", lambda e: e.copy(out=yt[:, g * 4:(g + 1) * 4, :], in_=p_kt[:]),
                              r=[("pkt", i) for i in range(4)], w=[kyt])
                    for k0 in range(0, KC, 4):
                        S.dma("sp", y_fm[k0 * 128:(k0 + 4) * 128, n * 128:(n + 1) * 128].rearrange("(kc p) t -> p kc t", p=128), yt[:, k0:k0 + 4, :], r=[kyt], w=["y_fm"])

                lc = [0]

                def lru_step(h, st):
                    bi = lc[0] % 2
                    lc[0] += 1
                    xcb, kxc = xcs[bi], "xcs%d" % bi
                    zbb, kzb = zbs[bi], "zbs%d" % bi
                    S.dma("sp", xcb[:], xc_fm[h * 256:(h + 1) * 256, st * SW:(st + 1) * SW].rearrange("(b p) t -> p b t", p=128), r=["fm_out"], w=[kxc])
                    S.dma("sp", zbb[:], szb_fm[h * 256:(h + 1) * 256, st * SW:(st + 1) * SW].rearrange("(b p) t -> p b t", p=128), r=["fm_out"], w=[kzb])
                    for jb in range(2):
                        blk = h * 2 + jb
                        pr_, kpr_ = p_acc[(2 * jb) % 4], "pacc%d" % ((2 * jb) % 4)
                        pi_, kpi_ = p_acc[(2 * jb + 1) % 4], "pacc%d" % ((2 * jb + 1) % 4)
                        for ib in range(2):
                            S.add("pe", lambda e: e.matmul(pr_[:, 0:SW], lhsT=wab[:, h, ib, jb * 128:(jb + 1) * 128], rhs=xcb[:, ib, :], start=(ib == 0), stop=(ib == 1)),
                                  r=["wab", kxc], w=[kpr_])
                        for ib in range(2):
                            S.add("pe", lambda e: e.matmul(pi_[:, 0:SW], lhsT=wxb[:, h, ib, jb * 128:(jb + 1) * 128], rhs=xcb[:, ib, :], start=(ib == 0), stop=(ib == 1)),
                                  r=["wxb", kxc], w=[kpi_])
                        (er, ker), (ei, kei), (la, kla), (a2, ka2), (hh, khh) = [(lt[nm][jb], "%s%d" % (nm, jb)) for nm in ("er", "ei", "la", "a2", "hh")]
                        S.add("act", lambda e: e.activation(out=er[:], in_=pr_[:, 0:SW], func=AF.Exp, scale=-1.0, bias=ccoef[:, 2, blk:blk + 1]),
                              r=[kpr_, "ccoef2"], w=[ker])
                        S.add("act", lambda e: e.activation(out=ei[:], in_=pi_[:, 0:SW], func=AF.Exp, scale=-1.0, bias=ccoef[:, 3, blk:blk + 1]),
                              r=[kpi_, "ccoef2"], w=[kei])
                        S.add("act", lambda e: e.activation(out=er[:], in_=er[:], func=AF.Ln, scale=1.0, bias=cc(C_ONE)), r=[ker, "cst"], w=[ker])
                        S.add("act", lambda e: e.activation(out=er[:], in_=er[:], func=AF.Exp, scale=-1.0), r=[ker], w=[ker])
                        S.add("act", lambda e: e.activation(out=ei[:], in_=ei[:], func=AF.Ln, scale=1.0, bias=cc(C_ONE)), r=[kei, "cst"], w=[kei])
                        S.add("act", lambda e: e.activation(out=ei[:], in_=ei[:], func=AF.Exp, scale=-1.0), r=[kei], w=[kei])
                        S.add("act", lambda e: e.activation(out=la[:], in_=er[:], func=AF.Exp, scale=ccoef[:, 0, blk:blk + 1]), r=[ker, "ccoef"], w=[kla])
                        S.add("act", lambda e: e.activation(out=a2[:], in_=er[:], func=AF.Exp, scale=ccoef[:, 1, blk:blk + 1]), r=[ker, "ccoef"], w=[ka2])
                        S.add("act", lambda e: e.activation(out=a2[:], in_=a2[:], func=AF.Ln, scale=-1.0, bias=cc(C_ONE)), r=[ka2, "cst"], w=[ka2])
                        S.add("act", lambda e: e.activation(out=a2[:], in_=a2[:], func=AF.Exp, scale=0.5), r=[ka2], w=[ka2])
                        S.add("dve", lambda e: e.tensor_tensor(out=ei[:], in0=ei[:], in1=xcb[:, jb, :], op=ALU.mult), r=[kei, kxc], w=[kei])
                        S.add("dve", lambda e: e.tensor_tensor(out=ei[:], in0=ei[:], in1=a2[:], op=ALU.mult), r=[kei, ka2], w=[kei])
                        S.add("dve", lambda e: e.tensor_tensor_scan(out=hh[:], data0=la[:], data1=ei[:], initial=hcar[:, blk:blk + 1],
                                                                    op0=ALU.mult, op1=ALU.add),
                              r=[kla, kei, ("hcar", blk)], w=[khh])
                        S.add("act", lambda e: e.copy(out=hcar[:, blk:blk + 1], in_=hh[:, SW - 1:SW]), r=[khh], w=[("hcar", blk)])
                        ybt, kyb = yb[jb], "yb%d" % jb
                        S.add("dve", lambda e: e.tensor_tensor(out=ybt[:], in0=hh[:], in1=zbb[:, jb, :], op=ALU.mult), r=[khh, kzb], w=[kyb])
                        S.dma("act", y_fm[D + blk * 128:D + (blk + 1) * 128, st * SW:(st + 1) * SW], ybt[:], r=[kyb], w=["y_fm"])

                for n in range(NT):
                    mlstm_tile(n)
                for h in range(H):
                    for st in range(NST):
                        lru_step(h, st)
                S.flush()
            if stop == "MR":
                return nc

            with contextlib.ExitStack() as es:
                wob = [sb(es, "wob%d" % i, [128, KO, OCB], BF16) for i in range(2)]
                yT = [sb(es, "yT%d" % i, [128, KO, SW], BF16) for i in range(2)]
                xr = [sb(es, "xr%d" % i, [128, OCB]) for i in range(3)]
                pso = [ps(es, "pso%d" % i, [128, 512]) for i in range(4)]
                NCB = D // OCB
                ostep = max(1, KO // 8)

                def load_wo(cb):
                    wb = wob[cb % 2]
                    for k0 in range(0, KO, ostep):
                        S.dma("pool", wb[:, k0:k0 + ostep, :].rearrange("p k c -> p (k c)"),
                              wod[cb][:, k0 * OCB:(k0 + ostep) * OCB], r=["wbd"], w=[("wob", cb % 2, k0)])
                load_wo(0)
                yi = 0
                xi = 0
                for cb in range(NCB):
                    if cb + 1 < NCB:
                        load_wo(cb + 1)
                    wb = wob[cb % 2]
                    wkeys = [("wob", cb % 2, k0) for k0 in range(0, KO, ostep)]
                    for st in range(NST):
                        yb_, kyb_ = yT[yi % 2], "yT%d" % (yi % 2)
                        yi += 1
                        hk = max(1, KO // 4)
                        for k0 in range(0, KO, hk):
                            S.dma("sp", yb_[:, k0:k0 + hk, :], y_fm[k0 * 128:(k0 + hk) * 128, st * SW:(st + 1) * SW].rearrange("(kc p) t -> p kc t", p=128),
                                  r=["y_fm"], w=[(kyb_, k0)])
                        for ti in range(TPS):
                            n = st * TPS + ti
                            xb_, kxb_ = xr[xi % 3], "xr%d" % (xi % 3)
                            po, kpo = pso[xi % 4], "pso%d" % (xi % 4)
                            xi += 1
                            S.dma("act", xb_[:], xsrc[n * 128:(n + 1) * 128, cb * OCB:(cb + 1) * OCB], r=[("xres", n, cb)], w=[kxb_])
                            for kc in range(KO):
                                S.add("pe", lambda e: e.matmul(po[:, 0:OCB], lhsT=yb_[:, kc, ti * 128:(ti + 1) * 128], rhs=wb[:, kc, :],
                                                               start=(kc == 0), stop=(kc == KO - 1)),
                                      r=wkeys + [(kyb_, k0) for k0 in range(0, KO, hk)], w=[kpo])
                            S.add("dve", lambda e: e.tensor_tensor(out=xb_[:], in0=po[:, 0:OCB], in1=xb_[:], op=ALU.add), r=[kpo, kxb_], w=[kxb_])
                            S.dma("act", xres[n * 128:(n + 1) * 128, cb * OCB:(cb + 1) * OCB], xb_[:], r=[kxb_], w=[("xres", n, cb)])
                S.flush()

        if stop == "P4":
            return nc
        with contextlib.ExitStack() as es:
            grep = sb(es, "fgrep", [128, D])
            xt = [sb(es, "fxt%d" % i, [128, D]) for i in range(2)]
            ot = [sb(es, "fot%d" % i, [128, D]) for i in range(2)]
            junk = sb(es, "fjunk", [128, D], BF16)
            sst = sb(es, "fsst", [128, 8])
            S.dma("act", grep[:], bass.AP(fin_g.tensor, 0, [[0, 128], [1, D]]), w=["grep"])
            for n in range(NT):
                bi = n % 2
                xb_, ob_ = xt[bi], ot[bi]
                kx, ko = "xt%d" % bi, "ot%d" % bi
                c0 = bi * 4
                S.dma("sp", xb_[:], xres[n * 128:(n + 1) * 128, :], w=[kx])
                S.add("act", lambda e: e.activation(out=junk[:], in_=xb_[:], func=AF.Square, accum_out=sst[:, c0:c0 + 1]), r=[kx], w=["junk", ("sst", c0)])
                S.add("act", lambda e: e.activation(out=sst[:, c0 + 1:c0 + 2], in_=sst[:, c0:c0 + 1], func=AF.Ln, scale=1.0 / D, bias=cc(C_EPS)),
                      r=[("sst", c0), "cst"], w=[("sst", c0 + 1)])
                S.add("act", lambda e: e.activation(out=sst[:, c0 + 2:c0 + 3], in_=sst[:, c0 + 1:c0 + 2], func=AF.Exp, scale=-0.5), r=[("sst", c0 + 1)], w=[("sst", c0 + 2)])
                S.add("dve", lambda e: e.scalar_tensor_tensor(out=ob_[:], in0=xb_[:], scalar=sst[:, c0 + 2:c0 + 3], in1=grep[:], op0=ALU.mult, op1=ALU.mult),
                      r=[kx, ("sst", c0 + 2), "grep"], w=[ko])
                hw_ = min(D, 1024)
                for c1 in range(0, D, hw_):
                    S.dma("sp", out_d[n * 128:(n + 1) * 128, c1:c1 + hw_], ob_[:, c1:c1 + hw_], r=[ko], w=["out"])
            S.flush()
    return nc


def host_inputs(cfg, x, norm_g, w_in, i_bias, f_bias, qk_conv, head_norm_g, lru_conv_w, lru_conv_b,
                w_a, b_a, w_x, b_x, lam, w_out, final_g):
    f = lambda a: np.ascontiguousarray(np.asarray(a, dtype=np.float32))
    L, D, T, NB = cfg.L, cfg.D, cfg.T, cfg.NB
    x = f(x)

    def chanpack(v):
        v = f(v)
        return v.reshape(L, -1, 128).transpose(0, 2, 1)

    def convpack(wc):
        wc = f(wc)
        C = wc.shape[2]
        return wc.reshape(L, 4, C // 128, 128).transpose(0, 3, 2, 1).reshape(L, 128, (C // 128) * 4)

    chan = np.concatenate([convpack(qk_conv), convpack(lru_conv_w), chanpack(lru_conv_b), chanpack(b_a),
                           chanpack(b_x), chanpack(lam)], axis=2)
    assert chan.shape == (L, 128, cfg.CW), chan.shape
    rowv = np.concatenate([f(norm_g), f(head_norm_g), f(i_bias), f(f_bias)], axis=1)
    assert rowv.shape == (L, cfg.RW)
    cstv = make_consts()
    shared = {
        "w_in": f(w_in)[:L], "w_out": f(w_out)[:L], "w_a": f(w_a)[:L], "w_x": f(w_x)[:L],
        "chan": np.ascontiguousarray(chan), "rowv": np.ascontiguousarray(rowv),
        "final_g": f(final_g).reshape(1, D), "cst": cstv,
    }
    maps = []
    zx = np.zeros((T, D), np.float32)
    for c in range(cfg.NCORES):
        m = dict(shared)
        m["x"] = np.ascontiguousarray(x[c]) if c < cfg.BATCH else zx
        maps.append(m)
    return maps


_NC_CACHE = {}


def run(cfg, inputs, dbg=()):
    key = (cfg.D, cfg.T, cfg.L, tuple(dbg))
    if key not in _NC_CACHE:
        _NC_CACHE[key] = build(cfg, dbg)
    nc = _NC_CACHE[key]
    maps = host_inputs(cfg, **inputs)
    res = run_bass_kernel_spmd(nc, maps, core_ids=list(range(cfg.NCORES)))
    return res


def kernel(**inputs):
    cfg = Cfg()
    res = run(cfg, inputs)
    return np.stack([np.asarray(res.results[b]["out"], dtype=np.float32) for b in range(cfg.BATCH)], axis=0)
```

```python
import contextlib
import numpy as np
import concourse.bass as bass
import concourse.mybir as mybir
from concourse.bass_utils import run_bass_kernel_spmd

F32, BF16 = mybir.dt.float32, mybir.dt.bfloat16
ALU = mybir.AluOpType
AF = mybir.ActivationFunctionType
AX = mybir.AxisListType
EPS = 1e-6
LRU_C = 8.0


class Cfg:
    def __init__(self, D=2048, T=8192, L=4, NCORES=8, BATCH=2, TM=2048):
        self.D, self.T, self.L, self.NCORES, self.BATCH = D, T, L, NCORES, BATCH
        self.NSEG = 1
        self.TM = min(TM, T)
        self.NMT = T // self.TM
        self.NTM = self.TM // 128
        H = self.H = D // 256
        self.NQK = H * 128
        self.NIN = 2 * self.NQK + 3 * D + 2 * H + 2 * D
        self.KC = D // 128
        self.NT = T // 128
        self.SW = min(512, self.TM)
        self.NST = T // self.SW
        self.NSTM = self.TM // self.SW
        self.TPS = self.SW // 128
        self.NB = D // 128
        self.c_q = 0
        self.c_k = self.NQK
        self.c_v = 2 * self.NQK
        self.c_o = self.c_v + D
        self.c_za = self.c_o + D
        self.c_i = self.c_za + D
        self.c_xb = self.c_i + 2 * H
        self.c_zb = self.c_xb + D
        self.HW = 3 * D // 128
        self.oC = 0
        self.oD = H * 257
        self.oHE = self.oD + H
        self.oA = self.oHE + self.NB
        self.SUMF = self.oA + self.NB
        self.nqkb = 2 * self.NQK // 128
        self.p_qkc = 0
        self.p_lcw = self.p_qkc + self.nqkb * 4
        self.p_lcb = self.p_lcw + self.NB * 4
        self.p_ba = self.p_lcb + self.NB
        self.p_bx = self.p_ba + self.NB
        self.p_lam = self.p_bx + self.NB
        self.CW = self.p_lam + self.NB
        self.r_ng = 0
        self.r_hg = D
        self.r_ib = 2 * D
        self.RW = 2 * D + 2 * H
        self.KO = 2 * D // 128
        self.OCB = min(512, D)


C_ID, C_L, C_B, C_E0, C_E1 = 0, 128, 256, 384, 512
C_HM0, C_HM1, C_EPS, C_ONE, C_LSK, C_EPS4 = 640, 641, 642, 643, 644, 645
CCW = 648


def make_consts():
    c = np.zeros((128, CCW), np.float32)
    s = np.arange(128)[:, None]
    t = np.arange(128)[None, :]
    c[:, C_ID:C_ID + 128] = (s == t)
    c[:, C_L:C_L + 128] = (s <= t) & (s // 64 == t // 64)
    c[:, C_B:C_B + 128] = (s // 64 == t // 64)
    c[:, C_E0:C_E0 + 128] = (s < 64)
    c[:, C_E1:C_E1 + 128] = (s >= 64)
    c[:, C_HM0] = (np.arange(128) < 64)
    c[:, C_HM1] = (np.arange(128) >= 64)
    c[:, C_EPS] = EPS
    c[:, C_ONE] = 1.0
    c[:, C_LSK] = 0.5 * np.log(128.0)
    c[:, C_EPS4] = EPS
    return c


class Op:
    __slots__ = ("eng", "fn", "deps", "kind", "sig", "slot", "cnt")

    def __init__(self, eng, fn, deps, kind):
        self.eng, self.fn, self.deps, self.kind = eng, fn, deps, kind
        self.sig = False
        self.slot = None
        self.cnt = 0


class _Rec:
    def __init__(self):
        self.call = None

    def __getattr__(self, name):
        def f(*a, **k):
            self.call = (name, a, k)
            return self
        return f

    def play(self, eng):
        name, a, k = self.call
        return getattr(eng, name)(*a, **k)


class Sched:
    NDS = 6
    RESET = False

    def __init__(self, nc, es):
        self.nc = nc
        self.eng = {"sp": nc.sync, "act": nc.scalar, "pool": nc.gpsimd, "dve": nc.vector, "pe": nc.tensor}
        self.esem = {e: es.enter_context(nc.semaphore("e_" + e)) for e in self.eng}
        self.ecnt = {e: 0 for e in self.eng}
        self.dsem = {q: [es.enter_context(nc.semaphore("d_%s%d" % (q, i))) for i in range(self.NDS)]
                     for q in ("sp", "act", "pool")}
        self.dcnt = {q: [0] * self.NDS for q in self.dsem}
        self.dnext = {q: 0 for q in self.dsem}
        self.csem = es.enter_context(nc.semaphore("cc"))
        self.ccnt = 0
        self.seen = {e: {} for e in self.eng}
        self.reset()

    def reset(self):
        self.ops = []
        self.last_w = {}
        self.readers = {}

    defer = None

    def add(self, eng, fn, r=(), w=(), kind="eng"):
        rec = _Rec()
        fn(rec)
        if self.defer is not None:
            self.defer.append((eng, rec, tuple(r), tuple(w), kind))
            return None
        return self.register(eng, rec, r, w, kind)

    def register(self, eng, rec, r=(), w=(), kind="eng"):
        idx = len(self.ops)
        deps = set()
        for k in r:
            if k in self.last_w:
                deps.add(self.last_w[k])
        for k in w:
            if k in self.last_w:
                deps.add(self.last_w[k])
            deps.update(self.readers.get(k, ()))
        for k in r:
            self.readers.setdefault(k, []).append(idx)
        for k in w:
            self.last_w[k] = idx
            self.readers[k] = []
        deps.discard(idx)
        latest = {}
        pruned = set()
        for d in deps:
            o = self.ops[d]
            if o.kind == "eng":
                if o.eng not in latest or latest[o.eng] < d:
                    latest[o.eng] = d
            else:
                pruned.add(d)
        pruned.update(latest.values())
        deps = pruned
        self.ops.append(Op(eng, rec, deps, kind))
        return idx

    def dma(self, q, out, in_, r=(), w=()):
        return self.add(q, lambda e: e.dma_start(out=out, in_=in_), r, w, kind="dma")

    def _wait(self, eng, sem, val):
        key = id(sem)
        if self.seen[eng].get(key, 0) >= val:
            return
        self.seen[eng][key] = val
        self.eng[eng].wait_ge(sem, val)

    def flush(self, barrier=True):
        ops = self.ops
        for op in ops:
            for d in op.deps:
                dep = ops[d]
                if dep.kind == "eng" and dep.eng == "pe" and op.eng == "pe" and op.kind == "eng":
                    continue
                dep.sig = True
        for op in ops:
            eng = op.eng
            for d in sorted(op.deps):
                dep = ops[d]
                if not dep.sig:
                    continue
                if dep.kind == "eng":
                    self._wait(eng, self.esem[dep.eng], dep.cnt)
                elif dep.kind == "dma":
                    self._wait(eng, self.dsem[dep.eng][dep.slot], dep.cnt)
                else:
                    self._wait(eng, self.csem, dep.cnt)
            if op.kind == "dma":
                s = self.dnext[eng]
                self.dnext[eng] = (s + 1) % self.NDS
                self._wait(eng, self.dsem[eng][s], self.dcnt[eng][s])
                ins = op.fn.play(self.eng[eng])
                self.dcnt[eng][s] += 16
                op.slot, op.cnt = s, self.dcnt[eng][s]
                ins.then_inc(self.dsem[eng][s], 16)
            elif op.kind == "cc":
                self._wait(eng, self.csem, self.ccnt)
                ins = op.fn.play(self.eng[eng])
                self.ccnt += 1
                op.cnt = self.ccnt
                ins.then_inc(self.csem, 1)
            else:
                ins = op.fn.play(self.eng[eng])
                if op.sig:
                    self.ecnt[eng] += 1
                    op.cnt = self.ecnt[eng]
                    ins.then_inc(self.esem[eng], 1)
        self.reset()
        if barrier:
            self.drain()

    def drain(self):
        for q in self.dsem:
            for s in range(self.NDS):
                self._wait(q, self.dsem[q][s], self.dcnt[q][s])
        self.nc.all_engine_barrier()
        if self.RESET:
            for e in self.esem:
                self.nc.gpsimd.sem_clear(self.esem[e])
                self.ecnt[e] = 0
            for q in self.dsem:
                for i in range(self.NDS):
                    self.nc.gpsimd.sem_clear(self.dsem[q][i])
                    self.dcnt[q][i] = 0
            self.seen = {e: {} for e in self.eng}
            self.nc.all_engine_barrier()


def apx(ap, extra):
    return bass.AP(ap.tensor, ap.offset, [list(a) for a in ap.ap] + [list(e) for e in extra])


def bc_mid(ap2, n):
    a = [list(x) for x in ap2.ap]
    return bass.AP(ap2.tensor, ap2.offset, [a[0], [0, n]] + a[1:])


def bc_last(ap, n):
    return apx(ap, [[0, n]])


def build(cfg, dbg=(), stop=None):
    D, T, L, H, KC, NT, SW, NST, TPS, NB = cfg.D, cfg.T, cfg.L, cfg.H, cfg.KC, cfg.NT, cfg.SW, cfg.NST, cfg.TPS, cfg.NB
    TM, NMT, NTM, NSTM = cfg.TM, cfg.NMT, cfg.NTM, cfg.NSTM
    NQK, NIN, KO, OCB = cfg.NQK, cfg.NIN, cfg.KO, cfg.OCB
    WB = min(512, D)
    NSB = WB // 128
    nc = bass.Bass("TRN2", target_bir_lowering=False)

    def din(name, shape, dt=F32):
        return nc.dram_tensor(name, list(shape), dt, kind="ExternalInput").ap()

    def dscr(name, shape, dt=F32):
        kind = "ExternalOutput" if name in dbg else "Internal"
        return nc.dram_tensor(name, list(shape), dt, kind=kind).ap()

    x_in = din("x", [T, D])
    w_in = din("w_in", [L, D, NIN])
    w_out = din("w_out", [L, 2 * D, D])
    w_a = din("w_a", [L, H, 256, 256])
    w_x = din("w_x", [L, H, 256, 256])
    chan = din("chan", [L, 128, cfg.CW])
    rowv = din("rowv", [L, cfg.RW])
    fin_g = din("final_g", [1, D])
    cst_d = din("cst", [128, CCW])
    out_d = nc.dram_tensor("out", [T, D], F32, kind="ExternalOutput").ap()

    xres = dscr("xres", [T, D])
    qk_fm = dscr("qk_fm", [2 * NQK, T], BF16)
    v_tok = dscr("v_tok", [T, D], BF16)
    so_tok = dscr("so_tok", [T, D], BF16)
    sza_tok = dscr("sza_tok", [T, D], BF16)
    if_d = dscr("if_d", [128, NT * 2 * H])
    xc_fm = dscr("xc_fm", [D, T], BF16)
    szb_fm = dscr("szb_fm", [D, T], BF16)
    y_fm = dscr("y_fm", [2 * D, T], BF16)
    NWB = (NIN - 2 * H) // WB
    wbd = dscr("wbd", [NWB, 128, KC * WB], BF16)
    wod = dscr("wod", [D // OCB, 128, KO * OCB], BF16)

    with contextlib.ExitStack() as es0:
        S = Sched(nc, es0)
        uid = [0]

        def sb(es, name, shape, dt=F32):
            uid[0] += 1
            return es.enter_context(nc.sbuf_tensor("s%d_%s" % (uid[0], name), list(shape), dt))

        def ps(es, name, shape, dt=F32):
            uid[0] += 1
            return es.enter_context(nc.psum_tensor("p%d_%s" % (uid[0], name), list(shape), dt))

        cst = sb(es0, "cst", [128, CCW])
        idb = sb(es0, "idb", [128, 128], BF16)
        mLb = sb(es0, "mLb", [128, 128], F32)
        chs = sb(es0, "chs", [128, cfg.CW])
        S.dma("sp", cst[:], cst_d, w=["cst"])
        S.add("dve", lambda e: e.tensor_copy(out=idb[:], in_=cst[:, C_ID:C_ID + 128]), r=["cst"], w=["idb"])
        S.add("dve", lambda e: e.tensor_copy(out=mLb[:], in_=cst[:, C_L:C_L + 128]), r=["cst"], w=["mLb"])
        S.flush()

        def cc(col):
            return cst[:, col:col + 1]

        for l in range(L):
            xsrc = x_in if l == 0 else xres
            S.dma("sp", chs[:], chan[l], w=["chs"])
            S.flush()

            blocks = []
            for c0 in range(cfg.c_q, cfg.c_v, WB):
                blocks.append(("qk", c0))
            for c0 in range(cfg.c_za, cfg.c_i, WB):
                blocks.append(("za", c0))
            for c0 in range(cfg.c_zb, NIN, WB):
                blocks.append(("zb", c0))
            for c0 in range(cfg.c_o, cfg.c_za, WB):
                blocks.append(("o", c0))
            for c0 in range(cfg.c_v, cfg.c_o, WB):
                blocks.append(("v", c0))
            for c0 in range(cfg.c_xb, cfg.c_zb, WB):
                blocks.append(("xb", c0))
            blocks.append(("if", cfg.c_i))
            NBLK = len(blocks)
            assert NBLK - 1 == NWB

            with contextlib.ExitStack() as es:
                KW = max(KC, KO)
                fst = [sb(es, "fst%d" % i, [128, KW // 2, WB]) for i in range(2)]
                bst = [sb(es, "bst%d" % i, [128, KW // 2, WB], BF16) for i in range(2)]
                jobs = []
                for bi in range(NWB):
                    c0 = blocks[bi][1]
                    for hf in range(2):
                        k0 = hf * (KC // 2)
                        jobs.append((w_in[l][k0 * 128:(k0 + KC // 2) * 128, c0:c0 + WB], KC // 2,
                                     wbd[bi][:, k0 * WB:(k0 + KC // 2) * WB]))
                for cb in range(D // OCB):
                    for hf in range(KO // (KW // 2)):
                        k0 = hf * (KW // 2)
                        jobs.append((w_out[l][k0 * 128:(k0 + KW // 2) * 128, cb * OCB:(cb + 1) * OCB], KW // 2,
                                     wod[cb][:, k0 * OCB:(k0 + KW // 2) * OCB]))
                for ji, (src, nk, dst) in enumerate(jobs):
                    fb, bb = fst[ji % 2], bst[ji % 2]
                    kf, kb_ = "fst%d" % (ji % 2), "bst%d" % (ji % 2)
                    q2 = max(1, nk // 2)
                    for k0 in range(0, nk, q2):
                        S.dma("act" if ji % 2 else "sp", fb[:, k0:k0 + q2, :], src[k0 * 128:(k0 + q2) * 128, :].rearrange("(kc p) c -> p kc c", p=128),
                              w=[(kf, k0)])
                    fkeys = [(kf, k0) for k0 in range(0, nk, q2)]
                    if ji % 2 == 0:
                        S.add("dve", lambda e: e.tensor_copy(out=bb[:, 0:nk, :], in_=fb[:, 0:nk, :]), r=fkeys, w=[kb_])
                    else:
                        S.add("act", lambda e: e.copy(out=bb[:, 0:nk, :], in_=fb[:, 0:nk, :]), r=fkeys, w=[kb_])
                    seg = min(nk, max(1, 2048 // WB))
                    for k0 in range(0, nk, seg):
                        S.dma("sp" if ji % 2 else "act", dst[:, k0 * WB:(k0 + seg) * WB], bb[:, k0:k0 + seg, :].rearrange("p k c -> p (k c)"),
                              r=[kb_], w=["wbd"])
                S.flush()

            with contextlib.ExitStack() as es:
                uT = sb(es, "uT", [128, KC, TM], BF16)
                grep = sb(es, "grep", [128, D])
                xt = [sb(es, "xt%d" % i, [128, D]) for i in range(2)]
                ub = [sb(es, "ub%d" % i, [128, D], BF16) for i in range(2)]
                junk = sb(es, "junk", [128, D], BF16)
                sst = sb(es, "sst", [128, 8])
                wbf = [sb(es, "wbf%d" % i, [128, KC, WB], BF16) for i in range(2)]
                wif = sb(es, "wif", [128, KC, 2 * H], BF16)
                pre = [sb(es, "pre%d" % i, [128, 4 + TM]) for i in range(2)]
                cacc = [sb(es, "cacc%d" % i, [128, TM]) for i in range(1)]
                stg = [sb(es, "stg%d" % i, [128, TM], BF16) for i in range(2)]
                tstg = [sb(es, "tstg%d" % i, [128, NTM, WB], BF16) for i in range(1)]
                ifs = sb(es, "ifs", [128, NTM, 2 * H])
                hsv = sb(es, "hsv", [128, cfg.nqkb + NB, 4])
                psA = [ps(es, "psA%d" % i, [128, 512]) for i in range(5)]
                pst = [ps(es, "pst%d" % i, [128, 4, 128], BF16) for i in range(2)]

                S.dma("act", grep[:], bass.AP(rowv.tensor, l * cfg.RW + cfg.r_ng, [[0, 128], [1, D]]), w=["grep"])
                S.add("pool", lambda e: e.memset(hsv[:], 0.0), w=["hsv"])

                def norm_tile(gn, n):
                    bi = gn % 2
                    xb_, ub_ = xt[bi], ub[bi]
                    kx, ku = "xt%d" % bi, "ub%d" % bi
                    S.dma("sp", xb_[:], xsrc[gn * 128:(gn + 1) * 128, :], r=["xres"], w=[kx])
                    c0 = (bi * 4)
                    S.add("act", lambda e: e.activation(out=junk[:], in_=xb_[:], func=AF.Square,
                                                        accum_out=sst[:, c0:c0 + 1]), r=[kx], w=["junk", ("sst", c0)])
                    S.add("act", lambda e: e.activation(out=sst[:, c0 + 1:c0 + 2], in_=sst[:, c0:c0 + 1], func=AF.Ln,
                                                        scale=1.0 / D, bias=cc(C_EPS)), r=[("sst", c0), "cst"], w=[("sst", c0 + 1)])
                    S.add("act", lambda e: e.activation(out=sst[:, c0 + 2:c0 + 3], in_=sst[:, c0 + 1:c0 + 2], func=AF.Exp,
                                                        scale=-0.5), r=[("sst", c0 + 1)], w=[("sst", c0 + 2)])
                    S.add("dve", lambda e: e.scalar_tensor_tensor(out=ub_[:], in0=xb_[:], scalar=sst[:, c0 + 2:c0 + 3],
                                                                  in1=grep[:], op0=ALU.mult, op1=ALU.mult),
                          r=[kx, ("sst", c0 + 2), "grep"], w=[ku])
                    for g in range(KC // 4):
                        pt = pst[g % 2]
                        for i in range(4):
                            kc = g * 4 + i
                            S.add("pe", lambda e: e.transpose(pt[:, i, :], ub_[:, kc * 128:(kc + 1) * 128], idb[:]),
                                  r=[ku, "idb"], w=["pst%d" % (g % 2)])
                        dst, kd = uT[:, g * 4:(g + 1) * 4, n * 128:(n + 1) * 128], ("uT", n)
                        if g % 2 == 0:
                            S.add("act", lambda e: e.copy(out=dst, in_=pt[:]), r=["pst%d" % (g % 2)], w=[kd])
                        else:
                            S.add("dve", lambda e: e.tensor_copy(out=dst, in_=pt[:]), r=["pst%d" % (g % 2)], w=[kd])

                wstep = max(1, KC // 4)
                wseq = [0]

                def load_w(bi):
                    kind, c0 = blocks[bi]
                    if kind == "if":
                        for k0 in range(0, KC, wstep):
                            S.dma("pool", wif[:, k0:k0 + wstep, :], w_in[l][k0 * 128:(k0 + wstep) * 128, c0:c0 + 2 * H].rearrange("(kc p) c -> p kc c", p=128), w=["wif"])
                        return None
                    slot = wseq[0] % 2
                    wseq[0] += 1
                    wb = wbf[slot]
                    for k0 in range(0, KC, wstep):
                        S.dma("pool", wb[:, k0:k0 + wstep, :].rearrange("p k c -> p (k c)"),
                              wbd[bi][:, k0 * WB:(k0 + wstep) * WB], r=["wbd"], w=[("wbf", slot, k0)])
                    return slot

                psi = [0]
                fmi = [0]

                def next_ps():
                    i = psi[0] % 5
                    psi[0] += 1
                    return psA[i], "psA%d" % i

                def fm_block(bi, slot, mt):
                    kind, c0 = blocks[bi]
                    wb = wbf[slot]
                    wkeys = [("wbf", slot, k0) for k0 in range(0, KC, wstep)]
                    t0 = mt * TM
                    for sbk in range(NSB):
                        fi = fmi[0] % 2
                        fmi[0] += 1
                        pr, ca, sg = pre[fi], cacc[0], stg[fi]
                        kpr, kca, ksg = "pre%d" % fi, "cacc0", "stg%d" % fi
                        conv = kind in ("qk", "xb")
                        if conv:
                            if kind == "qk":
                                blk = (c0 - cfg.c_q) // 128 + sbk
                                hb_ = blk
                                wcol = cfg.p_qkc + blk * 4
                                dst_rows = qk_fm[blk * 128:(blk + 1) * 128, t0:t0 + TM]
                            else:
                                blk = (c0 - cfg.c_xb) // 128 + sbk
                                hb_ = cfg.nqkb + blk
                                wcol = cfg.p_lcw + blk * 4
                                dst_rows = xc_fm[blk * 128:(blk + 1) * 128, t0:t0 + TM]
                            S.add("dve", lambda e: e.tensor_copy(out=pr[:, 0:3], in_=hsv[:, hb_, 0:3]), r=["hsv", ("hsv", hb_)], w=[(kpr, "h")])
                        else:
                            blk = (c0 - cfg.c_zb) // 128 + sbk
                            dst_rows = szb_fm[blk * 128:(blk + 1) * 128, t0:t0 + TM]
                        for st in range(NSTM):
                            pA, kA = next_ps()
                            for kc in range(KC):
                                S.add("pe", lambda e: e.matmul(
                                    pA[:, 0:SW], lhsT=wb[:, kc, sbk * 128:(sbk + 1) * 128], rhs=uT[:, kc, st * SW:(st + 1) * SW],
                                    start=(kc == 0), stop=(kc == KC - 1)),
                                    r=wkeys + [("uT", st * TPS + i) for i in range(TPS)], w=[kA])
                            if conv:
                                S.add("act", lambda e: e.copy(out=pr[:, 3 + st * SW:3 + (st + 1) * SW], in_=pA[:, 0:SW]),
                                      r=[kA], w=[(kpr, st)])
                            else:
                                S.add("act", lambda e: e.activation(out=sg[:, st * SW:(st + 1) * SW], in_=pA[:, 0:SW], func=AF.Silu),
                                      r=[kA], w=[ksg])
                        if conv:
                            prk = [(kpr, "h")] + [(kpr, st) for st in range(NSTM)]
                            S.add("act", lambda e: e.copy(out=hsv[:, hb_, 0:3], in_=pr[:, TM:TM + 3]), r=prk, w=[("hsv", hb_)])
                            if kind == "xb":
                                bcol = cfg.p_lcb + blk
                                S.add("dve", lambda e: e.tensor_scalar(out=ca[:], in0=pr[:, 0:TM], scalar1=chs[:, wcol:wcol + 1],
                                                                       scalar2=chs[:, bcol:bcol + 1], op0=ALU.mult, op1=ALU.add),
                                      r=prk + ["chs"], w=[kca])
                            else:
                                S.add("dve", lambda e: e.tensor_scalar(out=ca[:], in0=pr[:, 0:TM], scalar1=chs[:, wcol:wcol + 1],
                                                                       scalar2=0.0, op0=ALU.mult, op1=ALU.add),
                                      r=prk + ["chs"], w=[kca])
                            for j in (1, 2):
                                S.add("dve", lambda e: e.scalar_tensor_tensor(out=ca[:], in0=pr[:, j:j + TM], scalar=chs[:, wcol + j:wcol + j + 1],
                                                                              in1=ca[:], op0=ALU.mult, op1=ALU.add),
                                      r=prk + ["chs", kca], w=[kca])
                            if kind == "xb":
                                S.add("dve", lambda e: e.scalar_tensor_tensor(out=sg[:], in0=pr[:, 3:3 + TM], scalar=chs[:, wcol + 3:wcol + 4],
                                                                              in1=ca[:], op0=ALU.mult, op1=ALU.add),
                                      r=prk + ["chs", kca], w=[ksg])
                            else:
                                S.add("dve", lambda e: e.scalar_tensor_tensor(out=ca[:], in0=pr[:, 3:3 + TM], scalar=chs[:, wcol + 3:wcol + 4],
                                                                              in1=ca[:], op0=ALU.mult, op1=ALU.add),
                                      r=prk + ["chs", kca], w=[kca])
                                S.add("act", lambda e: e.activation(out=sg[:], in_=ca[:], func=AF.Silu), r=[kca], w=[ksg])
                        S.dma("sp", dst_rows, sg[:], r=[ksg], w=["fm_out"])

                def tm_block(bi, slot, mt):
                    kind, c0 = blocks[bi]
                    wb = wbf[slot]
                    wkeys = [("wbf", slot, k0) for k0 in range(0, KC, wstep)]
                    tg, ktg = tstg[0], "tstg0"
                    for n in range(NTM):
                        pA, kA = next_ps()
                        for kc in range(KC):
                            S.add("pe", lambda e: e.matmul(
                                pA[:, 0:WB], lhsT=uT[:, kc, n * 128:(n + 1) * 128], rhs=wb[:, kc, :],
                                start=(kc == 0), stop=(kc == KC - 1)), r=wkeys + [("uT", n)], w=[kA])
                        if kind == "v":
                            if n % 2 == 0:
                                S.add("dve", lambda e: e.tensor_copy(out=tg[:, n, :], in_=pA[:, 0:WB]), r=[kA], w=[(ktg, n)])
                            else:
                                S.add("act", lambda e: e.copy(out=tg[:, n, :], in_=pA[:, 0:WB]), r=[kA], w=[(ktg, n)])
                        else:
                            fn = AF.Sigmoid if kind == "o" else AF.Silu
                            S.add("act", lambda e: e.activation(out=tg[:, n, :], in_=pA[:, 0:WB], func=fn), r=[kA], w=[(ktg, n)])
                    base = {"v": (v_tok, cfg.c_v), "o": (so_tok, cfg.c_o), "za": (sza_tok, cfg.c_za)}[kind]
                    cc0 = c0 - base[1]
                    q4 = max(1, NTM // 4)
                    for n0 in range(0, NTM, q4):
                        r0 = mt * TM + n0 * 128
                        S.dma("sp", base[0][r0:r0 + q4 * 128, cc0:cc0 + WB].rearrange("(n p) c -> p n c", p=128),
                              tg[:, n0:n0 + q4, :], r=[(ktg, n) for n in range(n0, n0 + q4)], w=["tm_out"])

                def if_block(mt):
                    for n in range(NTM):
                        pA, kA = next_ps()
                        for kc in range(KC):
                            S.add("pe", lambda e: e.matmul(
                                pA[:, 0:2 * H], lhsT=uT[:, kc, n * 128:(n + 1) * 128], rhs=wif[:, kc, :],
                                start=(kc == 0), stop=(kc == KC - 1)), r=["wif", ("uT", n)], w=[kA])
                        S.add("dve", lambda e: e.tensor_copy(out=ifs[:, n, :], in_=pA[:, 0:2 * H]), r=[kA], w=["ifs"])
                    S.dma("sp", if_d[:, mt * NTM * 2 * H:(mt + 1) * NTM * 2 * H], ifs[:].rearrange("p n c -> p (n c)"), r=["ifs"], w=["if_d"])

                seq = [(mt, bi) for mt in range(NMT) for bi in range(NBLK)]
                slots = {}
                slots[0] = load_w(seq[0][1])
                for si, (mt, bi) in enumerate(seq):
                    if bi == 0:
                        for n in range(NTM):
                            norm_tile(mt * NTM + n, n)
                    if si + 1 < len(seq):
                        slots[si + 1] = load_w(seq[si + 1][1])
                    kind = blocks[bi][0]
                    if kind in ("qk", "xb", "zb"):
                        fm_block(bi, slots[si], mt)
                    elif kind == "if":
                        if_block(mt)
                    else:
                        tm_block(bi, slots[si], mt)
                S.flush()
            if stop == "P1":
                return nc

            with contextlib.ExitStack() as es:
                ifs = sb(es, "ifs2", [128, NT, 2 * H])
                ibf = sb(es, "ibf", [128, 2 * H])
                zf = sb(es, "zf", [128, NT, H])
                lp = sb(es, "lp", [128, NT, H])
                t1 = sb(es, "t1", [128, NT, H])
                t2 = sb(es, "t2", [128, NT, H])
                ga = sb(es, "ga", [128, NT, H])
                gwk = sb(es, "gwk", [128, NT, H])
                gwk0 = sb(es, "gwk0", [128, NT, H])
                gwk1 = sb(es, "gwk1", [128, NT, H])
                ge = sb(es, "ge", [128, NT, H])
                gd0 = sb(es, "gd0", [128, NT, H])
                gd1 = sb(es, "gd1", [128, NT, H])
                Cf = sb(es, "Cf", [128, H, 257])
                Cb = sb(es, "Cb", [128, H, 257], BF16)
                hgrep = sb(es, "hgrep", [128, D])
                qs = [sb(es, "qs%d" % i, [128, H, SW], BF16) for i in range(2)]
                ks = [sb(es, "ks%d" % i, [128, H, SW], BF16) for i in range(2)]
                va = [sb(es, "va%d" % i, [128, H, 257], BF16) for i in range(2)]
                sos = [sb(es, "sos%d" % i, [128, D], BF16) for i in range(2)]
                szs = [sb(es, "szs%d" % i, [128, D], BF16) for i in range(2)]
                qz0 = [sb(es, "qz0_%d" % i, [128, H, 128], BF16) for i in range(2)]
                qz1 = [sb(es, "qz1_%d" % i, [128, H, 128], BF16) for i in range(2)]
                kw0 = [sb(es, "kw0_%d" % i, [128, 128], BF16) for i in range(4)]
                kw1 = [sb(es, "kw1_%d" % i, [128, 128], BF16) for i in range(4)]
                scT = [sb(es, "scT%d" % i, [128, 128], BF16) for i in range(4)]
                dsm = sb(es, "dsm", [128, 16])
                hall = [sb(es, "hall%d" % i, [128, H, 256]) for i in range(1)]
                sqb = sb(es, "sqb", [128, H, 256])
                st8 = sb(es, "st8", [128, 8, H])
                ybf = sb(es, "ybf", [128, D], BF16)
                yts = [sb(es, "yts%d" % i, [128, KC, 128], BF16) for i in range(2)]
                wab = sb(es, "wab", [128, H, 2, 256], BF16)
                wxb = sb(es, "wxb", [128, H, 2, 256], BF16)
                ccoef = sb(es, "ccoef", [128, 4, NB])
                hcar = sb(es, "hcar", [128, NB])
                xcs = [sb(es, "xcs%d" % i, [128, 2, SW], BF16) for i in range(2)]
                zbs = [sb(es, "zbs%d" % i, [128, 2, SW], BF16) for i in range(2)]
                lt = {nm: [sb(es, "%s%d" % (nm, i), [128, SW]) for i in range(2)]
                      for nm in ("er", "ei", "la", "a2", "hh")}
                yb = [sb(es, "yb%d" % i, [128, SW], BF16) for i in range(2)]

                p_acc = [ps(es, "pacc%d" % i, [128, 512]) for i in range(4)]
                p_dc = [ps(es, "pdc%d" % i, [128, 512]) for i in range(2)]
                p_ss = ps(es, "pss", [128, 4, 128])
                p_kt = ps(es, "pkt", [128, 4, 128], BF16)

                S.dma("sp", ifs[:].rearrange("p n c -> p (n c)"), if_d, r=["if_d"], w=["ifs"])
                S.dma("act", ibf[:], bass.AP(rowv.tensor, l * cfg.RW + cfg.r_ib, [[0, 128], [1, 2 * H]]), w=["ibf"])
                S.dma("act", hgrep[:], bass.AP(rowv.tensor, l * cfg.RW + cfg.r_hg, [[0, 128], [1, D]]), w=["hgrep"])
                for h0 in range(0, H, 2):
                    S.dma("pool", wab[:, h0:h0 + 2], w_a[l][h0:h0 + 2].rearrange("h (ib p) j -> p h ib j", p=128), w=["wab"])
                    S.dma("pool", wxb[:, h0:h0 + 2], w_x[l][h0:h0 + 2].rearrange("h (ib p) j -> p h ib j", p=128), w=["wxb"])
                S.add("dve", lambda e: e.tensor_tensor(out=ifs[:, :, 0:H], in0=ifs[:, :, 0:H], in1=bc_mid(ibf[:, 0:H], NT), op=ALU.add),
                      r=["ifs", "ibf"], w=["ifs"])
                S.add("dve", lambda e: e.tensor_tensor(out=zf[:], in0=ifs[:, :, H:2 * H], in1=bc_mid(ibf[:, H:2 * H], NT), op=ALU.add),
                      r=["ifs", "ibf"], w=["zf"])
                S.add("act", lambda e: e.activation(out=zf[:], in_=zf[:], func=AF.Exp, scale=-1.0), r=["zf"], w=["zf"])
                S.add("act", lambda e: e.activation(out=lp[:], in_=zf[:], func=AF.Ln, scale=1.0, bias=cc(C_ONE)), r=["zf", "cst"], w=["lp"])
                GN = max(1, 512 // H)
                for g0 in range(0, NT, GN):
                    g1 = min(NT, g0 + GN)
                    NH = (g1 - g0) * H
                    lp2 = lp[:, g0:g1, :].rearrange("p n h -> p (n h)")
                    pg, pb_, pd0, pd1 = p_acc[0], p_acc[1], p_acc[2], p_acc[3]
                    S.add("pe", lambda e: e.matmul(pg[:, 0:NH], lhsT=cst[:, C_L:C_L + 128], rhs=lp2, start=True, stop=True), r=["lp", "cst"], w=["pacc0"])
                    S.add("pe", lambda e: e.matmul(pb_[:, 0:NH], lhsT=cst[:, C_B:C_B + 128], rhs=lp2, start=True, stop=True), r=["lp", "cst"], w=["pacc1"])
                    S.add("pe", lambda e: e.matmul(pd0[:, 0:NH], lhsT=cst[:, C_E0:C_E0 + 128], rhs=lp2, start=True, stop=True), r=["lp", "cst"], w=["pacc2"])
                    S.add("pe", lambda e: e.matmul(pd1[:, 0:NH], lhsT=cst[:, C_E1:C_E1 + 128], rhs=lp2, start=True, stop=True), r=["lp", "cst"], w=["pacc3"])

                    def v3(p):
                        return p[:, 0:NH].rearrange("p (n h) -> p n h", h=H)
                    gs = slice(g0, g1)
                    S.add("dve", lambda e: e.tensor_tensor(out=t1[:, gs, :], in0=ifs[:, gs, 0:H], in1=v3(pg), op=ALU.add), r=["ifs", "pacc0"], w=["t1"])
                    S.add("act", lambda e: e.activation(out=ga[:, gs, :], in_=t1[:, gs, :], func=AF.Exp), r=["t1"], w=["ga"])
                    S.add("dve", lambda e: e.tensor_tensor(out=t2[:, gs, :], in0=t1[:, gs, :], in1=v3(pb_), op=ALU.subtract), r=["t1", "pacc1"], w=["t2"])
                    S.add("act", lambda e: e.activation(out=gwk[:, gs, :], in_=t2[:, gs, :], func=AF.Exp), r=["t2"], w=["gwk"])
                    S.add("act", lambda e: e.activation(out=ge[:, gs, :], in_=v3(pg), func=AF.Exp, scale=1.0, bias=cc(C_LSK)), r=["pacc0", "cst"], w=["ge"])
                    S.add("act", lambda e: e.activation(out=gd0[:, gs, :], in_=v3(pd0), func=AF.Exp, scale=-1.0), r=["pacc2"], w=["gd0"])
                    S.add("act", lambda e: e.activation(out=gd1[:, gs, :], in_=v3(pd1), func=AF.Exp, scale=-1.0), r=["pacc3"], w=["gd1"])
                S.add("dve", lambda e: e.tensor_scalar(out=gwk0[:], in0=gwk[:], scalar1=cc(C_HM0), scalar2=0.0, op0=ALU.mult, op1=ALU.add), r=["gwk", "cst"], w=["gwk0"])
                S.add("dve", lambda e: e.tensor_scalar(out=gwk1[:], in0=gwk[:], scalar1=cc(C_HM1), scalar2=0.0, op0=ALU.mult, op1=ALU.add), r=["gwk", "cst"], w=["gwk1"])
                lam = chs[:, cfg.p_lam:cfg.p_lam + NB]
                S.add("act", lambda e: e.activation(out=ccoef[:, 0, :], in_=lam, func=AF.Exp, scale=-1.0), r=["chs"], w=["ccoef"])
                S.add("act", lambda e: e.activation(out=ccoef[:, 0, :], in_=ccoef[:, 0, :], func=AF.Ln, scale=1.0, bias=cc(C_ONE)), r=["ccoef", "cst"], w=["ccoef"])
                S.add("dve", lambda e: e.tensor_scalar(out=ccoef[:, 1, :], in0=ccoef[:, 0, :], scalar1=-2.0 * LRU_C, scalar2=0.0, op0=ALU.mult, op1=ALU.add), r=["ccoef"], w=["ccoef"])
                S.add("dve", lambda e: e.tensor_scalar(out=ccoef[:, 0, :], in0=ccoef[:, 0, :], scalar1=-LRU_C, scalar2=0.0, op0=ALU.mult, op1=ALU.add), r=["ccoef"], w=["ccoef"])
                S.add("dve", lambda e: e.tensor_scalar(out=ccoef[:, 2, :], in0=chs[:, cfg.p_ba:cfg.p_ba + NB], scalar1=-1.0, scalar2=0.0, op0=ALU.mult, op1=ALU.add),
                      r=["chs"], w=["ccoef2"])
                S.add("dve", lambda e: e.tensor_scalar(out=ccoef[:, 3, :], in0=chs[:, cfg.p_bx:cfg.p_bx + NB], scalar1=-1.0, scalar2=0.0, op0=ALU.mult, op1=ALU.add),
                      r=["chs"], w=["ccoef2"])
                S.add("pool", lambda e: e.memset(Cf[:], 0.0), w=["Cf"] + [("Cf", h) for h in range(H)])
                S.add("pool", lambda e: e.memset(Cb[:], 0.0), w=[("Cb", h) for h in range(H)])
                S.add("pool", lambda e: e.memset(hcar[:], 0.0), w=["hcar"] + [("hcar", b) for b in range(NB)])
                for i in range(2):
                    S.add("pool", lambda e: e.memset(qz0[i][:], 0.0), w=["qz0_%d" % i])
                    S.add("pool", lambda e: e.memset(qz1[i][:], 0.0), w=["qz1_%d" % i])
                for i in range(2):
                    S.add("pool", lambda e: e.memset(va[i][:, :, 256:257], 1.0), w=["va%d" % i])

                cnt = {"kw": 0, "sc": 0, "acc": 0, "dc": 0}

                def mlstm_tile(n):
                    st, ti = n // TPS, n % TPS
                    qb, kb = qs[st % 2], ks[st % 2]
                    kq, kk = "qs%d" % (st % 2), "ks%d" % (st % 2)
                    if ti == 0:
                        S.dma("sp", kb[:], qk_fm[NQK:2 * NQK, st * SW:(st + 1) * SW].rearrange("(h p) t -> p h t", p=128), r=["fm_out"], w=[kk])
                        S.dma("sp", qb[:], qk_fm[0:NQK, st * SW:(st + 1) * SW].rearrange("(h p) t -> p h t", p=128), r=["fm_out"], w=[kq])
                    vb, kv = va[n % 2], "va%d" % (n % 2)
                    S.dma("act", vb[:, :, 0:256], v_tok[n * 128:(n + 1) * 128, :].rearrange("p (h v) -> p h v", v=256), r=["tm_out"], w=[kv])
                    tc0, tc1 = ti * 128, (ti + 1) * 128
                    sob, szb_ = sos[n % 2], szs[n % 2]
                    kso, ksz = "sos%d" % (n % 2), "szs%d" % (n % 2)
                    S.dma("act", sob[:], so_tok[n * 128:(n + 1) * 128, :], r=["tm_out"], w=[kso])
                    S.dma("act", szb_[:], sza_tok[n * 128:(n + 1) * 128, :], r=["tm_out"], w=[ksz])
                    z0, z1 = qz0[n % 2], qz1[n % 2]
                    kz0, kz1 = "qz0_%d" % (n % 2), "qz1_%d" % (n % 2)
                    S.add("pool", lambda e: e.tensor_copy(out=z0[:, :, 0:64], in_=qb[:, :, tc0:tc0 + 64]), r=[kq], w=[kz0])
                    S.add("pool", lambda e: e.tensor_copy(out=z1[:, :, 64:128], in_=qb[:, :, tc0 + 64:tc0 + 128]), r=[kq], w=[kz1])
                    hl, khl = hall[0], "hall0"
                    for h0 in range(0, H, 2):
                        grp = list(range(h0, min(H, h0 + 2)))
                        info = {}
                        for h in grp:
                            ki = cnt["kw"] % 4
                            cnt["kw"] += 1
                            k0t, k1t = kw0[ki], kw1[ki]
                            kk0, kk1 = "kw0_%d" % ki, "kw1_%d" % ki
                            pk = ("pkt", ki)
                            S.add("pe", lambda e: e.transpose(p_kt[:, ki, :], kb[:, h, tc0:tc1], idb[:]), r=[kk, "idb"], w=[pk])
                            S.add("act", lambda e: e.activation(out=k0t[:], in_=p_kt[:, ki, :], func=AF.Copy, scale=gwk0[:, n, h:h + 1]),
                                  r=[pk, "gwk0"], w=[kk0])
                            S.add("act", lambda e: e.activation(out=k1t[:], in_=p_kt[:, ki, :], func=AF.Copy, scale=gwk1[:, n, h:h + 1]),
                                  r=[pk, "gwk1"], w=[kk1])
                            si = cnt["sc"] % 4
                            cnt["sc"] += 1
                            sct, ksc, pss_k = scT[si], "scT%d" % si, ("pss", si)
                            S.add("pe", lambda e: e.matmul(p_ss[:, si, :], lhsT=kb[:, h, tc0:tc1], rhs=qb[:, h, tc0:tc1], start=True, stop=True),
                                  r=[kk, kq], w=[pss_k])
                            S.add("dve", lambda e: e.scalar_tensor_tensor(out=sct[:], in0=p_ss[:, si, :], scalar=ga[:, n, h:h + 1],
                                                                          in1=mLb[:], op0=ALU.mult, op1=ALU.mult),
                                  r=[pss_k, "ga", "mLb"], w=[ksc])
                            ai = cnt["acc"] % 4
                            cnt["acc"] += 1
                            pa, kpa = p_acc[ai], "pacc%d" % ai
                            S.add("pe", lambda e: e.matmul(pa[:, 0:257], lhsT=sct[:], rhs=vb[:, h, :], start=True, stop=False),
                                  r=[ksc, kv], w=[kpa])
                            S.add("pe", lambda e: e.matmul(pa[:, 0:257], lhsT=z0[:, h, :], rhs=Cb[:, h, :], start=False, stop=False),
                                  r=[kz0, ("Cb", h)], w=[kpa])
                            info[h] = (k0t, k1t, kk0, kk1, pa, kpa)
                            di = cnt["dc"] % 2
                            cnt["dc"] += 1
                            pd, kpd = p_dc[di], "pdc%d" % di
                            S.add("pe", lambda e: e.matmul(pd[:, 0:257], lhsT=k0t[:], rhs=vb[:, h, :], start=True, stop=True),
                                  r=[kk0, kv], w=[kpd])
                            S.add("dve", lambda e: e.scalar_tensor_tensor(out=Cf[:, h, :], in0=Cf[:, h, :], scalar=gd0[:, n, h:h + 1],
                                                                          in1=pd[:, 0:257], op0=ALU.mult, op1=ALU.add),
                                  r=[("Cf", h), "gd0", kpd], w=[("Cf", h)])
                            S.add("act", lambda e: e.copy(out=Cb[:, h, :], in_=Cf[:, h, :]), r=[("Cf", h)], w=[("Cb", h)])
                        for h in grp:
                            k0t, k1t, kk0, kk1, pa, kpa = info[h]
                            S.add("pe", lambda e: e.matmul(pa[:, 0:257], lhsT=z1[:, h, :], rhs=Cb[:, h, :], start=False, stop=True),
                                  r=[kz1, ("Cb", h)], w=[kpa])
                            di = cnt["dc"] % 2
                            cnt["dc"] += 1
                            pd, kpd = p_dc[di], "pdc%d" % di
                            S.add("pe", lambda e: e.matmul(pd[:, 0:257], lhsT=k1t[:], rhs=vb[:, h, :], start=True, stop=True),
                                  r=[kk1, kv], w=[kpd])
                            S.add("dve", lambda e: e.scalar_tensor_tensor(out=Cf[:, h, :], in0=Cf[:, h, :], scalar=gd1[:, n, h:h + 1],
                                                                          in1=pd[:, 0:257], op0=ALU.mult, op1=ALU.add),
                                  r=[("Cf", h), "gd1", kpd], w=[("Cf", h)])
                            S.add("act", lambda e: e.copy(out=Cb[:, h, :], in_=Cf[:, h, :]), r=[("Cf", h)], w=[("Cb", h)])
                            dcol = (h % 8) * 2
                            S.add("act", lambda e: e.activation(out=dsm[:, dcol:dcol + 1], in_=pa[:, 256:257], func=AF.Abs),
                                  r=[kpa], w=[("dsm", dcol)])
                            S.add("dve", lambda e: e.tensor_tensor(out=dsm[:, dcol:dcol + 1], in0=dsm[:, dcol:dcol + 1], in1=ge[:, n, h:h + 1], op=ALU.max),
                                  r=[("dsm", dcol), "ge"], w=[("dsm", dcol)])
                            S.add("dve", lambda e: e.reciprocal(out=dsm[:, dcol + 1:dcol + 2], in_=dsm[:, dcol:dcol + 1]),
                                  r=[("dsm", dcol)], w=[("dsm", dcol + 1)])
                            S.add("act", lambda e: e.activation(out=hl[:, h, :], in_=pa[:, 0:256], func=AF.Copy, scale=dsm[:, dcol + 1:dcol + 2]),
                                  r=[kpa, ("dsm", dcol + 1)], w=[(khl, h)])
                    hk = [(khl, h) for h in range(H)]
                    hl2 = hl[:].rearrange("p h v -> p (h v)")
                    S.add("dve", lambda e: e.tensor_tensor(out=hl2, in0=hl2, in1=sob[:], op=ALU.mult), r=hk + [kso], w=[khl])
                    for h in range(H):
                        S.add("act", lambda e: e.activation(out=sqb[:, h, :], in_=hl[:, h, :], func=AF.Identity, accum_out=st8[:, 0, h:h + 1]),
                              r=[khl], w=["sqb", ("st8", 0)])
                        S.add("act", lambda e: e.activation(out=sqb[:, h, :], in_=hl[:, h, :], func=AF.Square, accum_out=st8[:, 1, h:h + 1]),
                              r=[khl], w=["sqb", ("st8", 1)])
                    S.add("dve", lambda e: e.tensor_scalar(out=st8[:, 2, :], in0=st8[:, 0, :], scalar1=1.0 / 256, scalar2=0.0, op0=ALU.mult, op1=ALU.add),
                          r=[("st8", 0)], w=[("st8", 2)])
                    S.add("dve", lambda e: e.tensor_tensor(out=st8[:, 3, :], in0=st8[:, 2, :], in1=st8[:, 2, :], op=ALU.mult), r=[("st8", 2)], w=[("st8", 3)])
                    S.add("dve", lambda e: e.scalar_tensor_tensor(out=st8[:, 4, :], in0=st8[:, 1, :], scalar=1.0 / 256, in1=st8[:, 3, :], op0=ALU.mult, op1=ALU.subtract),
                          r=[("st8", 1), ("st8", 3)], w=[("st8", 4)])
                    S.add("act", lambda e: e.activation(out=st8[:, 5, :], in_=st8[:, 4, :], func=AF.Ln, scale=1.0, bias=cc(C_EPS)), r=[("st8", 4), "cst"], w=[("st8", 5)])
                    S.add("act", lambda e: e.activation(out=st8[:, 6, :], in_=st8[:, 5, :], func=AF.Exp, scale=-0.5), r=[("st8", 5)], w=[("st8", 6)])
                    S.add("dve", lambda e: e.scalar_tensor_tensor(out=st8[:, 7, :], in0=st8[:, 2, :], scalar=-1.0, in1=st8[:, 6, :], op0=ALU.mult, op1=ALU.mult),
                          r=[("st8", 2), ("st8", 6)], w=[("st8", 7)])
                    for h in range(H):
                        S.add("act", lambda e: e.activation(out=hl[:, h, :], in_=hl[:, h, :], func=AF.Identity, scale=st8[:, 6, h:h + 1], bias=st8[:, 7, h:h + 1]),
                              r=[khl, ("st8", 6), ("st8", 7)], w=[khl])
                    S.add("dve", lambda e: e.tensor_tensor(out=hl2, in0=hl2, in1=hgrep[:], op=ALU.mult), r=[khl, "hgrep"], w=[khl])
                    S.add("dve", lambda e: e.tensor_tensor(out=ybf[:], in0=hl2, in1=szb_[:], op=ALU.mult), r=[khl, ksz], w=["ybf"])
                    yt, kyt = yts[n % 2], "yts%d" % (n % 2)
                    for g in range(KC // 4):
                        for i in range(4):
                            kc = g * 4 + i
                            S.add("pe", lambda e: e.transpose(p_kt[:, i, :], ybf[:, kc * 128:(kc + 1) * 128], idb[:]),
                                  r=["ybf", "idb"], w=[("pkt", i)])
                        S.add("act", lambda e: e.copy(out=yt[:, g * 4:(g + 1) * 4, :], in_=p_kt[:]),
                              r=[("pkt", i) for i in range(4)], w=[kyt])
                    for k0 in range(0, KC, 4):
                        S.dma("sp", y_fm[k0 * 128:(k0 + 4) * 128, n * 128:(n + 1) * 128].rearrange("(kc p) t -> p kc t", p=128), yt[:, k0:k0 + 4, :], r=[kyt], w=["y_fm"])

                lc = [0]

                def lru_step(h, st):
                    bi = lc[0] % 2
                    lc[0] += 1
                    xcb, kxc = xcs[bi], "xcs%d" % bi
                    zbb, kzb = zbs[bi], "zbs%d" % bi
                    S.dma("sp", xcb[:], xc_fm[h * 256:(h + 1) * 256, st * SW:(st + 1) * SW].rearrange("(b p) t -> p b t", p=128), r=["fm_out"], w=[kxc])
                    S.dma("sp", zbb[:], szb_fm[h * 256:(h + 1) * 256, st * SW:(st + 1) * SW].rearrange("(b p) t -> p b t", p=128), r=["fm_out"], w=[kzb])
                    pend = [[], []]
                    for jb in range(2):
                        S.defer = pend[jb]
                        blk = h * 2 + jb
                        pr_, kpr_ = p_acc[(2 * jb) % 4], "pacc%d" % ((2 * jb) % 4)
                        pi_, kpi_ = p_acc[(2 * jb + 1) % 4], "pacc%d" % ((2 * jb + 1) % 4)
                        for ib in range(2):
                            S.add("pe", lambda e: e.matmul(pr_[:, 0:SW], lhsT=wab[:, h, ib, jb * 128:(jb + 1) * 128], rhs=xcb[:, ib, :], start=(ib == 0), stop=(ib == 1)),
                                  r=["wab", kxc], w=[kpr_])
                        for ib in range(2):
                            S.add("pe", lambda e: e.matmul(pi_[:, 0:SW], lhsT=wxb[:, h, ib, jb * 128:(jb + 1) * 128], rhs=xcb[:, ib, :], start=(ib == 0), stop=(ib == 1)),
                                  r=["wxb", kxc], w=[kpi_])
                        (er, ker), (ei, kei), (la, kla), (a2, ka2), (hh, khh) = [(lt[nm][jb], "%s%d" % (nm, jb)) for nm in ("er", "ei", "la", "a2", "hh")]
                        S.add("act", lambda e: e.activation(out=er[:], in_=pr_[:, 0:SW], func=AF.Exp, scale=-1.0, bias=ccoef[:, 2, blk:blk + 1]),
                              r=[kpr_, "ccoef2"], w=[ker])
                        S.add("act", lambda e: e.activation(out=ei[:], in_=pi_[:, 0:SW], func=AF.Exp, scale=-1.0, bias=ccoef[:, 3, blk:blk + 1]),
                              r=[kpi_, "ccoef2"], w=[kei])
                        S.add("act", lambda e: e.activation(out=er[:], in_=er[:], func=AF.Ln, scale=1.0, bias=cc(C_ONE)), r=[ker, "cst"], w=[ker])
                        S.add("act", lambda e: e.activation(out=er[:], in_=er[:], func=AF.Exp, scale=-1.0), r=[ker], w=[ker])
                        S.add("act", lambda e: e.activation(out=ei[:], in_=ei[:], func=AF.Ln, scale=1.0, bias=cc(C_ONE)), r=[kei, "cst"], w=[kei])
                        S.add("act", lambda e: e.activation(out=ei[:], in_=ei[:], func=AF.Exp, scale=-1.0), r=[kei], w=[kei])
                        S.add("act", lambda e: e.activation(out=la[:], in_=er[:], func=AF.Exp, scale=ccoef[:, 0, blk:blk + 1]), r=[ker, "ccoef"], w=[kla])
                        S.add("act", lambda e: e.activation(out=a2[:], in_=er[:], func=AF.Exp, scale=ccoef[:, 1, blk:blk + 1]), r=[ker, "ccoef"], w=[ka2])
                        S.add("act", lambda e: e.activation(out=a2[:], in_=a2[:], func=AF.Ln, scale=-1.0, bias=cc(C_ONE)), r=[ka2, "cst"], w=[ka2])
                        S.add("act", lambda e: e.activation(out=a2[:], in_=a2[:], func=AF.Exp, scale=0.5), r=[ka2], w=[ka2])
                        S.add("dve", lambda e: e.tensor_tensor(out=ei[:], in0=ei[:], in1=xcb[:, jb, :], op=ALU.mult), r=[kei, kxc], w=[kei])
                        S.add("dve", lambda e: e.tensor_tensor(out=ei[:], in0=ei[:], in1=a2[:], op=ALU.mult), r=[kei, ka2], w=[kei])
                        S.add("dve", lambda e: e.tensor_tensor_scan(out=hh[:], data0=la[:], data1=ei[:], initial=hcar[:, blk:blk + 1],
                                                                    op0=ALU.mult, op1=ALU.add),
                              r=[kla, kei, ("hcar", blk)], w=[khh])
                        S.add("act", lambda e: e.copy(out=hcar[:, blk:blk + 1], in_=hh[:, SW - 1:SW]), r=[khh], w=[("hcar", blk)])
                        ybt, kyb = yb[jb], "yb%d" % jb
                        S.add("dve", lambda e: e.tensor_tensor(out=ybt[:], in0=hh[:], in1=zbb[:, jb, :], op=ALU.mult), r=[khh, kzb], w=[kyb])
                        S.dma("act", y_fm[D + blk * 128:D + (blk + 1) * 128, st * SW:(st + 1) * SW], ybt[:], r=[kyb], w=["y_fm"])
                    S.defer = None
                    for i_ in range(max(len(p_) for p_ in pend)):
                        for jb in range(2):
                            if i_ < len(pend[jb]):
                                S.register(*pend[jb][i_])

                for n in range(NT):
                    mlstm_tile(n)
                for h in range(H):
                    for st in range(NST):
                        lru_step(h, st)
                S.flush()
            if stop == "MR":
                return nc

            with contextlib.ExitStack() as es:
                wob = [sb(es, "wob%d" % i, [128, KO, OCB], BF16) for i in range(2)]
                yT = [sb(es, "yT%d" % i, [128, KO, SW], BF16) for i in range(2)]
                xr = [sb(es, "xr%d" % i, [128, OCB]) for i in range(3)]
                pso = [ps(es, "pso%d" % i, [128, 512]) for i in range(4)]
                NCB = D // OCB
                ostep = max(1, KO // 8)

                def load_wo(cb):
                    wb = wob[cb % 2]
                    for k0 in range(0, KO, ostep):
                        S.dma("pool", wb[:, k0:k0 + ostep, :].rearrange("p k c -> p (k c)"),
                              wod[cb][:, k0 * OCB:(k0 + ostep) * OCB], r=["wbd"], w=[("wob", cb % 2, k0)])
                load_wo(0)
                yi = 0
                xi = 0
                for cb in range(NCB):
                    if cb + 1 < NCB:
                        load_wo(cb + 1)
                    wb = wob[cb % 2]
                    wkeys = [("wob", cb % 2, k0) for k0 in range(0, KO, ostep)]
                    for st in range(NST):
                        yb_, kyb_ = yT[yi % 2], "yT%d" % (yi % 2)
                        yi += 1
                        hk = max(1, KO // 4)
                        for k0 in range(0, KO, hk):
                            S.dma("sp", yb_[:, k0:k0 + hk, :], y_fm[k0 * 128:(k0 + hk) * 128, st * SW:(st + 1) * SW].rearrange("(kc p) t -> p kc t", p=128),
                                  r=["y_fm"], w=[(kyb_, k0)])
                        for ti in range(TPS):
                            n = st * TPS + ti
                            xb_, kxb_ = xr[xi % 3], "xr%d" % (xi % 3)
                            po, kpo = pso[xi % 4], "pso%d" % (xi % 4)
                            xi += 1
                            S.dma("act", xb_[:], xsrc[n * 128:(n + 1) * 128, cb * OCB:(cb + 1) * OCB], r=[("xres", n, cb)], w=[kxb_])
                            for kc in range(KO):
                                S.add("pe", lambda e: e.matmul(po[:, 0:OCB], lhsT=yb_[:, kc, ti * 128:(ti + 1) * 128], rhs=wb[:, kc, :],
                                                               start=(kc == 0), stop=(kc == KO - 1)),
                                      r=wkeys + [(kyb_, k0) for k0 in range(0, KO, hk)], w=[kpo])
                            S.add("dve", lambda e: e.tensor_tensor(out=xb_[:], in0=po[:, 0:OCB], in1=xb_[:], op=ALU.add), r=[kpo, kxb_], w=[kxb_])
                            S.dma("act", xres[n * 128:(n + 1) * 128, cb * OCB:(cb + 1) * OCB], xb_[:], r=[kxb_], w=[("xres", n, cb)])
                S.flush()

        if stop == "P4":
            return nc
        with contextlib.ExitStack() as es:
            grep = sb(es, "fgrep", [128, D])
            xt = [sb(es, "fxt%d" % i, [128, D]) for i in range(2)]
            ot = [sb(es, "fot%d" % i, [128, D]) for i in range(2)]
            junk = sb(es, "fjunk", [128, D], BF16)
            sst = sb(es, "fsst", [128, 8])
            S.dma("act", grep[:], bass.AP(fin_g.tensor, 0, [[0, 128], [1, D]]), w=["grep"])
            for n in range(NT):
                bi = n % 2
                xb_, ob_ = xt[bi], ot[bi]
                kx, ko = "xt%d" % bi, "ot%d" % bi
                c0 = bi * 4
                S.dma("sp", xb_[:], xres[n * 128:(n + 1) * 128, :], w=[kx])
                S.add("act", lambda e: e.activation(out=junk[:], in_=xb_[:], func=AF.Square, accum_out=sst[:, c0:c0 + 1]), r=[kx], w=["junk", ("sst", c0)])
                S.add("act", lambda e: e.activation(out=sst[:, c0 + 1:c0 + 2], in_=sst[:, c0:c0 + 1], func=AF.Ln, scale=1.0 / D, bias=cc(C_EPS)),
                      r=[("sst", c0), "cst"], w=[("sst", c0 + 1)])
                S.add("act", lambda e: e.activation(out=sst[:, c0 + 2:c0 + 3], in_=sst[:, c0 + 1:c0 + 2], func=AF.Exp, scale=-0.5), r=[("sst", c0 + 1)], w=[("sst", c0 + 2)])
                S.add("dve", lambda e: e.scalar_tensor_tensor(out=ob_[:], in0=xb_[:], scalar=sst[:, c0 + 2:c0 + 3], in1=grep[:], op0=ALU.mult, op1=ALU.mult),
                      r=[kx, ("sst", c0 + 2), "grep"], w=[ko])
                hw_ = min(D, 1024)
                for c1 in range(0, D, hw_):
                    S.dma("sp", out_d[n * 128:(n + 1) * 128, c1:c1 + hw_], ob_[:, c1:c1 + hw_], r=[ko], w=["out"])
            S.flush()
    return nc


def host_inputs(cfg, x, norm_g, w_in, i_bias, f_bias, qk_conv, head_norm_g, lru_conv_w, lru_conv_b,
                w_a, b_a, w_x, b_x, lam, w_out, final_g):
    f = lambda a: np.ascontiguousarray(np.asarray(a, dtype=np.float32))
    L, D, T, NB = cfg.L, cfg.D, cfg.T, cfg.NB
    x = f(x)

    def chanpack(v):
        v = f(v)
        return v.reshape(L, -1, 128).transpose(0, 2, 1)

    def convpack(wc):
        wc = f(wc)
        C = wc.shape[2]
        return wc.reshape(L, 4, C // 128, 128).transpose(0, 3, 2, 1).reshape(L, 128, (C // 128) * 4)

    chan = np.concatenate([convpack(qk_conv), convpack(lru_conv_w), chanpack(lru_conv_b), chanpack(b_a),
                           chanpack(b_x), chanpack(lam)], axis=2)
    assert chan.shape == (L, 128, cfg.CW), chan.shape
    rowv = np.concatenate([f(norm_g), f(head_norm_g), f(i_bias), f(f_bias)], axis=1)
    assert rowv.shape == (L, cfg.RW)
    cstv = make_consts()
    shared = {
        "w_in": f(w_in)[:L], "w_out": f(w_out)[:L], "w_a": f(w_a)[:L], "w_x": f(w_x)[:L],
        "chan": np.ascontiguousarray(chan), "rowv": np.ascontiguousarray(rowv),
        "final_g": f(final_g).reshape(1, D), "cst": cstv,
    }
    maps = []
    zx = np.zeros((T, D), np.float32)
    for c in range(cfg.NCORES):
        m = dict(shared)
        m["x"] = np.ascontiguousarray(x[c]) if c < cfg.BATCH else zx
        maps.append(m)
    return maps


_NC_CACHE = {}


def run(cfg, inputs, dbg=()):
    key = (cfg.D, cfg.T, cfg.L, tuple(dbg))
    if key not in _NC_CACHE:
        _NC_CACHE[key] = build(cfg, dbg)
    nc = _NC_CACHE[key]
    maps = host_inputs(cfg, **inputs)
    res = run_bass_kernel_spmd(nc, maps, core_ids=list(range(cfg.NCORES)))
    return res


def kernel(**inputs):
    cfg = Cfg()
    res = run(cfg, inputs)
    return np.stack([np.asarray(res.results[b]["out"], dtype=np.float32) for b in range(cfg.BATCH)], axis=0)
```

```python
import contextlib
import numpy as np
import concourse.bass as bass
import concourse.mybir as mybir
from concourse.bass_utils import run_bass_kernel_spmd

F32, BF16 = mybir.dt.float32, mybir.dt.bfloat16
ALU = mybir.AluOpType
AF = mybir.ActivationFunctionType
AX = mybir.AxisListType
EPS = 1e-6
LRU_C = 8.0


class Cfg:
    def __init__(self, D=2048, T=8192, L=4, NCORES=8, BATCH=2, TM=2048):
        self.D, self.T, self.L, self.NCORES, self.BATCH = D, T, L, NCORES, BATCH
        self.NSEG = 1
        self.TM = min(TM, T)
        self.NMT = T // self.TM
        self.NTM = self.TM // 128
        H = self.H = D // 256
        self.NQK = H * 128
        self.NIN = 2 * self.NQK + 3 * D + 2 * H + 2 * D
        self.KC = D // 128
        self.NT = T // 128
        self.SW = min(512, self.TM)
        self.NST = T // self.SW
        self.NSTM = self.TM // self.SW
        self.TPS = self.SW // 128
        self.NB = D // 128
        self.c_q = 0
        self.c_k = self.NQK
        self.c_v = 2 * self.NQK
        self.c_o = self.c_v + D
        self.c_za = self.c_o + D
        self.c_i = self.c_za + D
        self.c_xb = self.c_i + 2 * H
        self.c_zb = self.c_xb + D
        self.HW = 3 * D // 128
        self.oC = 0
        self.oD = H * 257
        self.oHE = self.oD + H
        self.oA = self.oHE + self.NB
        self.SUMF = self.oA + self.NB
        self.nqkb = 2 * self.NQK // 128
        self.p_qkc = 0
        self.p_lcw = self.p_qkc + self.nqkb * 4
        self.p_lcb = self.p_lcw + self.NB * 4
        self.p_ba = self.p_lcb + self.NB
        self.p_bx = self.p_ba + self.NB
        self.p_lam = self.p_bx + self.NB
        self.CW = self.p_lam + self.NB
        self.r_ng = 0
        self.r_hg = D
        self.r_ib = 2 * D
        self.RW = 2 * D + 2 * H
        self.KO = 2 * D // 128
        self.OCB = min(512, D)


C_ID, C_L, C_B, C_E0, C_E1 = 0, 128, 256, 384, 512
C_HM0, C_HM1, C_EPS, C_ONE, C_LSK, C_EPS4 = 640, 641, 642, 643, 644, 645
CCW = 648


def make_consts():
    c = np.zeros((128, CCW), np.float32)
    s = np.arange(128)[:, None]
    t = np.arange(128)[None, :]
    c[:, C_ID:C_ID + 128] = (s == t)
    c[:, C_L:C_L + 128] = (s <= t) & (s // 64 == t // 64)
    c[:, C_B:C_B + 128] = (s // 64 == t // 64)
    c[:, C_E0:C_E0 + 128] = (s < 64)
    c[:, C_E1:C_E1 + 128] = (s >= 64)
    c[:, C_HM0] = (np.arange(128) < 64)
    c[:, C_HM1] = (np.arange(128) >= 64)
    c[:, C_EPS] = EPS
    c[:, C_ONE] = 1.0
    c[:, C_LSK] = 0.5 * np.log(128.0)
    c[:, C_EPS4] = EPS
    return c


class Op:
    __slots__ = ("eng", "fn", "deps", "kind", "sig", "slot", "cnt")

    def __init__(self, eng, fn, deps, kind):
        self.eng, self.fn, self.deps, self.kind = eng, fn, deps, kind
        self.sig = False
        self.slot = None
        self.cnt = 0


class _Rec:
    def __init__(self):
        self.call = None

    def __getattr__(self, name):
        def f(*a, **k):
            self.call = (name, a, k)
            return self
        return f

    def play(self, eng):
        name, a, k = self.call
        return getattr(eng, name)(*a, **k)


class Sched:
    NDS = 6
    RESET = False

    def __init__(self, nc, es):
        self.nc = nc
        self.eng = {"sp": nc.sync, "act": nc.scalar, "pool": nc.gpsimd, "dve": nc.vector, "pe": nc.tensor}
        self.esem = {e: es.enter_context(nc.semaphore("e_" + e)) for e in self.eng}
        self.ecnt = {e: 0 for e in self.eng}
        self.dsem = {q: [es.enter_context(nc.semaphore("d_%s%d" % (q, i))) for i in range(self.NDS)]
                     for q in ("sp", "act", "pool")}
        self.dcnt = {q: [0] * self.NDS for q in self.dsem}
        self.dnext = {q: 0 for q in self.dsem}
        self.csem = es.enter_context(nc.semaphore("cc"))
        self.ccnt = 0
        self.seen = {e: {} for e in self.eng}
        self.reset()

    def reset(self):
        self.ops = []
        self.last_w = {}
        self.readers = {}

    defer = None

    def add(self, eng, fn, r=(), w=(), kind="eng"):
        rec = _Rec()
        fn(rec)
        if self.defer is not None:
            self.defer.append((eng, rec, tuple(r), tuple(w), kind))
            return None
        return self.register(eng, rec, r, w, kind)

    def register(self, eng, rec, r=(), w=(), kind="eng"):
        idx = len(self.ops)
        deps = set()
        for k in r:
            if k in self.last_w:
                deps.add(self.last_w[k])
        for k in w:
            if k in self.last_w:
                deps.add(self.last_w[k])
            deps.update(self.readers.get(k, ()))
        for k in r:
            self.readers.setdefault(k, []).append(idx)
        for k in w:
            self.last_w[k] = idx
            self.readers[k] = []
        deps.discard(idx)
        latest = {}
        pruned = set()
        for d in deps:
            o = self.ops[d]
            if o.kind == "eng":
                if o.eng not in latest or latest[o.eng] < d:
                    latest[o.eng] = d
            else:
                pruned.add(d)
        pruned.update(latest.values())
        deps = pruned
        self.ops.append(Op(eng, rec, deps, kind))
        return idx

    def dma(self, q, out, in_, r=(), w=()):
        return self.add(q, lambda e: e.dma_start(out=out, in_=in_), r, w, kind="dma")

    def _wait(self, eng, sem, val):
        key = id(sem)
        if self.seen[eng].get(key, 0) >= val:
            return
        self.seen[eng][key] = val
        self.eng[eng].wait_ge(sem, val)

    def flush(self, barrier=True):
        ops = self.ops
        for op in ops:
            for d in op.deps:
                dep = ops[d]
                if dep.kind == "eng" and dep.eng == "pe" and op.eng == "pe" and op.kind == "eng":
                    continue
                dep.sig = True
        for op in ops:
            eng = op.eng
            for d in sorted(op.deps):
                dep = ops[d]
                if not dep.sig:
                    continue
                if dep.kind == "eng":
                    self._wait(eng, self.esem[dep.eng], dep.cnt)
                elif dep.kind == "dma":
                    self._wait(eng, self.dsem[dep.eng][dep.slot], dep.cnt)
                else:
                    self._wait(eng, self.csem, dep.cnt)
            if op.kind == "dma":
                s = self.dnext[eng]
                self.dnext[eng] = (s + 1) % self.NDS
                self._wait(eng, self.dsem[eng][s], self.dcnt[eng][s])
                ins = op.fn.play(self.eng[eng])
                self.dcnt[eng][s] += 16
                op.slot, op.cnt = s, self.dcnt[eng][s]
                ins.then_inc(self.dsem[eng][s], 16)
            elif op.kind == "cc":
                self._wait(eng, self.csem, self.ccnt)
                ins = op.fn.play(self.eng[eng])
                self.ccnt += 1
                op.cnt = self.ccnt
                ins.then_inc(self.csem, 1)
            else:
                ins = op.fn.play(self.eng[eng])
                if op.sig:
                    self.ecnt[eng] += 1
                    op.cnt = self.ecnt[eng]
                    ins.then_inc(self.esem[eng], 1)
        self.reset()
        if barrier:
            self.drain()

    def drain(self):
        for q in self.dsem:
            for s in range(self.NDS):
                self._wait(q, self.dsem[q][s], self.dcnt[q][s])
        self.nc.all_engine_barrier()
        if self.RESET:
            for e in self.esem:
                self.nc.gpsimd.sem_clear(self.esem[e])
                self.ecnt[e] = 0
            for q in self.dsem:
                for i in range(self.NDS):
                    self.nc.gpsimd.sem_clear(self.dsem[q][i])
                    self.dcnt[q][i] = 0
            self.seen = {e: {} for e in self.eng}
            self.nc.all_engine_barrier()


def apx(ap, extra):
    return bass.AP(ap.tensor, ap.offset, [list(a) for a in ap.ap] + [list(e) for e in extra])


def bc_mid(ap2, n):
    a = [list(x) for x in ap2.ap]
    return bass.AP(ap2.tensor, ap2.offset, [a[0], [0, n]] + a[1:])


def bc_last(ap, n):
    return apx(ap, [[0, n]])


def build(cfg, dbg=(), stop=None):
    D, T, L, H, KC, NT, SW, NST, TPS, NB = cfg.D, cfg.T, cfg.L, cfg.H, cfg.KC, cfg.NT, cfg.SW, cfg.NST, cfg.TPS, cfg.NB
    TM, NMT, NTM, NSTM = cfg.TM, cfg.NMT, cfg.NTM, cfg.NSTM
    NQK, NIN, KO, OCB = cfg.NQK, cfg.NIN, cfg.KO, cfg.OCB
    WB = min(512, D)
    NSB = WB // 128
    nc = bass.Bass("TRN2", target_bir_lowering=False)

    def din(name, shape, dt=F32):
        return nc.dram_tensor(name, list(shape), dt, kind="ExternalInput").ap()

    def dscr(name, shape, dt=F32):
        kind = "ExternalOutput" if name in dbg else "Internal"
        return nc.dram_tensor(name, list(shape), dt, kind=kind).ap()

    x_in = din("x", [T, D])
    w_in = din("w_in", [L, D, NIN])
    w_out = din("w_out", [L, 2 * D, D])
    w_a = din("w_a", [L, H, 256, 256])
    w_x = din("w_x", [L, H, 256, 256])
    chan = din("chan", [L, 128, cfg.CW])
    rowv = din("rowv", [L, cfg.RW])
    fin_g = din("final_g", [1, D])
    cst_d = din("cst", [128, CCW])
    out_d = nc.dram_tensor("out", [T, D], F32, kind="ExternalOutput").ap()

    xres = dscr("xres", [T, D])
    qk_fm = dscr("qk_fm", [2 * NQK, T], BF16)
    v_tok = dscr("v_tok", [T, D], BF16)
    so_tok = dscr("so_tok", [T, D], BF16)
    sza_tok = dscr("sza_tok", [T, D], BF16)
    if_d = dscr("if_d", [128, NT * 2 * H])
    xc_fm = dscr("xc_fm", [D, T], BF16)
    szb_fm = dscr("szb_fm", [D, T], BF16)
    y_fm = dscr("y_fm", [2 * D, T], BF16)
    NWB = (NIN - 2 * H) // WB
    wbd = dscr("wbd", [NWB, 128, KC * WB], BF16)
    wod = dscr("wod", [D // OCB, 128, KO * OCB], BF16)

    with contextlib.ExitStack() as es0:
        S = Sched(nc, es0)
        uid = [0]

        def sb(es, name, shape, dt=F32):
            uid[0] += 1
            return es.enter_context(nc.sbuf_tensor("s%d_%s" % (uid[0], name), list(shape), dt))

        def ps(es, name, shape, dt=F32):
            uid[0] += 1
            return es.enter_context(nc.psum_tensor("p%d_%s" % (uid[0], name), list(shape), dt))

        cst = sb(es0, "cst", [128, CCW])
        idb = sb(es0, "idb", [128, 128], BF16)
        mLb = sb(es0, "mLb", [128, 128], F32)
        chs = sb(es0, "chs", [128, cfg.CW])
        S.dma("sp", cst[:], cst_d, w=["cst"])
        S.add("dve", lambda e: e.tensor_copy(out=idb[:], in_=cst[:, C_ID:C_ID + 128]), r=["cst"], w=["idb"])
        S.add("dve", lambda e: e.tensor_copy(out=mLb[:], in_=cst[:, C_L:C_L + 128]), r=["cst"], w=["mLb"])
        S.flush()

        def cc(col):
            return cst[:, col:col + 1]

        for l in range(L):
            xsrc = x_in if l == 0 else xres
            S.dma("sp", chs[:], chan[l], w=["chs"])
            S.flush()

            blocks = []
            for c0 in range(cfg.c_q, cfg.c_v, WB):
                blocks.append(("qk", c0))
            for c0 in range(cfg.c_za, cfg.c_i, WB):
                blocks.append(("za", c0))
            for c0 in range(cfg.c_zb, NIN, WB):
                blocks.append(("zb", c0))
            for c0 in range(cfg.c_o, cfg.c_za, WB):
                blocks.append(("o", c0))
            for c0 in range(cfg.c_v, cfg.c_o, WB):
                blocks.append(("v", c0))
            for c0 in range(cfg.c_xb, cfg.c_zb, WB):
                blocks.append(("xb", c0))
            blocks.append(("if", cfg.c_i))
            NBLK = len(blocks)
            assert NBLK - 1 == NWB

            with contextlib.ExitStack() as es:
                KW = max(KC, KO)
                fst = [sb(es, "fst%d" % i, [128, KW // 2, WB]) for i in range(2)]
                bst = [sb(es, "bst%d" % i, [128, KW // 2, WB], BF16) for i in range(2)]
                jobs = []
                for bi in range(NWB):
                    c0 = blocks[bi][1]
                    for hf in range(2):
                        k0 = hf * (KC // 2)
                        jobs.append((w_in[l][k0 * 128:(k0 + KC // 2) * 128, c0:c0 + WB], KC // 2,
                                     wbd[bi][:, k0 * WB:(k0 + KC // 2) * WB]))
                for cb in range(D // OCB):
                    for hf in range(KO // (KW // 2)):
                        k0 = hf * (KW // 2)
                        jobs.append((w_out[l][k0 * 128:(k0 + KW // 2) * 128, cb * OCB:(cb + 1) * OCB], KW // 2,
                                     wod[cb][:, k0 * OCB:(k0 + KW // 2) * OCB]))
                for ji, (src, nk, dst) in enumerate(jobs):
                    fb, bb = fst[ji % 2], bst[ji % 2]
                    kf, kb_ = "fst%d" % (ji % 2), "bst%d" % (ji % 2)
                    q2 = max(1, nk // 2)
                    for k0 in range(0, nk, q2):
                        S.dma("act" if ji % 2 else "sp", fb[:, k0:k0 + q2, :], src[k0 * 128:(k0 + q2) * 128, :].rearrange("(kc p) c -> p kc c", p=128),
                              w=[(kf, k0)])
                    fkeys = [(kf, k0) for k0 in range(0, nk, q2)]
                    if ji % 2 == 0:
                        S.add("dve", lambda e: e.tensor_copy(out=bb[:, 0:nk, :], in_=fb[:, 0:nk, :]), r=fkeys, w=[kb_])
                    else:
                        S.add("act", lambda e: e.copy(out=bb[:, 0:nk, :], in_=fb[:, 0:nk, :]), r=fkeys, w=[kb_])
                    seg = min(nk, max(1, 2048 // WB))
                    for k0 in range(0, nk, seg):
                        S.dma("sp" if ji % 2 else "act", dst[:, k0 * WB:(k0 + seg) * WB], bb[:, k0:k0 + seg, :].rearrange("p k c -> p (k c)"),
                              r=[kb_], w=["wbd"])
                S.flush()

            with contextlib.ExitStack() as es:
                uT = sb(es, "uT", [128, KC, TM], BF16)
                grep = sb(es, "grep", [128, D])
                xt = [sb(es, "xt%d" % i, [128, D]) for i in range(2)]
                ub = [sb(es, "ub%d" % i, [128, D], BF16) for i in range(2)]
                junk = sb(es, "junk", [128, D], BF16)
                sst = sb(es, "sst", [128, 8])
                wbf = [sb(es, "wbf%d" % i, [128, KC, WB], BF16) for i in range(2)]
                wif = sb(es, "wif", [128, KC, 2 * H], BF16)
                pre = [sb(es, "pre%d" % i, [128, 4 + TM]) for i in range(2)]
                cacc = [sb(es, "cacc%d" % i, [128, TM]) for i in range(1)]
                stg = [sb(es, "stg%d" % i, [128, TM], BF16) for i in range(2)]
                tstg = [sb(es, "tstg%d" % i, [128, NTM, WB], BF16) for i in range(1)]
                ifs = sb(es, "ifs", [128, NTM, 2 * H])
                hsv = sb(es, "hsv", [128, cfg.nqkb + NB, 4])
                psA = [ps(es, "psA%d" % i, [128, 512]) for i in range(5)]
                pst = [ps(es, "pst%d" % i, [128, 4, 128], BF16) for i in range(2)]

                S.dma("act", grep[:], bass.AP(rowv.tensor, l * cfg.RW + cfg.r_ng, [[0, 128], [1, D]]), w=["grep"])
                S.add("pool", lambda e: e.memset(hsv[:], 0.0), w=["hsv"])

                def norm_tile(gn, n):
                    bi = gn % 2
                    xb_, ub_ = xt[bi], ub[bi]
                    kx, ku = "xt%d" % bi, "ub%d" % bi
                    S.dma("sp", xb_[:], xsrc[gn * 128:(gn + 1) * 128, :], r=["xres"], w=[kx])
                    c0 = (bi * 4)
                    S.add("act", lambda e: e.activation(out=junk[:], in_=xb_[:], func=AF.Square,
                                                        accum_out=sst[:, c0:c0 + 1]), r=[kx], w=["junk", ("sst", c0)])
                    S.add("act", lambda e: e.activation(out=sst[:, c0 + 1:c0 + 2], in_=sst[:, c0:c0 + 1], func=AF.Ln,
                                                        scale=1.0 / D, bias=cc(C_EPS)), r=[("sst", c0), "cst"], w=[("sst", c0 + 1)])
                    S.add("act", lambda e: e.activation(out=sst[:, c0 + 2:c0 + 3], in_=sst[:, c0 + 1:c0 + 2], func=AF.Exp,
                                                        scale=-0.5), r=[("sst", c0 + 1)], w=[("sst", c0 + 2)])
                    S.add("dve", lambda e: e.scalar_tensor_tensor(out=ub_[:], in0=xb_[:], scalar=sst[:, c0 + 2:c0 + 3],
                                                                  in1=grep[:], op0=ALU.mult, op1=ALU.mult),
                          r=[kx, ("sst", c0 + 2), "grep"], w=[ku])
                    for g in range(KC // 4):
                        pt = pst[g % 2]
                        for i in range(4):
                            kc = g * 4 + i
                            S.add("pe", lambda e: e.transpose(pt[:, i, :], ub_[:, kc * 128:(kc + 1) * 128], idb[:]),
                                  r=[ku, "idb"], w=["pst%d" % (g % 2)])
                        dst, kd = uT[:, g * 4:(g + 1) * 4, n * 128:(n + 1) * 128], ("uT", n)
                        if g % 2 == 0:
                            S.add("act", lambda e: e.copy(out=dst, in_=pt[:]), r=["pst%d" % (g % 2)], w=[kd])
                        else:
                            S.add("dve", lambda e: e.tensor_copy(out=dst, in_=pt[:]), r=["pst%d" % (g % 2)], w=[kd])

                wstep = max(1, KC // 4)
                wseq = [0]

                def load_w(bi):
                    kind, c0 = blocks[bi]
                    if kind == "if":
                        for k0 in range(0, KC, wstep):
                            S.dma("pool", wif[:, k0:k0 + wstep, :], w_in[l][k0 * 128:(k0 + wstep) * 128, c0:c0 + 2 * H].rearrange("(kc p) c -> p kc c", p=128), w=["wif"])
                        return None
                    slot = wseq[0] % 2
                    wseq[0] += 1
                    wb = wbf[slot]
                    for k0 in range(0, KC, wstep):
                        S.dma("pool", wb[:, k0:k0 + wstep, :].rearrange("p k c -> p (k c)"),
                              wbd[bi][:, k0 * WB:(k0 + wstep) * WB], r=["wbd"], w=[("wbf", slot, k0)])
                    return slot

                psi = [0]
                fmi = [0]

                def next_ps():
                    i = psi[0] % 5
                    psi[0] += 1
                    return psA[i], "psA%d" % i

                def fm_block(bi, slot, mt):
                    kind, c0 = blocks[bi]
                    wb = wbf[slot]
                    wkeys = [("wbf", slot, k0) for k0 in range(0, KC, wstep)]
                    t0 = mt * TM
                    for sbk in range(NSB):
                        fi = fmi[0] % 2
                        fmi[0] += 1
                        pr, ca, sg = pre[fi], cacc[0], stg[fi]
                        kpr, kca, ksg = "pre%d" % fi, "cacc0", "stg%d" % fi
                        conv = kind in ("qk", "xb")
                        if conv:
                            if kind == "qk":
                                blk = (c0 - cfg.c_q) // 128 + sbk
                                hb_ = blk
                                wcol = cfg.p_qkc + blk * 4
                                dst_rows = qk_fm[blk * 128:(blk + 1) * 128, t0:t0 + TM]
                            else:
                                blk = (c0 - cfg.c_xb) // 128 + sbk
                                hb_ = cfg.nqkb + blk
                                wcol = cfg.p_lcw + blk * 4
                                dst_rows = xc_fm[blk * 128:(blk + 1) * 128, t0:t0 + TM]
                            S.add("dve", lambda e: e.tensor_copy(out=pr[:, 0:3], in_=hsv[:, hb_, 0:3]), r=["hsv", ("hsv", hb_)], w=[(kpr, "h")])
                        else:
                            blk = (c0 - cfg.c_zb) // 128 + sbk
                            dst_rows = szb_fm[blk * 128:(blk + 1) * 128, t0:t0 + TM]
                        for st in range(NSTM):
                            pA, kA = next_ps()
                            for kc in range(KC):
                                S.add("pe", lambda e: e.matmul(
                                    pA[:, 0:SW], lhsT=wb[:, kc, sbk * 128:(sbk + 1) * 128], rhs=uT[:, kc, st * SW:(st + 1) * SW],
                                    start=(kc == 0), stop=(kc == KC - 1)),
                                    r=wkeys + [("uT", st * TPS + i) for i in range(TPS)], w=[kA])
                            if conv:
                                S.add("act", lambda e: e.copy(out=pr[:, 3 + st * SW:3 + (st + 1) * SW], in_=pA[:, 0:SW]),
                                      r=[kA], w=[(kpr, st)])
                            else:
                                S.add("act", lambda e: e.activation(out=sg[:, st * SW:(st + 1) * SW], in_=pA[:, 0:SW], func=AF.Silu),
                                      r=[kA], w=[ksg])
                        if conv:
                            prk = [(kpr, "h")] + [(kpr, st) for st in range(NSTM)]
                            S.add("act", lambda e: e.copy(out=hsv[:, hb_, 0:3], in_=pr[:, TM:TM + 3]), r=prk, w=[("hsv", hb_)])
                            if kind == "xb":
                                bcol = cfg.p_lcb + blk
                                S.add("dve", lambda e: e.tensor_scalar(out=ca[:], in0=pr[:, 0:TM], scalar1=chs[:, wcol:wcol + 1],
                                                                       scalar2=chs[:, bcol:bcol + 1], op0=ALU.mult, op1=ALU.add),
                                      r=prk + ["chs"], w=[kca])
                            else:
                                S.add("dve", lambda e: e.tensor_scalar(out=ca[:], in0=pr[:, 0:TM], scalar1=chs[:, wcol:wcol + 1],
                                                                       scalar2=0.0, op0=ALU.mult, op1=ALU.add),
                                      r=prk + ["chs"], w=[kca])
                            for j in (1, 2):
                                S.add("dve", lambda e: e.scalar_tensor_tensor(out=ca[:], in0=pr[:, j:j + TM], scalar=chs[:, wcol + j:wcol + j + 1],
                                                                              in1=ca[:], op0=ALU.mult, op1=ALU.add),
                                      r=prk + ["chs", kca], w=[kca])
                            if kind == "xb":
                                S.add("dve", lambda e: e.scalar_tensor_tensor(out=sg[:], in0=pr[:, 3:3 + TM], scalar=chs[:, wcol + 3:wcol + 4],
                                                                              in1=ca[:], op0=ALU.mult, op1=ALU.add),
                                      r=prk + ["chs", kca], w=[ksg])
                            else:
                                S.add("dve", lambda e: e.scalar_tensor_tensor(out=ca[:], in0=pr[:, 3:3 + TM], scalar=chs[:, wcol + 3:wcol + 4],
                                                                              in1=ca[:], op0=ALU.mult, op1=ALU.add),
                                      r=prk + ["chs", kca], w=[kca])
                                S.add("act", lambda e: e.activation(out=sg[:], in_=ca[:], func=AF.Silu), r=[kca], w=[ksg])
                        S.dma("sp", dst_rows, sg[:], r=[ksg], w=["fm_out"])

                def tm_block(bi, slot, mt):
                    kind, c0 = blocks[bi]
                    wb = wbf[slot]
                    wkeys = [("wbf", slot, k0) for k0 in range(0, KC, wstep)]
                    tg, ktg = tstg[0], "tstg0"
                    for n in range(NTM):
                        pA, kA = next_ps()
                        for kc in range(KC):
                            S.add("pe", lambda e: e.matmul(
                                pA[:, 0:WB], lhsT=uT[:, kc, n * 128:(n + 1) * 128], rhs=wb[:, kc, :],
                                start=(kc == 0), stop=(kc == KC - 1)), r=wkeys + [("uT", n)], w=[kA])
                        if kind == "v":
                            if n % 2 == 0:
                                S.add("dve", lambda e: e.tensor_copy(out=tg[:, n, :], in_=pA[:, 0:WB]), r=[kA], w=[(ktg, n)])
                            else:
                                S.add("act", lambda e: e.copy(out=tg[:, n, :], in_=pA[:, 0:WB]), r=[kA], w=[(ktg, n)])
                        else:
                            fn = AF.Sigmoid if kind == "o" else AF.Silu
                            S.add("act", lambda e: e.activation(out=tg[:, n, :], in_=pA[:, 0:WB], func=fn), r=[kA], w=[(ktg, n)])
                    base = {"v": (v_tok, cfg.c_v), "o": (so_tok, cfg.c_o), "za": (sza_tok, cfg.c_za)}[kind]
                    cc0 = c0 - base[1]
                    q4 = max(1, NTM // 4)
                    for n0 in range(0, NTM, q4):
                        r0 = mt * TM + n0 * 128
                        S.dma("sp", base[0][r0:r0 + q4 * 128, cc0:cc0 + WB].rearrange("(n p) c -> p n c", p=128),
                              tg[:, n0:n0 + q4, :], r=[(ktg, n) for n in range(n0, n0 + q4)], w=["tm_out"])

                def if_block(mt):
                    for n in range(NTM):
                        pA, kA = next_ps()
                        for kc in range(KC):
                            S.add("pe", lambda e: e.matmul(
                                pA[:, 0:2 * H], lhsT=uT[:, kc, n * 128:(n + 1) * 128], rhs=wif[:, kc, :],
                                start=(kc == 0), stop=(kc == KC - 1)), r=["wif", ("uT", n)], w=[kA])
                        S.add("dve", lambda e: e.tensor_copy(out=ifs[:, n, :], in_=pA[:, 0:2 * H]), r=[kA], w=["ifs"])
                    S.dma("sp", if_d[:, mt * NTM * 2 * H:(mt + 1) * NTM * 2 * H], ifs[:].rearrange("p n c -> p (n c)"), r=["ifs"], w=["if_d"])

                seq = [(mt, bi) for mt in range(NMT) for bi in range(NBLK)]
                slots = {}
                slots[0] = load_w(seq[0][1])
                for si, (mt, bi) in enumerate(seq):
                    if bi == 0:
                        for n in range(NTM):
                            norm_tile(mt * NTM + n, n)
                    if si + 1 < len(seq):
                        slots[si + 1] = load_w(seq[si + 1][1])
                    kind = blocks[bi][0]
                    if kind in ("qk", "xb", "zb"):
                        fm_block(bi, slots[si], mt)
                    elif kind == "if":
                        if_block(mt)
                    else:
                        tm_block(bi, slots[si], mt)
                S.flush()
            if stop == "P1":
                return nc

            with contextlib.ExitStack() as es:
                ifs = sb(es, "ifs2", [128, NT, 2 * H])
                ibf = sb(es, "ibf", [128, 2 * H])
                zf = sb(es, "zf", [128, NT, H])
                lp = sb(es, "lp", [128, NT, H])
                t1 = sb(es, "t1", [128, NT, H])
                t2 = sb(es, "t2", [128, NT, H])
                ga = sb(es, "ga", [128, NT, H])
                gwk = sb(es, "gwk", [128, NT, H])
                gwk0 = sb(es, "gwk0", [128, NT, H])
                gwk1 = sb(es, "gwk1", [128, NT, H])
                ge = sb(es, "ge", [128, NT, H])
                gd0 = sb(es, "gd0", [128, NT, H])
                gd1 = sb(es, "gd1", [128, NT, H])
                Cf = sb(es, "Cf", [128, H, 257])
                Cb = sb(es, "Cb", [128, H, 257], BF16)
                hgrep = sb(es, "hgrep", [128, D])
                qs = [sb(es, "qs%d" % i, [128, H, SW], BF16) for i in range(2)]
                ks = [sb(es, "ks%d" % i, [128, H, SW], BF16) for i in range(2)]
                va = [sb(es, "va%d" % i, [128, H, 257], BF16) for i in range(2)]
                sos = [sb(es, "sos%d" % i, [128, D], BF16) for i in range(2)]
                szs = [sb(es, "szs%d" % i, [128, D], BF16) for i in range(2)]
                qz0 = [sb(es, "qz0_%d" % i, [128, H, 128], BF16) for i in range(2)]
                qz1 = [sb(es, "qz1_%d" % i, [128, H, 128], BF16) for i in range(2)]
                kw0 = [sb(es, "kw0_%d" % i, [128, 128], BF16) for i in range(4)]
                kw1 = [sb(es, "kw1_%d" % i, [128, 128], BF16) for i in range(4)]
                scT = [sb(es, "scT%d" % i, [128, 128], BF16) for i in range(4)]
                dsm = sb(es, "dsm", [128, 16])
                hall = [sb(es, "hall%d" % i, [128, H, 256]) for i in range(1)]
                sqb = sb(es, "sqb", [128, H, 256])
                st8 = sb(es, "st8", [128, 8, H])
                ybf = sb(es, "ybf", [128, D], BF16)
                yts = [sb(es, "yts%d" % i, [128, KC, 128], BF16) for i in range(2)]
                wab = sb(es, "wab", [128, H, 2, 256], BF16)
                wxb = sb(es, "wxb", [128, H, 2, 256], BF16)
                ccoef = sb(es, "ccoef", [128, 4, NB])
                hcar = sb(es, "hcar", [128, NB])
                xcs = [sb(es, "xcs%d" % i, [128, 2, SW], BF16) for i in range(2)]
                zbs = [sb(es, "zbs%d" % i, [128, 2, SW], BF16) for i in range(2)]
                lt = {nm: [sb(es, "%s%d" % (nm, i), [128, SW]) for i in range(2)]
                      for nm in ("er", "ei", "la", "a2", "hh")}
                yb = [sb(es, "yb%d" % i, [128, SW], BF16) for i in range(2)]

                p_acc = [ps(es, "pacc%d" % i, [128, 512]) for i in range(4)]
                p_dc = [ps(es, "pdc%d" % i, [128, 512]) for i in range(2)]
                p_ss = ps(es, "pss", [128, 4, 128])
                p_kt = ps(es, "pkt", [128, 4, 128], BF16)

                S.dma("sp", ifs[:].rearrange("p n c -> p (n c)"), if_d, r=["if_d"], w=["ifs"])
                S.dma("act", ibf[:], bass.AP(rowv.tensor, l * cfg.RW + cfg.r_ib, [[0, 128], [1, 2 * H]]), w=["ibf"])
                S.dma("act", hgrep[:], bass.AP(rowv.tensor, l * cfg.RW + cfg.r_hg, [[0, 128], [1, D]]), w=["hgrep"])
                for h0 in range(0, H, 2):
                    S.dma("pool", wab[:, h0:h0 + 2], w_a[l][h0:h0 + 2].rearrange("h (ib p) j -> p h ib j", p=128), w=["wab"])
                    S.dma("pool", wxb[:, h0:h0 + 2], w_x[l][h0:h0 + 2].rearrange("h (ib p) j -> p h ib j", p=128), w=["wxb"])
                S.add("dve", lambda e: e.tensor_tensor(out=ifs[:, :, 0:H], in0=ifs[:, :, 0:H], in1=bc_mid(ibf[:, 0:H], NT), op=ALU.add),
                      r=["ifs", "ibf"], w=["ifs"])
                S.add("dve", lambda e: e.tensor_tensor(out=zf[:], in0=ifs[:, :, H:2 * H], in1=bc_mid(ibf[:, H:2 * H], NT), op=ALU.add),
                      r=["ifs", "ibf"], w=["zf"])
                S.add("act", lambda e: e.activation(out=zf[:], in_=zf[:], func=AF.Exp, scale=-1.0), r=["zf"], w=["zf"])
                S.add("act", lambda e: e.activation(out=lp[:], in_=zf[:], func=AF.Ln, scale=1.0, bias=cc(C_ONE)), r=["zf", "cst"], w=["lp"])
                GN = max(1, 512 // H)
                for g0 in range(0, NT, GN):
                    g1 = min(NT, g0 + GN)
                    NH = (g1 - g0) * H
                    lp2 = lp[:, g0:g1, :].rearrange("p n h -> p (n h)")
                    pg, pb_, pd0, pd1 = p_acc[0], p_acc[1], p_acc[2], p_acc[3]
                    S.add("pe", lambda e: e.matmul(pg[:, 0:NH], lhsT=cst[:, C_L:C_L + 128], rhs=lp2, start=True, stop=True), r=["lp", "cst"], w=["pacc0"])
                    S.add("pe", lambda e: e.matmul(pb_[:, 0:NH], lhsT=cst[:, C_B:C_B + 128], rhs=lp2, start=True, stop=True), r=["lp", "cst"], w=["pacc1"])
                    S.add("pe", lambda e: e.matmul(pd0[:, 0:NH], lhsT=cst[:, C_E0:C_E0 + 128], rhs=lp2, start=True, stop=True), r=["lp", "cst"], w=["pacc2"])
                    S.add("pe", lambda e: e.matmul(pd1[:, 0:NH], lhsT=cst[:, C_E1:C_E1 + 128], rhs=lp2, start=True, stop=True), r=["lp", "cst"], w=["pacc3"])

                    def v3(p):
                        return p[:, 0:NH].rearrange("p (n h) -> p n h", h=H)
                    gs = slice(g0, g1)
                    S.add("dve", lambda e: e.tensor_tensor(out=t1[:, gs, :], in0=ifs[:, gs, 0:H], in1=v3(pg), op=ALU.add), r=["ifs", "pacc0"], w=["t1"])
                    S.add("act", lambda e: e.activation(out=ga[:, gs, :], in_=t1[:, gs, :], func=AF.Exp), r=["t1"], w=["ga"])
                    S.add("dve", lambda e: e.tensor_tensor(out=t2[:, gs, :], in0=t1[:, gs, :], in1=v3(pb_), op=ALU.subtract), r=["t1", "pacc1"], w=["t2"])
                    S.add("act", lambda e: e.activation(out=gwk[:, gs, :], in_=t2[:, gs, :], func=AF.Exp), r=["t2"], w=["gwk"])
                    S.add("act", lambda e: e.activation(out=ge[:, gs, :], in_=v3(pg), func=AF.Exp, scale=1.0, bias=cc(C_LSK)), r=["pacc0", "cst"], w=["ge"])
                    S.add("act", lambda e: e.activation(out=gd0[:, gs, :], in_=v3(pd0), func=AF.Exp, scale=-1.0), r=["pacc2"], w=["gd0"])
                    S.add("act", lambda e: e.activation(out=gd1[:, gs, :], in_=v3(pd1), func=AF.Exp, scale=-1.0), r=["pacc3"], w=["gd1"])
                S.add("dve", lambda e: e.tensor_scalar(out=gwk0[:], in0=gwk[:], scalar1=cc(C_HM0), scalar2=0.0, op0=ALU.mult, op1=ALU.add), r=["gwk", "cst"], w=["gwk0"])
                S.add("dve", lambda e: e.tensor_scalar(out=gwk1[:], in0=gwk[:], scalar1=cc(C_HM1), scalar2=0.0, op0=ALU.mult, op1=ALU.add), r=["gwk", "cst"], w=["gwk1"])
                lam = chs[:, cfg.p_lam:cfg.p_lam + NB]
                S.add("act", lambda e: e.activation(out=ccoef[:, 0, :], in_=lam, func=AF.Exp, scale=-1.0), r=["chs"], w=["ccoef"])
                S.add("act", lambda e: e.activation(out=ccoef[:, 0, :], in_=ccoef[:, 0, :], func=AF.Ln, scale=1.0, bias=cc(C_ONE)), r=["ccoef", "cst"], w=["ccoef"])
                S.add("dve", lambda e: e.tensor_scalar(out=ccoef[:, 1, :], in0=ccoef[:, 0, :], scalar1=-2.0 * LRU_C, scalar2=0.0, op0=ALU.mult, op1=ALU.add), r=["ccoef"], w=["ccoef"])
                S.add("dve", lambda e: e.tensor_scalar(out=ccoef[:, 0, :], in0=ccoef[:, 0, :], scalar1=-LRU_C, scalar2=0.0, op0=ALU.mult, op1=ALU.add), r=["ccoef"], w=["ccoef"])
                S.add("dve", lambda e: e.tensor_scalar(out=ccoef[:, 2, :], in0=chs[:, cfg.p_ba:cfg.p_ba + NB], scalar1=-1.0, scalar2=0.0, op0=ALU.mult, op1=ALU.add),
                      r=["chs"], w=["ccoef2"])
                S.add("dve", lambda e: e.tensor_scalar(out=ccoef[:, 3, :], in0=chs[:, cfg.p_bx:cfg.p_bx + NB], scalar1=-1.0, scalar2=0.0, op0=ALU.mult, op1=ALU.add),
                      r=["chs"], w=["ccoef2"])
                S.add("pool", lambda e: e.memset(Cf[:], 0.0), w=["Cf"] + [("Cf", h) for h in range(H)])
                S.add("pool", lambda e: e.memset(Cb[:], 0.0), w=[("Cb", h) for h in range(H)])
                S.add("pool", lambda e: e.memset(hcar[:], 0.0), w=["hcar"] + [("hcar", b) for b in range(NB)])
                for i in range(2):
                    S.add("pool", lambda e: e.memset(qz0[i][:], 0.0), w=["qz0_%d" % i])
                    S.add("pool", lambda e: e.memset(qz1[i][:], 0.0), w=["qz1_%d" % i])
                for i in range(2):
                    S.add("pool", lambda e: e.memset(va[i][:, :, 256:257], 1.0), w=["va%d" % i])

                cnt = {"kw": 0, "sc": 0, "acc": 0, "dc": 0}

                def mlstm_tile(n):
                    st, ti = n // TPS, n % TPS
                    qb, kb = qs[st % 2], ks[st % 2]
                    kq, kk = "qs%d" % (st % 2), "ks%d" % (st % 2)
                    if ti == 0:
                        S.dma("sp", kb[:], qk_fm[NQK:2 * NQK, st * SW:(st + 1) * SW].rearrange("(h p) t -> p h t", p=128), r=["fm_out"], w=[kk])
                        S.dma("sp", qb[:], qk_fm[0:NQK, st * SW:(st + 1) * SW].rearrange("(h p) t -> p h t", p=128), r=["fm_out"], w=[kq])
                    vb, kv = va[n % 2], "va%d" % (n % 2)
                    S.dma("act", vb[:, :, 0:256], v_tok[n * 128:(n + 1) * 128, :].rearrange("p (h v) -> p h v", v=256), r=["tm_out"], w=[kv])
                    tc0, tc1 = ti * 128, (ti + 1) * 128
                    sob, szb_ = sos[n % 2], szs[n % 2]
                    kso, ksz = "sos%d" % (n % 2), "szs%d" % (n % 2)
                    S.dma("act", sob[:], so_tok[n * 128:(n + 1) * 128, :], r=["tm_out"], w=[kso])
                    S.dma("act", szb_[:], sza_tok[n * 128:(n + 1) * 128, :], r=["tm_out"], w=[ksz])
                    z0, z1 = qz0[n % 2], qz1[n % 2]
                    kz0, kz1 = "qz0_%d" % (n % 2), "qz1_%d" % (n % 2)
                    S.add("pool", lambda e: e.tensor_copy(out=z0[:, :, 0:64], in_=qb[:, :, tc0:tc0 + 64]), r=[kq], w=[kz0])
                    S.add("pool", lambda e: e.tensor_copy(out=z1[:, :, 64:128], in_=qb[:, :, tc0 + 64:tc0 + 128]), r=[kq], w=[kz1])
                    hl, khl = hall[0], "hall0"
                    for h0 in range(0, H, 2):
                        grp = list(range(h0, min(H, h0 + 2)))
                        info = {}
                        for h in grp:
                            ki = cnt["kw"] % 4
                            cnt["kw"] += 1
                            k0t, k1t = kw0[ki], kw1[ki]
                            kk0, kk1 = "kw0_%d" % ki, "kw1_%d" % ki
                            pk = ("pkt", ki)
                            S.add("pe", lambda e: e.transpose(p_kt[:, ki, :], kb[:, h, tc0:tc1], idb[:]), r=[kk, "idb"], w=[pk])
                            S.add("act", lambda e: e.activation(out=k0t[:], in_=p_kt[:, ki, :], func=AF.Copy, scale=gwk0[:, n, h:h + 1]),
                                  r=[pk, "gwk0"], w=[kk0])
                            S.add("act", lambda e: e.activation(out=k1t[:], in_=p_kt[:, ki, :], func=AF.Copy, scale=gwk1[:, n, h:h + 1]),
                                  r=[pk, "gwk1"], w=[kk1])
                            si = cnt["sc"] % 4
                            cnt["sc"] += 1
                            sct, ksc, pss_k = scT[si], "scT%d" % si, ("pss", si)
                            S.add("pe", lambda e: e.matmul(p_ss[:, si, :], lhsT=kb[:, h, tc0:tc1], rhs=qb[:, h, tc0:tc1], start=True, stop=True),
                                  r=[kk, kq], w=[pss_k])
                            S.add("dve", lambda e: e.scalar_tensor_tensor(out=sct[:], in0=p_ss[:, si, :], scalar=ga[:, n, h:h + 1],
                                                                          in1=mLb[:], op0=ALU.mult, op1=ALU.mult),
                                  r=[pss_k, "ga", "mLb"], w=[ksc])
                            ai = cnt["acc"] % 4
                            cnt["acc"] += 1
                            pa, kpa = p_acc[ai], "pacc%d" % ai
                            S.add("pe", lambda e: e.matmul(pa[:, 0:257], lhsT=sct[:], rhs=vb[:, h, :], start=True, stop=False),
                                  r=[ksc, kv], w=[kpa])
                            S.add("pe", lambda e: e.matmul(pa[:, 0:257], lhsT=z0[:, h, :], rhs=Cb[:, h, :], start=False, stop=False),
                                  r=[kz0, ("Cb", h)], w=[kpa])
                            info[h] = (k0t, k1t, kk0, kk1, pa, kpa)
                            di = cnt["dc"] % 2
                            cnt["dc"] += 1
                            pd, kpd = p_dc[di], "pdc%d" % di
                            S.add("pe", lambda e: e.matmul(pd[:, 0:257], lhsT=k0t[:], rhs=vb[:, h, :], start=True, stop=True),
                                  r=[kk0, kv], w=[kpd])
                            S.add("dve", lambda e: e.scalar_tensor_tensor(out=Cf[:, h, :], in0=Cf[:, h, :], scalar=gd0[:, n, h:h + 1],
                                                                          in1=pd[:, 0:257], op0=ALU.mult, op1=ALU.add),
                                  r=[("Cf", h), "gd0", kpd], w=[("Cf", h)])
                            S.add("act", lambda e: e.copy(out=Cb[:, h, :], in_=Cf[:, h, :]), r=[("Cf", h)], w=[("Cb", h)])
                        for h in grp:
                            k0t, k1t, kk0, kk1, pa, kpa = info[h]
                            S.add("pe", lambda e: e.matmul(pa[:, 0:257], lhsT=z1[:, h, :], rhs=Cb[:, h, :], start=False, stop=True),
                                  r=[kz1, ("Cb", h)], w=[kpa])
                            di = cnt["dc"] % 2
                            cnt["dc"] += 1
                            pd, kpd = p_dc[di], "pdc%d" % di
                            S.add("pe", lambda e: e.matmul(pd[:, 0:257], lhsT=k1t[:], rhs=vb[:, h, :], start=True, stop=True),
                                  r=[kk1, kv], w=[kpd])
                            S.add("dve", lambda e: e.scalar_tensor_tensor(out=Cf[:, h, :], in0=Cf[:, h, :], scalar=gd1[:, n, h:h + 1],
                                                                          in1=pd[:, 0:257], op0=ALU.mult, op1=ALU.add),
                                  r=[("Cf", h), "gd1", kpd], w=[("Cf", h)])
                            S.add("act", lambda e: e.copy(out=Cb[:, h, :], in_=Cf[:, h, :]), r=[("Cf", h)], w=[("Cb", h)])
                            dcol = (h % 8) * 2
                            S.add("act", lambda e: e.activation(out=dsm[:, dcol:dcol + 1], in_=pa[:, 256:257], func=AF.Abs),
                                  r=[kpa], w=[("dsm", dcol)])
                            S.add("dve", lambda e: e.tensor_tensor(out=dsm[:, dcol:dcol + 1], in0=dsm[:, dcol:dcol + 1], in1=ge[:, n, h:h + 1], op=ALU.max),
                                  r=[("dsm", dcol), "ge"], w=[("dsm", dcol)])
                            S.add("dve", lambda e: e.reciprocal(out=dsm[:, dcol + 1:dcol + 2], in_=dsm[:, dcol:dcol + 1]),
                                  r=[("dsm", dcol)], w=[("dsm", dcol + 1)])
                            S.add("act", lambda e: e.activation(out=hl[:, h, :], in_=pa[:, 0:256], func=AF.Copy, scale=dsm[:, dcol + 1:dcol + 2]),
                                  r=[kpa, ("dsm", dcol + 1)], w=[(khl, h)])
                    hk = [(khl, h) for h in range(H)]
                    hl2 = hl[:].rearrange("p h v -> p (h v)")
                    S.add("dve", lambda e: e.tensor_tensor(out=hl2, in0=hl2, in1=sob[:], op=ALU.mult), r=hk + [kso], w=[khl])
                    for h in range(H):
                        S.add("act", lambda e: e.activation(out=sqb[:, h, :], in_=hl[:, h, :], func=AF.Identity, accum_out=st8[:, 0, h:h + 1]),
                              r=[khl], w=["sqb", ("st8", 0)])
                        S.add("act", lambda e: e.activation(out=sqb[:, h, :], in_=hl[:, h, :], func=AF.Square, accum_out=st8[:, 1, h:h + 1]),
                              r=[khl], w=["sqb", ("st8", 1)])
                    S.add("dve", lambda e: e.tensor_scalar(out=st8[:, 2, :], in0=st8[:, 0, :], scalar1=1.0 / 256, scalar2=0.0, op0=ALU.mult, op1=ALU.add),
                          r=[("st8", 0)], w=[("st8", 2)])
                    S.add("dve", lambda e: e.tensor_tensor(out=st8[:, 3, :], in0=st8[:, 2, :], in1=st8[:, 2, :], op=ALU.mult), r=[("st8", 2)], w=[("st8", 3)])
                    S.add("dve", lambda e: e.scalar_tensor_tensor(out=st8[:, 4, :], in0=st8[:, 1, :], scalar=1.0 / 256, in1=st8[:, 3, :], op0=ALU.mult, op1=ALU.subtract),
                          r=[("st8", 1), ("st8", 3)], w=[("st8", 4)])
                    S.add("act", lambda e: e.activation(out=st8[:, 5, :], in_=st8[:, 4, :], func=AF.Ln, scale=1.0, bias=cc(C_EPS)), r=[("st8", 4), "cst"], w=[("st8", 5)])
                    S.add("act", lambda e: e.activation(out=st8[:, 6, :], in_=st8[:, 5, :], func=AF.Exp, scale=-0.5), r=[("st8", 5)], w=[("st8", 6)])
                    S.add("dve", lambda e: e.scalar_tensor_tensor(out=st8[:, 7, :], in0=st8[:, 2, :], scalar=-1.0, in1=st8[:, 6, :], op0=ALU.mult, op1=ALU.mult),
                          r=[("st8", 2), ("st8", 6)], w=[("st8", 7)])
                    for h in range(H):
                        S.add("act", lambda e: e.activation(out=hl[:, h, :], in_=hl[:, h, :], func=AF.Identity, scale=st8[:, 6, h:h + 1], bias=st8[:, 7, h:h + 1]),
                              r=[khl, ("st8", 6), ("st8", 7)], w=[khl])
                    S.add("dve", lambda e: e.tensor_tensor(out=hl2, in0=hl2, in1=hgrep[:], op=ALU.mult), r=[khl, "hgrep"], w=[khl])
                    S.add("dve", lambda e: e.tensor_tensor(out=ybf[:], in0=hl2, in1=szb_[:], op=ALU.mult), r=[khl, ksz], w=["ybf"])
                    yt, kyt = yts[n % 2], "yts%d" % (n % 2)
                    for g in range(KC // 4):
                        for i in range(4):
                            kc = g * 4 + i
                            S.add("pe", lambda e: e.transpose(p_kt[:, i, :], ybf[:, kc * 128:(kc + 1) * 128], idb[:]),
                                  r=["ybf", "idb"], w=[("pkt", i)])
                        S.add("act", lambda e: e.copy(out=yt[:, g * 4:(g + 1) * 4, :], in_=p_kt[:]),
                              r=[("pkt", i) for i in range(4)], w=[kyt])
                    for k0 in range(0, KC, 4):
                        S.dma("sp", y_fm[k0 * 128:(k0 + 4) * 128, n * 128:(n + 1) * 128].rearrange("(kc p) t -> p kc t", p=128), yt[:, k0:k0 + 4, :], r=[kyt], w=["y_fm"])

                lc = [0]

                def lru_step(h, st):
                    bi = lc[0] % 2
                    lc[0] += 1
                    xcb, kxc = xcs[bi], "xcs%d" % bi
                    zbb, kzb = zbs[bi], "zbs%d" % bi
                    S.dma("sp", xcb[:], xc_fm[h * 256:(h + 1) * 256, st * SW:(st + 1) * SW].rearrange("(b p) t -> p b t", p=128), r=["fm_out"], w=[kxc])
                    S.dma("sp", zbb[:], szb_fm[h * 256:(h + 1) * 256, st * SW:(st + 1) * SW].rearrange("(b p) t -> p b t", p=128), r=["fm_out"], w=[kzb])
                    pend = [[], []]
                    for jb in range(2):
                        S.defer = pend[jb]
                        blk = h * 2 + jb
                        pr_, kpr_ = p_acc[(2 * jb) % 4], "pacc%d" % ((2 * jb) % 4)
                        pi_, kpi_ = p_acc[(2 * jb + 1) % 4], "pacc%d" % ((2 * jb + 1) % 4)
                        for ib in range(2):
                            S.add("pe", lambda e: e.matmul(pr_[:, 0:SW], lhsT=wab[:, h, ib, jb * 128:(jb + 1) * 128], rhs=xcb[:, ib, :], start=(ib == 0), stop=(ib == 1)),
                                  r=["wab", kxc], w=[kpr_])
                        for ib in range(2):
                            S.add("pe", lambda e: e.matmul(pi_[:, 0:SW], lhsT=wxb[:, h, ib, jb * 128:(jb + 1) * 128], rhs=xcb[:, ib, :], start=(ib == 0), stop=(ib == 1)),
                                  r=["wxb", kxc], w=[kpi_])
                        (er, ker), (ei, kei), (la, kla), (a2, ka2), (hh, khh) = [(lt[nm][jb], "%s%d" % (nm, jb)) for nm in ("er", "ei", "la", "a2", "hh")]
                        S.add("act", lambda e: e.activation(out=er[:], in_=pr_[:, 0:SW], func=AF.Exp, scale=-1.0, bias=ccoef[:, 2, blk:blk + 1]),
                              r=[kpr_, "ccoef2"], w=[ker])
                        S.add("act", lambda e: e.activation(out=ei[:], in_=pi_[:, 0:SW], func=AF.Exp, scale=-1.0, bias=ccoef[:, 3, blk:blk + 1]),
                              r=[kpi_, "ccoef2"], w=[kei])
                        S.add("act", lambda e: e.activation(out=er[:], in_=er[:], func=AF.Ln, scale=1.0, bias=cc(C_ONE)), r=[ker, "cst"], w=[ker])
                        S.add("act", lambda e: e.activation(out=er[:], in_=er[:], func=AF.Exp, scale=-1.0), r=[ker], w=[ker])
                        S.add("act", lambda e: e.activation(out=ei[:], in_=ei[:], func=AF.Ln, scale=1.0, bias=cc(C_ONE)), r=[kei, "cst"], w=[kei])
                        S.add("act", lambda e: e.activation(out=ei[:], in_=ei[:], func=AF.Exp, scale=-1.0), r=[kei], w=[kei])
                        S.add("act", lambda e: e.activation(out=la[:], in_=er[:], func=AF.Exp, scale=ccoef[:, 0, blk:blk + 1]), r=[ker, "ccoef"], w=[kla])
                        S.add("act", lambda e: e.activation(out=a2[:], in_=er[:], func=AF.Exp, scale=ccoef[:, 1, blk:blk + 1]), r=[ker, "ccoef"], w=[ka2])
                        S.add("act", lambda e: e.activation(out=a2[:], in_=a2[:], func=AF.Ln, scale=-1.0, bias=cc(C_ONE)), r=[ka2, "cst"], w=[ka2])
                        S.add("act", lambda e: e.activation(out=a2[:], in_=a2[:], func=AF.Exp, scale=0.5), r=[ka2], w=[ka2])
                        S.add("dve", lambda e: e.tensor_tensor(out=ei[:], in0=ei[:], in1=xcb[:, jb, :], op=ALU.mult), r=[kei, kxc], w=[kei])
                        S.add("dve", lambda e: e.tensor_tensor(out=ei[:], in0=ei[:], in1=a2[:], op=ALU.mult), r=[kei, ka2], w=[kei])
                        S.add("dve", lambda e: e.tensor_tensor_scan(out=hh[:], data0=la[:], data1=ei[:], initial=hcar[:, blk:blk + 1],
                                                                    op0=ALU.mult, op1=ALU.add),
                              r=[kla, kei, ("hcar", blk)], w=[khh])
                        S.add("act", lambda e: e.copy(out=hcar[:, blk:blk + 1], in_=hh[:, SW - 1:SW]), r=[khh], w=[("hcar", blk)])
                        ybt, kyb = yb[jb], "yb%d" % jb
                        S.add("dve", lambda e: e.tensor_tensor(out=ybt[:], in0=hh[:], in1=zbb[:, jb, :], op=ALU.mult), r=[khh, kzb], w=[kyb])
                        S.dma("act", y_fm[D + blk * 128:D + (blk + 1) * 128, st * SW:(st + 1) * SW], ybt[:], r=[kyb], w=["y_fm"])
                    S.defer = None
                    for i_ in range(max(len(p_) for p_ in pend)):
                        for jb in range(2):
                            if i_ < len(pend[jb]):
                                S.register(*pend[jb][i_])

                for n in range(NT):
                    mlstm_tile(n)
                for st in range(NST):
                    for h in range(H):
                        lru_step(h, st)
                S.flush()
            if stop == "MR":
                return nc

            with contextlib.ExitStack() as es:
                wob = [sb(es, "wob%d" % i, [128, KO, OCB], BF16) for i in range(2)]
                yT = [sb(es, "yT%d" % i, [128, KO, SW], BF16) for i in range(2)]
                xr = [sb(es, "xr%d" % i, [128, OCB]) for i in range(3)]
                pso = [ps(es, "pso%d" % i, [128, 512]) for i in range(4)]
                NCB = D // OCB
                ostep = max(1, KO // 8)

                def load_wo(cb):
                    wb = wob[cb % 2]
                    for k0 in range(0, KO, ostep):
                        S.dma("pool", wb[:, k0:k0 + ostep, :].rearrange("p k c -> p (k c)"),
                              wod[cb][:, k0 * OCB:(k0 + ostep) * OCB], r=["wbd"], w=[("wob", cb % 2, k0)])
                load_wo(0)
                yi = 0
                xi = 0
                for cb in range(NCB):
                    if cb + 1 < NCB:
                        load_wo(cb + 1)
                    wb = wob[cb % 2]
                    wkeys = [("wob", cb % 2, k0) for k0 in range(0, KO, ostep)]
                    for st in range(NST):
                        yb_, kyb_ = yT[yi % 2], "yT%d" % (yi % 2)
                        yi += 1
                        hk = max(1, KO // 4)
                        for k0 in range(0, KO, hk):
                            S.dma("sp", yb_[:, k0:k0 + hk, :], y_fm[k0 * 128:(k0 + hk) * 128, st * SW:(st + 1) * SW].rearrange("(kc p) t -> p kc t", p=128),
                                  r=["y_fm"], w=[(kyb_, k0)])
                        for ti in range(TPS):
                            n = st * TPS + ti
                            xb_, kxb_ = xr[xi % 3], "xr%d" % (xi % 3)
                            po, kpo = pso[xi % 4], "pso%d" % (xi % 4)
                            xi += 1
                            S.dma("act", xb_[:], xsrc[n * 128:(n + 1) * 128, cb * OCB:(cb + 1) * OCB], r=[("xres", n, cb)], w=[kxb_])
                            for kc in range(KO):
                                S.add("pe", lambda e: e.matmul(po[:, 0:OCB], lhsT=yb_[:, kc, ti * 128:(ti + 1) * 128], rhs=wb[:, kc, :],
                                                               start=(kc == 0), stop=(kc == KO - 1)),
                                      r=wkeys + [(kyb_, k0) for k0 in range(0, KO, hk)], w=[kpo])
                            S.add("dve", lambda e: e.tensor_tensor(out=xb_[:], in0=po[:, 0:OCB], in1=xb_[:], op=ALU.add), r=[kpo, kxb_], w=[kxb_])
                            S.dma("act", xres[n * 128:(n + 1) * 128, cb * OCB:(cb + 1) * OCB], xb_[:], r=[kxb_], w=[("xres", n, cb)])
                S.flush()

        if stop == "P4":
            return nc
        with contextlib.ExitStack() as es:
            grep = sb(es, "fgrep", [128, D])
            xt = [sb(es, "fxt%d" % i, [128, D]) for i in range(2)]
            ot = [sb(es, "fot%d" % i, [128, D]) for i in range(2)]
            junk = sb(es, "fjunk", [128, D], BF16)
            sst = sb(es, "fsst", [128, 8])
            S.dma("act", grep[:], bass.AP(fin_g.tensor, 0, [[0, 128], [1, D]]), w=["grep"])
            for n in range(NT):
                bi = n % 2
                xb_, ob_ = xt[bi], ot[bi]
                kx, ko = "xt%d" % bi, "ot%d" % bi
                c0 = bi * 4
                S.dma("sp", xb_[:], xres[n * 128:(n + 1) * 128, :], w=[kx])
                S.add("act", lambda e: e.activation(out=junk[:], in_=xb_[:], func=AF.Square, accum_out=sst[:, c0:c0 + 1]), r=[kx], w=["junk", ("sst", c0)])
                S.add("act", lambda e: e.activation(out=sst[:, c0 + 1:c0 + 2], in_=sst[:, c0:c0 + 1], func=AF.Ln, scale=1.0 / D, bias=cc(C_EPS)),
                      r=[("sst", c0), "cst"], w=[("sst", c0 + 1)])
                S.add("act", lambda e: e.activation(out=sst[:, c0 + 2:c0 + 3], in_=sst[:, c0 + 1:c0 + 2], func=AF.Exp, scale=-0.5), r=[("sst", c0 + 1)], w=[("sst", c0 + 2)])
                S.add("dve", lambda e: e.scalar_tensor_tensor(out=ob_[:], in0=xb_[:], scalar=sst[:, c0 + 2:c0 + 3], in1=grep[:], op0=ALU.mult, op1=ALU.mult),
                      r=[kx, ("sst", c0 + 2), "grep"], w=[ko])
                hw_ = min(D, 1024)
                for c1 in range(0, D, hw_):
                    S.dma("sp", out_d[n * 128:(n + 1) * 128, c1:c1 + hw_], ob_[:, c1:c1 + hw_], r=[ko], w=["out"])
            S.flush()
    return nc


def host_inputs(cfg, x, norm_g, w_in, i_bias, f_bias, qk_conv, head_norm_g, lru_conv_w, lru_conv_b,
                w_a, b_a, w_x, b_x, lam, w_out, final_g):
    f = lambda a: np.ascontiguousarray(np.asarray(a, dtype=np.float32))
    L, D, T, NB = cfg.L, cfg.D, cfg.T, cfg.NB
    x = f(x)

    def chanpack(v):
        v = f(v)
        return v.reshape(L, -1, 128).transpose(0, 2, 1)

    def convpack(wc):
        wc = f(wc)
        C = wc.shape[2]
        return wc.reshape(L, 4, C // 128, 128).transpose(0, 3, 2, 1).reshape(L, 128, (C // 128) * 4)

    chan = np.concatenate([convpack(qk_conv), convpack(lru_conv_w), chanpack(lru_conv_b), chanpack(b_a),
                           chanpack(b_x), chanpack(lam)], axis=2)
    assert chan.shape == (L, 128, cfg.CW), chan.shape
    rowv = np.concatenate([f(norm_g), f(head_norm_g), f(i_bias), f(f_bias)], axis=1)
    assert rowv.shape == (L, cfg.RW)
    cstv = make_consts()
    shared = {
        "w_in": f(w_in)[:L], "w_out": f(w_out)[:L], "w_a": f(w_a)[:L], "w_x": f(w_x)[:L],
        "chan": np.ascontiguousarray(chan), "rowv": np.ascontiguousarray(rowv),
        "final_g": f(final_g).reshape(1, D), "cst": cstv,
    }
    maps = []
    zx = np.zeros((T, D), np.float32)
    for c in range(cfg.NCORES):
        m = dict(shared)
        m["x"] = np.ascontiguousarray(x[c]) if c < cfg.BATCH else zx
        maps.append(m)
    return maps


_NC_CACHE = {}


def run(cfg, inputs, dbg=()):
    key = (cfg.D, cfg.T, cfg.L, tuple(dbg))
    if key not in _NC_CACHE:
        _NC_CACHE[key] = build(cfg, dbg)
    nc = _NC_CACHE[key]
    maps = host_inputs(cfg, **inputs)
    res = run_bass_kernel_spmd(nc, maps, core_ids=list(range(cfg.NCORES)))
    return res


def kernel(**inputs):
    cfg = Cfg()
    res = run(cfg, inputs)
    return np.stack([np.asarray(res.results[b]["out"], dtype=np.float32) for b in range(cfg.BATCH)], axis=0)
```
